# Optimizing a Trainium2 kernel written in Bass

```python
import math
import jax, jax.numpy as jnp
from jax import lax
import numpy as np


D_MODEL = 1024
BATCH = 2
SEQ = 16384
DEPTH = 2

HEAD_DIM = 64
GRID_W = 64
QBLOCK = 128
ROPE_THETA = 10000.0
RMS_EPS = 1e-6
A_Q_HEADS = 8
A_KV_HEADS = 2
B_GROUPS = ((128, 1), (512, 4), (2048, 16))
B_HEADS_PER_GROUP = 4
C_HEADS = 16
NA_ROWS = 8
NA_COLS = 16
MLP_HIDDEN = 4 * D_MODEL

N_EVEN = (DEPTH + 1) // 2
N_ODD = DEPTH // 2
A_Q_W = A_Q_HEADS * HEAD_DIM
A_KV_W = A_KV_HEADS * HEAD_DIM
B_W = len(B_GROUPS) * B_HEADS_PER_GROUP * HEAD_DIM
AB_IN = A_Q_W + 2 * A_KV_W + 3 * B_W
AB_OUT = A_Q_W + B_HEADS_PER_GROUP * HEAD_DIM
C_W = C_HEADS * HEAD_DIM
C_IN = 3 * C_W

kernel_name = 'hybrid_axial_gqa_dilated_neighbourhood_encoder'


def rms_norm(x, g):
    xf = x.astype(jnp.float32)
    y = xf * lax.rsqrt(jnp.mean(xf * xf, axis=-1, keepdims=True) + RMS_EPS)
    return (y * g.astype(jnp.float32)).astype(x.dtype)


def rope_cos_sin(pos, dim):
    inv_freq = ROPE_THETA ** (-jnp.arange(0, dim, 2, dtype=jnp.float32) / dim)
    ang = pos.astype(jnp.float32)[:, None] * inv_freq[None, :]
    return jnp.cos(ang), jnp.sin(ang)


def apply_rope(x, cos, sin):
    xf = x.astype(jnp.float32)
    x1, x2 = jnp.split(xf, 2, axis=-1)
    return jnp.concatenate([x1 * cos - x2 * sin, x2 * cos + x1 * sin], axis=-1).astype(x.dtype)


def apply_axial_rope(x, row_cs, col_cs):
    half = x.shape[-1] // 2
    return jnp.concatenate([apply_rope(x[..., :half], *row_cs),
                            apply_rope(x[..., half:], *col_cs)], axis=-1)


def dense_gqa_attention(q, k, v):
    b, hkv, g, s, dh = q.shape
    nb = s // QBLOCK
    scale = 1.0 / math.sqrt(dh)
    qb = jnp.moveaxis(q.reshape(b, hkv, g, nb, QBLOCK, dh), 3, 0)

    def one_block(q_blk):
        sc = jnp.einsum('bkgqd,bksd->bkgqs', q_blk, k, preferred_element_type=jnp.float32) * scale
        p = jax.nn.softmax(sc, axis=-1)
        return jnp.einsum('bkgqs,bksd->bkgqd', p.astype(v.dtype), v)

    o = lax.map(one_block, qb)
    return jnp.moveaxis(o, 0, 3).reshape(b, hkv * g, s, dh)


def gathered_attention(q, k, v, idx, extra, score_mod):
    b, h, s, dh = q.shape
    nk = idx.shape[-1]
    nb = s // QBLOCK
    scale = 1.0 / math.sqrt(dh)
    qb = jnp.moveaxis(q.reshape(b, h, nb, QBLOCK, dh), 2, 0)
    ib = idx.reshape(nb, QBLOCK, nk)
    eb = extra.reshape(nb, QBLOCK, nk)

    def one_block(args):
        q_blk, i_blk, e_blk = args
        kg = jnp.take(k, i_blk, axis=2)
        vg = jnp.take(v, i_blk, axis=2)
        sc = jnp.einsum('bhqd,bhqkd->bhqk', q_blk, kg, preferred_element_type=jnp.float32) * scale
        sc = score_mod(sc, e_blk)
        m = jnp.max(sc, axis=-1, keepdims=True)
        p = jnp.exp(sc - m)
        l = jnp.sum(p, axis=-1, keepdims=True)
        o = jnp.einsum('bhqk,bhqkd->bhqd', (p / l).astype(vg.dtype), vg)
        return o, (m + jnp.log(l))[..., 0]

    o, lse = lax.map(one_block, (qb, ib, eb))
    o = jnp.moveaxis(o, 0, 2).reshape(b, h, s, dh)
    lse = jnp.moveaxis(lse, 0, 2).reshape(b, h, s)
    return o, lse


def dilated_indices(s, window, dilation):
    half = window // (2 * dilation)
    off = dilation * jnp.arange(-half, half + 1, dtype=jnp.int32)
    pos = jnp.arange(s, dtype=jnp.int32)[:, None] + off[None, :]
    valid = (pos >= 0) & (pos < s)
    return jnp.clip(pos, 0, s - 1), valid


def neighbourhood_indices(s):
    rows = s // GRID_W
    wh = min(NA_ROWS, rows)
    t = jnp.arange(s, dtype=jnp.int32)
    r, c = t // GRID_W, t % GRID_W
    rs = jnp.clip(r - wh // 2, 0, rows - wh)
    cs = jnp.clip(c - NA_COLS // 2, 0, GRID_W - NA_COLS)
    kr = rs[:, None] + jnp.arange(wh, dtype=jnp.int32)[None, :]
    kc = cs[:, None] + jnp.arange(NA_COLS, dtype=jnp.int32)[None, :]
    idx = (kr[:, :, None] * GRID_W + kc[:, None, :]).reshape(s, wh * NA_COLS)
    dr = kr - r[:, None] + (NA_ROWS - 1)
    dc = kc - c[:, None] + (NA_COLS - 1)
    bidx = (dr[:, :, None] * (2 * NA_COLS - 1) + dc[:, None, :]).reshape(s, wh * NA_COLS)
    return idx, bidx


def mask_mod(sc, valid):
    return jnp.where(valid[None, None], sc, -jnp.inf)


def mixer_ab(h, w_in, w_out, q_gain, k_gain):
    b, s, _ = h.shape
    proj = h @ w_in
    p1 = A_Q_W
    p2 = p1 + A_KV_W
    p3 = p2 + A_KV_W
    p4 = p3 + B_W
    p5 = p4 + B_W
    qa, ka, va, qb, kb, vb = jnp.split(proj, [p1, p2, p3, p4, p5], axis=-1)
    t = jnp.arange(s, dtype=jnp.int32)

    grp = A_Q_HEADS // A_KV_HEADS
    qa = qa.reshape(b, s, A_KV_HEADS, grp, HEAD_DIM).transpose(0, 2, 3, 1, 4)
    ka = ka.reshape(b, s, A_KV_HEADS, HEAD_DIM).transpose(0, 2, 1, 3)
    va = va.reshape(b, s, A_KV_HEADS, HEAD_DIM).transpose(0, 2, 1, 3)
    row_cs = rope_cos_sin(t // GRID_W, HEAD_DIM // 2)
    col_cs = rope_cos_sin(t % GRID_W, HEAD_DIM // 2)
    qa = apply_axial_rope(rms_norm(qa, q_gain), row_cs, col_cs)
    ka = apply_axial_rope(rms_norm(ka, k_gain), row_cs, col_cs)
    oa = dense_gqa_attention(qa, ka, va)

    n_g = len(B_GROUPS)

    def heads(z):
        return z.reshape(b, s, n_g, B_HEADS_PER_GROUP, HEAD_DIM).transpose(2, 0, 3, 1, 4)

    cs1 = rope_cos_sin(t, HEAD_DIM)
    qb = apply_rope(heads(qb), *cs1)
    kb = apply_rope(heads(kb), *cs1)
    vb = heads(vb)
    outs, lses = [], []
    for gi, (window, dil) in enumerate(B_GROUPS):
        idx, valid = dilated_indices(s, window, dil)
        o_g, lse_g = gathered_attention(qb[gi], kb[gi], vb[gi], idx, valid, mask_mod)
        outs.append(o_g.astype(jnp.float32))
        lses.append(lse_g)
    wts = jax.nn.softmax(jnp.stack(lses, axis=0), axis=0)
    ob = jnp.sum(wts[..., None] * jnp.stack(outs, axis=0), axis=0).astype(h.dtype)

    o = jnp.concatenate([oa, ob], axis=1)
    return o.transpose(0, 2, 1, 3).reshape(b, s, AB_OUT) @ w_out


def mixer_c(h, w_in, w_out, rpb):
    b, s, _ = h.shape
    q, k, v = jnp.split(h @ w_in, 3, axis=-1)
    q, k, v = [z.reshape(b, s, C_HEADS, HEAD_DIM).transpose(0, 2, 1, 3) for z in (q, k, v)]
    idx, bidx = neighbourhood_indices(s)
    table = rpb.reshape(C_HEADS, -1).astype(jnp.float32)
    o, _ = gathered_attention(q, k, v, idx, bidx, lambda sc, e: sc + table[:, e][None])
    return o.transpose(0, 2, 1, 3).reshape(b, s, C_W) @ w_out


def sq_relu_mlp(h, w_up, w_down):
    return jnp.square(jax.nn.relu(h @ w_up)) @ w_down


def sandwich(x, mod, g_pre, g_post, fn):
    shift, scale, gate = jnp.split(mod[:, None, :], 3, axis=-1)
    h = rms_norm(x, g_pre) * (1 + scale) + shift
    return x + gate * rms_norm(fn(h), g_post)


def setup_inputs(seed: int = 0) -> dict:
    key = jax.random.key(seed)
    ks = jax.random.split(key, 14)
    D = D_MODEL

    def normal(k, shape, std):
        return jax.random.normal(k, shape, jnp.float32) * std

    return {
        'x': normal(ks[0], (BATCH, SEQ, D), 1.0),
        'c': normal(ks[1], (BATCH, D), 1.0),
        'ada_w': normal(ks[2], (DEPTH, 2, D, 3 * D), D ** -0.5),
        'ada_b': normal(ks[3], (DEPTH, 2, 3 * D), 0.02),
        'norm_g': 1.0 + normal(ks[4], (DEPTH, 4, D), 0.1),
        'ab_w_in': normal(ks[5], (N_EVEN, D, AB_IN), D ** -0.5),
        'ab_w_out': normal(ks[6], (N_EVEN, AB_OUT, D), AB_OUT ** -0.5),
        'a_q_gain': 1.0 + normal(ks[7], (N_EVEN, HEAD_DIM), 0.1),
        'a_k_gain': 1.0 + normal(ks[8], (N_EVEN, HEAD_DIM), 0.1),
        'c_w_in': normal(ks[9], (N_ODD, D, C_IN), D ** -0.5),
        'c_w_out': normal(ks[10], (N_ODD, C_W, D), C_W ** -0.5),
        'c_rpb': normal(ks[11], (N_ODD, C_HEADS, 2 * NA_ROWS - 1, 2 * NA_COLS - 1), 0.1),
        'mlp_w_up': normal(ks[12], (DEPTH, D, MLP_HIDDEN), D ** -0.5),
        'mlp_w_down': normal(ks[13], (DEPTH, MLP_HIDDEN, D), MLP_HIDDEN ** -0.5),
    }


def reference(x, c, ada_w, ada_b, norm_g, ab_w_in, ab_w_out, a_q_gain, a_k_gain,
              c_w_in, c_w_out, c_rpb, mlp_w_up, mlp_w_down):
    cond = jax.nn.silu(c)
    for layer in range(DEPTH):
        mod_mix = cond @ ada_w[layer, 0] + ada_b[layer, 0]
        mod_mlp = cond @ ada_w[layer, 1] + ada_b[layer, 1]
        i = layer // 2
        if layer % 2 == 0:
            mix = functools_partial_ab(ab_w_in[i], ab_w_out[i], a_q_gain[i], a_k_gain[i])
        else:
            mix = functools_partial_c(c_w_in[i], c_w_out[i], c_rpb[i])
        x = sandwich(x, mod_mix, norm_g[layer, 0], norm_g[layer, 1], mix)
        x = sandwich(x, mod_mlp, norm_g[layer, 2], norm_g[layer, 3],
                     lambda h, wu=mlp_w_up[layer], wd=mlp_w_down[layer]: sq_relu_mlp(h, wu, wd))
    return x


def functools_partial_ab(w_in, w_out, q_gain, k_gain):
    return lambda h: mixer_ab(h, w_in, w_out, q_gain, k_gain)


def functools_partial_c(w_in, w_out, rpb):
    return lambda h: mixer_c(h, w_in, w_out, rpb)
```

```python
import numpy as np
from contextlib import ExitStack
import concourse.bass as bass
import concourse.mybir as mybir
from concourse.bass_utils import run_bass_kernel_spmd

F32 = mybir.dt.float32
BF16 = mybir.dt.bfloat16
AF = mybir.ActivationFunctionType
ALU = mybir.AluOpType
AX = mybir.AxisListType

D = 1024
NB = 16384
NBT = 128
OWN = 4608
OWNT = 36
HALO = 256
WPAD = 2048
WIN = OWN + 2 * WPAD
WINT = WIN // 128
OWN0T = WPAD // 128
EPS = 1e-6
NEG = -30000.0
VS = 72


class Prog:
    ENGS = ("pe", "act", "dve", "pool", "sp")

    def __init__(self, nc):
        self.nc = nc
        self.root = ExitStack()
        self.scopes = []
        self.ops = []
        self.nalloc = 0
        self.freed = []
        self.alias = {}
        self.scope_names = []

    def _stack(self):
        return self.scopes[-1] if self.scopes else self.root

    def push(self):
        self.scopes.append(ExitStack())
        self.scope_names.append([])

    def pop(self):
        self.scopes.pop().close()
        self.freed.extend(self.scope_names.pop())

    def sb(self, name, shape, dtype):
        self.nalloc += 1
        nm = "%s_%d" % (name, self.nalloc)
        t = self._stack().enter_context(self.nc.sbuf_tensor(nm, list(shape), dtype))
        if self.scope_names:
            self.scope_names[-1].append(nm)
        if self.freed:
            self.alias[nm] = len(self.freed)
        return t

    def ps(self, name, shape, dtype):
        return self.root.enter_context(self.nc.psum_tensor(name, list(shape), dtype))

    def dram(self, name, shape, dtype, kind):
        return self.nc.dram_tensor(name, list(shape), dtype, kind=kind).ap()

    def rot(self, name, shape, dtype, n):
        ts = [self.sb("%s%d" % (name, i), shape, dtype) for i in range(n)]
        st = {"i": -1}

        def nxt():
            st["i"] += 1
            return ts[st["i"] % n]
        nxt.tensors = ts
        return nxt

    def op(self, eng, fn, reads, writes, dma=None):
        rd = [r if isinstance(r, str) else r.name for r in reads]
        wr = [w if isinstance(w, str) else w.name for w in writes]
        self.ops.append(dict(eng=eng, fn=fn, reads=rd, writes=wr, dma=dma))

    def dma(self, q, out, in_, sem, reads=None, writes=None):
        self.op(q, lambda e: e.dma_start(out=out, in_=in_),
                [in_] if reads is None else reads, [out] if writes is None else writes, dma=sem)

    def mm(self, out, lhsT, rhs, start, stop, reads, writes, **kw):
        self.op("pe", lambda e: e.matmul(out, lhsT=lhsT, rhs=rhs, start=start, stop=stop, **kw), reads, writes)

    def tr(self, out, in_, ident, reads, writes):
        self.op("pe", lambda e: e.transpose(out, in_, ident), reads, writes)

    def act(self, out, in_, func, reads, writes, **kw):
        self.op("act", lambda e: e.activation(out, in_, func, **kw), reads, writes)

    def copy(self, eng, out, in_, reads, writes):
        if eng == "act":
            self.op("act", lambda e: e.copy(out, in_), reads, writes)
        else:
            self.op(eng, lambda e: e.tensor_copy(out, in_), reads, writes)

    def tt(self, eng, out, in0, in1, op, reads, writes):
        self.op(eng, lambda e: e.tensor_tensor(out=out, in0=in0, in1=in1, op=op), reads, writes)

    def ts(self, out, in0, s1, s2, op0, op1, reads, writes):
        if op1 is None:
            self.op("dve", lambda e: e.tensor_scalar(out=out, in0=in0, scalar1=s1, scalar2=None, op0=op0), reads, writes)
        else:
            self.op("dve", lambda e: e.tensor_scalar(out=out, in0=in0, scalar1=s1, scalar2=s2, op0=op0, op1=op1), reads, writes)

    def stt(self, out, in0, scalar, in1, op0, op1, reads, writes):
        self.op("dve", lambda e: e.scalar_tensor_tensor(out=out, in0=in0, scalar=scalar, in1=in1, op0=op0, op1=op1), reads, writes)

    def memset(self, eng, out, val, writes):
        self.op(eng, lambda e: e.memset(out, val), [], writes)

    def finish(self):
        nc = self.nc
        ops = self.ops
        wstate, rstate = {}, {}
        seen = set()
        deps = [None] * len(ops)
        needed = [False] * len(ops)
        for i, o in enumerate(ops):
            sk = ("dma", o["dma"]) if o["dma"] else ("eng", o["eng"])
            o["sk"] = sk
            d = {}
            isdma = bool(o["dma"])
            ispe = o["eng"] == "pe"
            for t in o["reads"] + o["writes"]:
                if t in self.alias and t not in seen:
                    seen.add(t)
                    mr = rstate.setdefault(t, {})
                    for a in self.freed[:self.alias[t]]:
                        for stt_ in (wstate.get(a), rstate.get(a)):
                            if stt_:
                                for skp, j in stt_.items():
                                    if mr.get(skp, -1) < j:
                                        mr[skp] = j
            for t in o["reads"]:
                for skp, j in wstate.get(t, {}).items():
                    if skp == sk and (isdma or ispe):
                        continue
                    if d.get(skp, -1) < j:
                        d[skp] = j
            for t in o["writes"]:
                for skp, j in wstate.get(t, {}).items():
                    if skp == sk:
                        continue
                    if d.get(skp, -1) < j:
                        d[skp] = j
                for skp, j in rstate.get(t, {}).items():
                    if skp == sk:
                        continue
                    if d.get(skp, -1) < j:
                        d[skp] = j
            deps[i] = d
            for j in d.values():
                needed[j] = True
            for t in o["reads"]:
                rstate.setdefault(t, {})[sk] = i
            for t in o["writes"]:
                wstate.setdefault(t, {})[sk] = i
        cnt = {}
        val = [None] * len(ops)
        issued_at = [None] * len(ops)
        run = {}
        for i, o in enumerate(ops):
            sk = o["sk"]
            if o["dma"]:
                cnt[sk] = cnt.get(sk, 0) + 16
                val[i] = cnt[sk]
                run[sk] = val[i]
            elif needed[i]:
                cnt[sk] = cnt.get(sk, 0) + 1
                val[i] = cnt[sk]
            issued_at[i] = dict(run) if deps[i] and any(k[0] == "dma" for k in deps[i]) else None
        sems = {}
        for sk in sorted(cnt, key=str):
            sems[sk] = self.root.enter_context(nc.semaphore("s_%s_%s" % sk))
        self.n_sems = len(sems)
        self.cnt = dict(cnt)
        per = {e: [] for e in self.ENGS}
        for i, o in enumerate(ops):
            per[o["eng"]].append(i)

        def emit(engname, e):
            waited = {}
            for i in per[engname]:
                o = ops[i]
                for skp in sorted(deps[i], key=str):
                    v = val[deps[i][skp]]
                    if skp[0] == "dma":
                        v = issued_at[i][skp]
                    if waited.get(skp, 0) >= v:
                        continue
                    e.wait_ge(sems[skp], v)
                    waited[skp] = v
                ins = o["fn"](e)
                if o["dma"]:
                    ins.then_inc(sems[o["sk"]], 16)
                elif needed[i]:
                    ins.then_inc(sems[o["sk"]], 1)
            if engname == "sp":
                for sk in sorted(cnt, key=str):
                    if sk[0] == "dma" and waited.get(sk, 0) < cnt[sk]:
                        e.wait_ge(sems[sk], cnt[sk])

        with nc.Block() as block:
            @block.tensor
            def _(e):
                emit("pe", e)

            @block.scalar
            def _(e):
                emit("act", e)

            @block.vector
            def _(e):
                emit("dve", e)

            @block.gpsimd
            def _(e):
                emit("pool", e)

            @block.sync
            def _(e):
                emit("sp", e)
        while self.scopes:
            self.pop()
        self.root.close()


def rows_ap(t, row0, nrows, rstride, ncols, rowlen):
    return bass.AP(t.tensor, row0 * rowlen, [[rstride * rowlen, nrows], [1, ncols]])


class Builder:
    def __init__(self, nc, part, dbg=(), upto=None):
        self.nc = nc
        self.part = part
        self.dbg = set(dbg)
        self.upto = upto
        P = self.P = Prog(nc)
        L0 = part in ("L0", "ALL")
        L1 = part in ("L1", "ALL")
        I = "ExternalInput"
        self.cmod = P.dram("cmod", [128, 8], F32, I)
        self.ada_w = P.dram("ada_w", [4, 128, 8, 3072], F32, I)
        self.ada_b = P.dram("ada_b", [4, 3072], F32, I)
        self.norm_g = P.dram("norm_g", [8, 1024], F32, I)
        self.ident_in = P.dram("ident", [128, 128], F32, I)
        self.w_up = P.dram("w_up", [2, 128, 8, 4096], F32, I)
        self.w_down = P.dram("w_down", [2, 128, 32, 1024], F32, I)
        if L0:
            self.xw = P.dram("xw", [WIN, D], F32, I)
            self.xb = P.dram("xb", [NB, D], F32, I)
            self.w_a = P.dram("w_a", [128, 8, 768], F32, I)
            self.w_b = P.dram("w_b", [3, 128, 8, 768], F32, I)
            self.w_out0a = P.dram("w_out0a", [64, 8, 1024], F32, I)
            self.w_out0b = P.dram("w_out0b", [128, 2, 1024], F32, I)
            self.gains = P.dram("gains", [2, 64], F32, I)
            self.ropeA_b = P.dram("ropeA_b", [NB, 128], F32, I)
            self.ropeA_o = P.dram("ropeA_o", [OWN, 128], F32, I)
            self.ropeB_w = P.dram("ropeB_w", [WIN, 192], F32, I)
            self.bmask = P.dram("bmask", [128, 3, 128], F32, I)
        if L1:
            self.w_in1 = P.dram("w_in1", [128, 8, 3072], F32, I)
            self.w_out1 = P.dram("w_out1", [128, 8, 1024], F32, I)
            self.biasI = P.dram("biasI", [128, 16, 5, 128], F32, I)
            self.biasE = P.dram("biasE", [4, 128, 16, 6, 128], F32, I)
        if part == "L0":
            self.x1 = P.dram("x1", [OWN, D], F32, "ExternalOutput")
        elif part == "L1":
            self.x1 = P.dram("x1", [OWN, D], F32, I)
        else:
            self.x1 = P.dram("x1", [OWN, D], F32, "Internal")
        if L1:
            self.out = P.dram("out", [4096, D], F32, "ExternalOutput")
        self.modrows = P.dram("modrows", [12, D], F32, "Internal")
        if L0:
            self.h_w = P.dram("h_w", [WIN, D], BF16, "Internal")
            self.OB = P.dram("OB", [OWN, 3 * 260], F32, "Internal")
            self.x_mid = P.dram("x_mid", [OWN, D], F32, "Internal")
        if L1:
            self.x2 = P.dram("x2", [4096, D], F32, "Internal")
        self.dbg_out = {}
        self.psum = P.ps("psum", [128, 4096], F32)
        self.psum_bf = self.psum[:].bitcast(BF16)
        self.ident = P.sb("ident", [128, 128], BF16)
        P.dma("pool", self.ident[:], self.ident_in, "c_ident")
        self.m05 = P.sb("m05", [128, 16], F32)
        P.memset("pool", self.m05[:], -0.5, [self.m05])
        self.junk = P.sb("junk", [128, 1024], BF16)
        self.small = P.rot("small", [128, 16], F32, 12)

    def bank(self, b0, ncols=512, p0=0, p1=128):
        return self.psum[p0:p1, b0 * 512: b0 * 512 + ncols]

    def bank_bf(self, b0, ncols=1024):
        return self.psum_bf[:, b0 * 1024: b0 * 1024 + ncols]

    def dbg_dump_dram(self, name, src_ap, shape, dtype):
        if name in self.dbg:
            o = self.P.dram("dbg_" + name, shape, dtype, "ExternalOutput")
            self.P.dma("sp", o, src_ap, "dbg", reads=[src_ap.name], writes=["dbg_" + name])

    def dbg_dump_sb(self, name, t, shape, dtype):
        if name in self.dbg:
            o = self.P.dram("dbg_" + name, shape, dtype, "ExternalOutput")
            self.P.dma("sp", o, t, "dbg", reads=[t.name], writes=["dbg_" + name])

    def load_bc(self, m, which, name):
        t = self.P.sb(name, [128, D], F32)
        r = 3 * m + which
        self.P.dma("sp", t[:], self.modrows[r:r + 1, :].partition_broadcast(128), "ld_bc",
                   reads=["modrows"], writes=[t])
        return t

    def rstd_from_ss(self, ss, n, inv):
        P = self.P
        v = self.small()
        P.ts(v[:, 0:n], ss, inv, EPS, ALU.mult, ALU.add, [ss], [v])
        r = self.small()
        P.tt("pool", r[:, 0:n], v[:, 0:n], self.m05[:, 0:n], ALU.pow, [v, self.m05], [r])
        return r

    def prenorm(self, xt, A_bc, B_bc, h_out, tmp):
        P = self.P
        ss = self.small()
        P.memset("pool", ss[:, 0:1], 0.0, [ss])
        P.act(self.junk[:], xt[:], AF.Square, [xt], [self.junk, ss], accum_out=ss[:, 0:1])
        r = self.rstd_from_ss(ss[:, 0:1], 1, 1.0 / D)
        P.stt(tmp[:], xt[:], r[:, 0:1], A_bc[:], ALU.mult, ALU.mult, [xt, r, A_bc], [tmp])
        P.tt("pool", h_out[:], tmp[:], B_bc[:], ALU.add, [tmp, B_bc], [h_out])

    def transposes(self, src, ncol, bankno, dst_ap, dst_key, eng="act", src_key=None):
        P = self.P
        key = "ps%d" % bankno
        pb = self.bank_bf(bankno, ncol)
        sk = src_key if src_key is not None else src
        for c in range(ncol // 128):
            P.tr(pb[:, c * 128:(c + 1) * 128], src[:, c * 128:(c + 1) * 128], self.ident[:],
                 [sk, self.ident], [key])
        return pb, key

    def post_residual(self, ykey, yap, xt, G_bc, xout, tmp):
        P = self.P
        ss = self.small()
        P.memset("pool", ss[:, 0:1], 0.0, [ss])
        P.act(self.junk[:], yap, AF.Square, [ykey], [self.junk, ss], accum_out=ss[:, 0:1])
        r = self.rstd_from_ss(ss[:, 0:1], 1, 1.0 / D)
        P.stt(tmp[:], yap, r[:, 0:1], G_bc[:], ALU.mult, ALU.mult, [ykey, r, G_bc], [tmp])
        P.tt("pool", xout[:], tmp[:], xt[:], ALU.add, [tmp, xt], [xout])

    def qk_rope(self, src_ap, src_key, H, gain, rope_t, out_bf, axial, st):
        P = self.P
        W = H * 64
        qs, t1, t2 = st["qs"], st["t1"], st["t2"]
        P.copy("act", qs[:, 0:W], src_ap, [src_key], [qs])
        v3 = lambda t: t[:, 0:W].rearrange("p (h d) -> p h d", d=64)
        if gain is not None:
            P.tt("pool", t1[:, 0:W], qs[:, 0:W], qs[:, 0:W], ALU.mult, [qs], [t1])
            ssq = self.small()
            self.P.op("dve", lambda e: e.tensor_reduce(out=ssq[:, 0:H], in_=v3(t1), axis=AX.X, op=ALU.add), [t1], [ssq])
            r = self.rstd_from_ss(ssq[:, 0:H], H, 1.0 / 64)
            P.tt("dve", v3(t2), v3(qs), r[:, 0:H].unsqueeze(2).to_broadcast([128, H, 64]), ALU.mult, [qs, r], [t2])
            P.tt("pool", v3(qs), v3(t2), gain[:, 0:64].unsqueeze(1).to_broadcast([128, H, 64]), ALU.mult, [t2, gain], [qs])
        cosb = rope_t[:, 0:64].unsqueeze(1).to_broadcast([128, H, 64])
        P.tt("dve", v3(t1), v3(qs), cosb, ALU.mult, [qs, rope_t], [t1])
        if axial:
            hv = lambda t, off: bass.AP(t[:].tensor, off, [[t[:].ap[0][0], 128], [64, H], [32, 2], [1, 16]])
            sv = lambda off: bass.AP(rope_t[:].tensor, 64 + off, [[rope_t[:].ap[0][0], 128], [0, H], [32, 2], [1, 16]])
            hw = 16
        else:
            hv = lambda t, off: bass.AP(t[:].tensor, off, [[t[:].ap[0][0], 128], [64, H], [1, 32]])
            sv = lambda off: bass.AP(rope_t[:].tensor, 64 + off, [[rope_t[:].ap[0][0], 128], [0, H], [1, 32]])
            hw = 32
        P.tt("pool", hv(t2, 0), hv(qs, hw), sv(0), ALU.mult, [qs, rope_t], [t2])
        P.tt("pool", hv(t2, hw), hv(qs, 0), sv(hw), ALU.mult, [qs, rope_t], [t2])
        P.tt("dve", out_bf, t1[:, 0:W], t2[:, 0:W], ALU.add, [t1, t2], [out_bf.name if hasattr(out_bf, "name") else out_bf])

    def phase_mods(self, ms):
        P = self.P
        P.push()
        cT = P.sb("cT", [128, 8], F32)
        condT = P.sb("condT", [128, 8], F32)
        P.dma("sp", cT[:], self.cmod, "ld_c")
        P.act(condT[:], cT[:], AF.Silu, [cT], [condT])
        brow = P.sb("brow", [1, 3072], F32)
        grow = P.sb("grow", [1, 2, D], F32)
        mrow = P.sb("mrow", [1, 3072], F32)
        orow = P.sb("orow", [1, 3, D], F32)
        wch = P.rot("wch", [128, 8, 512], F32, 2)
        for m in ms:
            P.dma("sp", brow[:], self.ada_b[m:m + 1, :], "ld_b")
            P.dma("sp", grow[:], self.norm_g[2 * m:2 * m + 2, :].rearrange("(o r) d -> o r d", o=1), "ld_g")
            for n6 in range(6):
                w = wch()
                P.dma("sp", w[:], self.ada_w[m, :, :, n6 * 512:(n6 + 1) * 512], "ld_w%d" % (n6 % 2))
                for kc in range(8):
                    P.mm(self.bank(0, 512, 0, 1), condT[:, kc:kc + 1], w[:, kc, :], kc == 0, kc == 7,
                         [condT, w], ["ps0"])
                P.tt("dve", mrow[:, n6 * 512:(n6 + 1) * 512], self.bank(0, 512, 0, 1), brow[:, n6 * 512:(n6 + 1) * 512],
                     ALU.add, ["ps0", brow], [mrow])
            P.stt(orow[:, 0, :], mrow[:, D:2 * D], 1.0, grow[:, 0, :], ALU.add, ALU.mult, [mrow, grow], [orow])
            P.copy("dve", orow[:, 1, :], mrow[:, 0:D], [mrow], [orow])
            P.tt("dve", orow[:, 2, :], mrow[:, 2 * D:3 * D], grow[:, 1, :], ALU.mult, [mrow, grow], [orow])
            P.dma("sp", self.modrows[3 * m:3 * m + 3, :].rearrange("(o r) d -> o r d", o=1), orow[:], "st_mod",
                  reads=[orow], writes=["modrows"])
        P.pop()
        self.dbg_dump_dram("modrows", self.modrows, [12, D], F32)

    def phase_l0_prep(self):
        P = self.P
        P.push()
        self.KAT = P.sb("KAT", [128, NB], BF16)
        self.VA = P.sb("VA", [128, NBT, 2, VS], BF16)
        self.QAT = P.sb("QAT", [128, 4, OWN], BF16)
        P.memset("pool", self.VA[:, :, :, 64:65], 1.0, [self.VA])
        P.push()
        WA = P.sb("WA", [128, 8, 768], BF16)
        P.dma("pool", WA[:], self.w_a, "ld_WA")
        gains = P.sb("gains", [128, 2, 64], F32)
        P.dma("sp", gains[:], bass.AP(self.gains.tensor, 0, [[0, 128], [64, 2], [1, 64]]), "ld_gain", reads=[], writes=[gains])
        A_bc = self.load_bc(0, 0, "A_bc")
        B_bc = self.load_bc(0, 1, "B_bc")
        xrot = P.rot("xt", [128, D], F32, 3)
        tmp = P.rot("tmp", [128, D], F32, 2)
        hrot = P.rot("h", [128, D], BF16, 3)
        hTrot = P.rot("hT", [128, 8, 128], BF16, 2)
        rrot = P.rot("ropeT", [128, 128], F32, 3)
        st = dict(qs=P.sb("qs", [128, 512], F32), t1=P.sb("t1", [128, 512], F32), t2=P.sb("t2", [128, 512], F32))
        qbf = P.rot("qbf", [128, 512], BF16, 2)
        nld = [0]

        def load_x(src, row0):
            xt = xrot()
            k = nld[0] % 3
            nld[0] += 1
            P.dma("sp", xt[:], src[row0:row0 + 128, :], "ld_x%d" % k)
            return xt, k

        for ti in range(WINT):
            xt, k = load_x(self.xw, ti * 128)
            h = hrot()
            self.prenorm(xt, A_bc, B_bc, h, tmp())
            P.dma("pool", self.h_w[ti * 128:(ti + 1) * 128, :], h[:], "st_h%d" % k, reads=[h], writes=["h_w"])
            t = ti - OWN0T
            if 0 <= t < OWNT:
                rt = rrot()
                P.dma("sp", rt[:], self.ropeA_o[t * 128:(t + 1) * 128, :], "ld_r%d" % (t % 3))
                hT = hTrot()
                pb, key = self.transposes(h, D, 0, None, None)
                P.copy("act", hT[:].rearrange("p k t -> p (k t)"), pb, [key], [hT])
                for kc in range(8):
                    P.mm(self.bank(1), hT[:, kc, :], WA[:, kc, 0:512], kc == 0, kc == 7, [hT, WA], ["ps1"])
                qb = qbf()
                self.qk_rope(self.bank(1), "ps1", 8, gains[:, 0, :], rt, qb[:], True, st)
                pb2, key2 = self.transposes(qb, 512, 2, None, None)
                P.copy("act", self.QAT[:, :, t * 128:(t + 1) * 128], pb2.rearrange("p (g t) -> p g t", g=4), [key2], [self.QAT])
        for c in range(NBT):
            xt, k = load_x(self.xb, c * 128)
            h = hrot()
            self.prenorm(xt, A_bc, B_bc, h, tmp())
            rt = rrot()
            P.dma("sp", rt[:], self.ropeA_b[c * 128:(c + 1) * 128, :], "ld_rb%d" % (c % 3))
            hT = hTrot()
            pb, key = self.transposes(h, D, 0, None, None)
            P.copy("act", hT[:].rearrange("p k t -> p (k t)"), pb, [key], [hT])
            for kc in range(8):
                P.mm(self.bank(1, 256), hT[:, kc, :], WA[:, kc, 512:768], kc == 0, kc == 7, [hT, WA], ["ps1"])
            kb = qbf()
            self.qk_rope(self.bank(1, 128), "ps1", 2, gains[:, 1, :], rt, kb[:, 0:128], True, st)
            P.copy("act", self.VA[:, c, :, 0:64], self.psum[:, 512 + 128:512 + 256].rearrange("p (h d) -> p h d", d=64),
                   ["ps1"], [self.VA])
            pb2, key2 = self.transposes(kb, 128, 2, None, None)
            P.copy("dve", self.KAT[:, c * 128:(c + 1) * 128], pb2, [key2], [self.KAT])
        P.pop()
        self.dbg_dump_dram("h_w", self.h_w, [WIN, D], BF16)
        self.dbg_dump_sb("KAT", self.KAT[:], [128, NB], BF16)
        self.dbg_dump_sb("VA", self.VA[:], [128, NBT, 2, VS], BF16)
        self.dbg_dump_sb("QAT", self.QAT[:], [128, 4, OWN], BF16)

    def phase_l0_B(self):
        P = self.P
        P.push()
        bmask = P.sb("bmask", [128, 3, 128], BF16)
        P.dma("pool", bmask[:], self.bmask, "ld_bmask")
        hrot = P.rot("hB", [128, D], BF16, 3)
        hTrot = P.rot("hTB", [128, 8, 128], BF16, 2)
        rrot = P.rot("ropeB", [128, 192], F32, 3)
        st = dict(qs=P.sb("qsB", [128, 512], F32), t1=P.sb("t1B", [128, 512], F32), t2=P.sb("t2B", [128, 512], F32))
        qkbf = P.rot("qkbf", [128, 512], BF16, 2)
        QKT = P.rot("QKT", [128, 4, 128], BF16, 4)
        VB = P.rot("VB", [128, 4, VS], BF16, 4)
        PT = P.rot("PTB", [128, 1536], BF16, 2)
        osb = P.rot("osb", [128, 260], F32, 2)
        WB = P.rot("WB", [128, 8, 768], BF16, 2)
        nchunk = [0]
        nblk = [0]
        for g, d in enumerate((1, 4, 16)):
            W = WB()
            P.dma("pool", W[:], self.w_b[g], "ld_WB%d" % (g % 2))
            own_u = OWN // d
            nb = (own_u + 127) // 128
            jmax = (own_u + 64 + 127) // 128 - 1
            for r in range(d):
                chunks = {}
                nxt = [0]
                for j in range(-1, jmax + 1):
                  todo = []
                  if True:
                    n = nchunk[0]
                    nchunk[0] += 1
                    w0 = WPAD + r + d * 128 * j
                    h = hrot()
                    P.dma("sp", h[:], rows_ap(self.h_w, w0, 128, d, D, D), "ld_hB%d" % (n % 3), reads=["h_w"], writes=[h])
                    rt = rrot()
                    P.dma("sp", rt[:], rows_ap(self.ropeB_w, w0, 128, d, 192, 192), "ld_rB%d" % (n % 3), reads=[], writes=[rt])
                    vt = rt[:, 128:129]
                    hT = hTrot()
                    pb, key = self.transposes(h, D, 0, None, None)
                    P.copy("act", hT[:].rearrange("p k t -> p (k t)"), pb, [key], [hT])
                    for kc in range(8):
                        P.mm(self.bank(1), hT[:, kc, :], W[:, kc, 0:512], kc == 0, kc == 7, [hT, W], ["ps1"])
                    for kc in range(8):
                        P.mm(self.bank(2, 256), hT[:, kc, :], W[:, kc, 512:768], kc == 0, kc == 7, [hT, W], ["ps2"])
                    qk = qkbf()
                    self.qk_rope(self.bank(1), "ps1", 8, None, rt, qk[:], False, st)
                    V = VB()
                    P.act(V[:, :, 0:64], self.bank(2, 256).rearrange("p (h d) -> p h d", d=64), AF.Identity, ["ps2", vt], [V],
                          scale=vt)
                    P.copy("dve", V[:, :, 64:65], vt.unsqueeze(1).to_broadcast([128, 4, 1]), [vt], [V])
                    T = QKT()
                    pb2, key2 = self.transposes(qk, 512, 3, None, None)
                    P.copy("dve", T[:].rearrange("p a t -> p (a t)"), pb2, [key2], [T])
                    chunks[j] = (T, V)
                    todo = []
                    while nxt[0] < nb and min(nxt[0] + 1, jmax) <= j:
                        todo.append(nxt[0])
                        nxt[0] += 1
                  for jb in todo:
                    rels = [rel for rel in (-1, 0, 1) if (jb + rel) in chunks]
                    nr = len(rels)
                    Tq = chunks[jb][0]
                    bi = nblk[0]
                    nblk[0] += 1
                    for hb in range(4):
                        pr, hf = hb // 2, hb % 2
                        for ri, rel in enumerate(rels):
                            Tk = chunks[jb + rel][0]
                            col = 4 * 512 + (hb * nr + ri) * 128
                            so = self.psum[:, col:col + 128]
                            P.mm(so, Tk[hf * 64:(hf + 1) * 64, 2 + pr, :], Tq[hf * 64:(hf + 1) * 64, pr, :], True, False,
                                 [Tk, Tq], ["psS"])
                            P.mm(so, self.ident[:], bmask[:, rel + 1, :], False, True, [self.ident, bmask], ["psS"])
                    pt = PT()
                    ncol = 4 * nr * 128
                    P.act(pt[:, 0:ncol], self.psum[:, 4 * 512:4 * 512 + ncol], AF.Exp, ["psS"], [pt], scale=0.125)
                    for hb in range(4):
                        for ri, rel in enumerate(rels):
                            Vk = chunks[jb + rel][1]
                            blk = (hb * nr + ri) * 128
                            P.mm(self.psum[:, 7 * 512 + hb * 65: 7 * 512 + hb * 65 + 65], pt[:, blk:blk + 128], Vk[:, hb, 0:65],
                                 ri == 0, ri == nr - 1, [pt, Vk], ["psO"])
                    o = osb()
                    P.copy("dve", o[:], self.psum[:, 7 * 512:7 * 512 + 260], ["psO"], [o])
                    nq = min(128, own_u - jb * 128)
                    t0 = r + d * 128 * jb
                    dst = bass.AP(self.OB.tensor, t0 * 780 + g * 260, [[d * 780, nq], [1, 260]])
                    P.dma("pool", dst, o[0:nq, :], "st_OB%d" % (bi % 2), reads=[o], writes=["OB"])
        P.pop()
        self.dbg_dump_dram("OB", self.OB, [OWN, 780], F32)

    def phase_l0_attnA(self):
        P = self.P
        P.push()
        WoutA = P.sb("Wout0A", [64, 8, D], BF16)
        WoutB = P.sb("Wout0B", [128, 2, D], BF16)
        P.dma("pool", WoutA[:], self.w_out0a, "ld_Wout0")
        P.dma("pool", WoutB[:], self.w_out0b, "ld_Wout0")
        onesf = P.sb("onesf", [128, 64], F32)
        P.memset("pool", onesf[:], 1.0, [onesf])
        G_bc = self.load_bc(0, 2, "G_bc0")
        PT = P.rot("PTA", [128, 1024], BF16, 3)
        obrot = P.rot("obin", [128, 780], F32, 2)
        xrot = P.rot("xtA", [128, D], F32, 2)
        orot = P.rot("otokB", [128, 256], BF16, 2)
        oTA = [P.rot("oTA%d" % kv, [64, 4, 128], BF16, 2) for kv in range(2)]
        oTBrot = P.rot("oTB", [128, 2, 128], BF16, 2)
        rlrow = P.rot("rlrow", [128, 512], F32, 2)
        rlbc = P.rot("rlbc", [64, 512], F32, 2)
        tmp = P.rot("tmpA", [128, D], F32, 2)
        xo = P.rot("xoA", [128, D], F32, 2)
        bsum = P.sb("bsum", [128, 260], F32)
        NS = NBT // 2
        for t in range(OWNT):
            q0 = t * 128
            xt = xrot()
            P.dma("sp", xt[:], self.xw[WPAD + q0:WPAD + q0 + 128, :], "ld_xA%d" % (t % 2))
            ob = obrot()
            P.dma("sp", ob[:], self.OB[q0:q0 + 128, :], "ld_ob%d" % (t % 2), reads=["OB"], writes=[ob])
            steps = [(kv, c2) for kv in range(2) for c2 in range(NS)]
            pts = {}
            oT_cur = [oTA[0](), oTA[1]()]
            rl_cur = {}

            def emitS(k):
                kv, c2 = steps[k]
                slot = k % 2
                skey = "psSA%d" % slot
                for u in range(2):
                    c = 2 * c2 + u
                    so = self.psum[:, slot * 1024 + u * 512: slot * 1024 + (u + 1) * 512]
                    P.mm(so, self.KAT[kv * 64:(kv + 1) * 64, c * 128:(c + 1) * 128],
                         self.QAT[kv * 64:(kv + 1) * 64, :, q0:q0 + 128], True, True, [self.KAT, self.QAT], [skey])

            def emitExp(k):
                slot = k % 2
                pt = PT()
                pts[k] = pt
                P.act(pt[:], self.psum[:, slot * 1024:(slot + 1) * 1024], AF.Exp, ["psSA%d" % slot], [pt], scale=0.125)

            def emitPV(k):
                kv, c2 = steps[k]
                okey = "psOA%d" % kv
                pt = pts.pop(k)
                for u in range(2):
                    c = 2 * c2 + u
                    P.mm(self.psum[0:65, (4 + kv) * 512:(5 + kv) * 512], self.VA[:, c, kv, 0:65], pt[:, u * 512:(u + 1) * 512],
                         c == 0, c == NBT - 1, [pt, self.VA], [okey])

            def norm_recip(kv):
                r = rlrow()
                rl_cur[kv] = r
                P.op("dve", (lambda r=r, kv=kv: (lambda e: e.reciprocal(r[64:65, :], self.psum[64:65, (4 + kv) * 512:(5 + kv) * 512])))(),
                     ["psOA%d" % kv], [r])

            def norm_apply(kv):
                r = rl_cur[kv]
                P.mm(self.psum[0:64, 6 * 512:7 * 512], onesf[64:65, 0:64], r[64:65, :], True, True, [onesf, r], ["ps6"])
                rb = rlbc()
                P.copy("dve", rb[:], self.psum[0:64, 6 * 512:7 * 512], ["ps6"], [rb])
                P.tt("dve", oT_cur[kv][:].rearrange("p g t -> p (g t)"), self.psum[0:64, (4 + kv) * 512:(5 + kv) * 512], rb[:],
                     ALU.mult, ["psOA%d" % kv, rb], [oT_cur[kv]])

            emitS(0)
            for k in range(len(steps)):
                if k + 1 < len(steps):
                    emitS(k + 1)
                emitExp(k)
                emitPV(k)
                if k == NS - 1:
                    norm_recip(0)
                if k == NS + 2:
                    norm_apply(0)
            norm_recip(1)
            o = orot()
            obv = ob[:].rearrange("p (g c) -> p g c", g=3)
            P.tt("pool", bsum[:], obv[:, 0, :], obv[:, 1, :], ALU.add, [ob], [bsum])
            P.tt("pool", bsum[:], bsum[:], obv[:, 2, :], ALU.add, [ob, bsum], [bsum])
            bv = bsum[:].rearrange("p (g d) -> p g d", d=65)
            rlb = self.small()
            P.op("dve", (lambda rlb=rlb, bv=bv: (lambda e: e.reciprocal(rlb[:, 0:4], bv[:, :, 64])))(), [bsum], [rlb])
            P.tt("dve", o[:].rearrange("p (g d) -> p g d", d=64), bv[:, :, 0:64],
                 rlb[:, 0:4].unsqueeze(2).to_broadcast([128, 4, 64]), ALU.mult, [bsum, rlb], [o])
            oTB = oTBrot()
            pb, key = self.transposes(o, 256, 7, None, None)
            P.copy("dve", oTB[:].rearrange("p k t -> p (k t)"), pb, [key], [oTB])
            norm_apply(1)
            for n2 in range(2):
                bkey = "ps%d" % (6 + n2)
                first = True
                for kv in range(2):
                    for g in range(4):
                        P.mm(self.bank(6 + n2), oT_cur[kv][:, g, :], WoutA[:, kv * 4 + g, n2 * 512:(n2 + 1) * 512], first, False,
                             [oT_cur[kv], WoutA], [bkey])
                        first = False
                for c in range(2):
                    P.mm(self.bank(6 + n2), oTB[:, c, :], WoutB[:, c, n2 * 512:(n2 + 1) * 512], False, c == 1, [oTB, WoutB], [bkey])
            xout = xo()
            self.post_residual_2(6, xt, G_bc, xout, tmp())
            P.dma("pool", self.x_mid[q0:q0 + 128, :], xout[:], "st_xm%d" % (t % 2), reads=[xout], writes=["x_mid"])
        P.pop()
        P.pop()
        self.dbg_dump_dram("x_mid", self.x_mid, [OWN, D], F32)

    def post_residual_2(self, b0, xt, G_bc, xout, tmp):
        yap = self.psum[:, b0 * 512:(b0 + 2) * 512]
        P = self.P
        keys = ["ps%d" % b0, "ps%d" % (b0 + 1)]
        ss = self.small()
        P.memset("pool", ss[:, 0:1], 0.0, [ss])
        P.act(self.junk[:], yap, AF.Square, keys, [self.junk, ss], accum_out=ss[:, 0:1])
        r = self.rstd_from_ss(ss[:, 0:1], 1, 1.0 / D)
        P.stt(tmp[:], yap, r[:, 0:1], G_bc[:], ALU.mult, ALU.mult, keys + [r, G_bc], [tmp])
        P.tt("pool", xout[:], tmp[:], xt[:], ALU.add, [tmp, xt], [xout])

    def phase_mlp(self, layer, src, dst, ntiles, tag):
        P = self.P
        m = 2 * layer + 1
        P.push()
        Wup = P.sb("Wup", [128, 8, 4096], BF16)
        Wdn = P.sb("Wdn", [128, 32, D], BF16)
        for q4 in range(4):
            P.dma("pool", Wup[:, :, q4 * 1024:(q4 + 1) * 1024], self.w_up[layer, :, :, q4 * 1024:(q4 + 1) * 1024], "ld_Wup",
                  reads=[], writes=[Wup])
            P.dma("pool", Wdn[:, q4 * 8:(q4 + 1) * 8, :], self.w_down[layer, :, q4 * 8:(q4 + 1) * 8, :], "ld_Wdn",
                  reads=[], writes=[Wdn])
        A_bc = self.load_bc(m, 0, "A_bcM")
        B_bc = self.load_bc(m, 1, "B_bcM")
        G_bc = self.load_bc(m, 2, "G_bcM")
        xrot = P.rot("xtM", [128, D], F32, 4)
        tmp = P.rot("tmpM", [128, D], F32, 2)
        hrot = P.rot("hM", [128, D], BF16, 2)
        hT = P.sb("hTM", [128, 8, 256], BF16)
        uT = P.sb("uTM", [128, 32, 256], BF16)
        rl = P.rot("relu", [128, 512], F32, 2)
        xo = P.rot("xoM", [128, D], F32, 2)
        srckey = src.name
        for gidx in range(ntiles // 2):
            xts = []
            for i in range(2):
                ti = gidx * 2 + i
                xt = xrot()
                P.dma("sp", xt[:], src[ti * 128:(ti + 1) * 128, :], "ld_xM%d" % (ti % 4), reads=[srckey], writes=[xt])
                xts.append(xt)
                h = hrot()
                self.prenorm(xt, A_bc, B_bc, h, tmp())
                pb, key = self.transposes(h, D, 4 + i, None, None)
                P.copy("act", hT[:, :, i * 128:(i + 1) * 128], pb.rearrange("p (k t) -> p k t", k=8), [key], [hT])
            for hp in range(16):
                bno = hp % 2
                bkey = "ps%d" % bno
                for u in range(2):
                    hc = 2 * hp + u
                    for kc in range(8):
                        P.mm(self.psum[:, bno * 512 + u * 256: bno * 512 + (u + 1) * 256], Wup[:, kc, hc * 128:(hc + 1) * 128],
                             hT[:, kc, :], kc == 0, kc == 7, [Wup, hT], [bkey])
                r = rl()
                P.act(r[:], self.bank(bno), AF.Relu, [bkey], [r])
                P.tt("dve", uT[:, 2 * hp:2 * hp + 2, :].rearrange("p a t -> p (a t)"), r[:], r[:], ALU.mult, [r], [uT])
            for i in range(2):
                ti = gidx * 2 + i
                b0 = 4 + 2 * i
                for n2 in range(2):
                    for hc in range(32):
                        P.mm(self.bank(b0 + n2), uT[:, hc, i * 128:(i + 1) * 128], Wdn[:, hc, n2 * 512:(n2 + 1) * 512],
                             hc == 0, hc == 31, [uT, Wdn], ["ps%d" % (b0 + n2)])
                xout = xo()
                self.post_residual_2(b0, xts[i], G_bc, xout, tmp())
                P.dma("pool", dst[ti * 128:(ti + 1) * 128, :], xout[:], "st_%s%d" % (tag, ti % 2), reads=[xout], writes=[dst.name])
        P.pop()

    def phase_l1(self):
        P = self.P
        P.push()
        Win = P.sb("Win1", [128, 8, 3072], BF16)
        for q3 in range(3):
            P.dma("pool", Win[:, :, q3 * 1024:(q3 + 1) * 1024], self.w_in1[:, :, q3 * 1024:(q3 + 1) * 1024], "ld_Win1",
                  reads=[], writes=[Win])
        Wout = P.sb("Wout1", [128, 8, D], BF16)
        P.dma("pool", Wout[:], self.w_out1, "ld_Wout1")
        biasI = P.sb("biasI", [128, 16, 5, 128], BF16)
        for q4 in range(4):
            P.dma("pool", biasI[:, q4 * 4:(q4 + 1) * 4], self.biasI[:, q4 * 4:(q4 + 1) * 4], "ld_biasI", reads=[], writes=[biasI])
        biasE = P.sb("biasE", [128, 16, 6, 128], BF16)
        A_bc = self.load_bc(2, 0, "A_bc1")
        B_bc = self.load_bc(2, 1, "B_bc1")
        G_bc = self.load_bc(2, 2, "G_bc1")
        xrot = P.rot("xt1", [128, D], F32, 2)
        tmp = P.rot("tmp1", [128, D], F32, 1)
        hrot = P.rot("h1", [128, D], BF16, 1)
        hTrot = P.rot("hT1", [128, 8, 128], BF16, 2)
        qkbf = P.rot("qkbf1", [128, 1024], BF16, 1)
        NS = 7
        QT = P.rot("QT1", [128, 8, 128], BF16, NS)
        KT = P.rot("KT1", [128, 8, 128], BF16, NS)
        VC = P.rot("VC1", [128, 16, VS], BF16, NS)
        for v in VC.tensors:
            P.memset("pool", v[:, :, 64:65], 1.0, [v])
        PT = P.rot("PT1", [128, 768], BF16, 3)
        orot = P.rot("otok1", [128, D], BF16, 2)
        oTrot = P.rot("oT1", [128, 8, 128], BF16, 2)
        xo = P.rot("xo1", [128, D], F32, 1)
        chunks = {}
        npass = [0]

        def produce(j):
            xt = xrot()
            P.dma("sp", xt[:], self.x1[j * 128:(j + 1) * 128, :], "ld_x1%d" % (j % 3), reads=["x1"], writes=[xt])
            h = hrot()
            self.prenorm(xt, A_bc, B_bc, h, tmp())
            hT = hTrot()
            pb, key = self.transposes(h, D, 0, None, None)
            P.copy("dve", hT[:].rearrange("p k t -> p (k t)"), pb, [key], [hT])
            Tq, Tk, V = QT(), KT(), VC()
            for part, T in ((0, Tq), (1, Tk)):
                for n2 in range(2):
                    for kc in range(8):
                        c0 = part * 1024 + n2 * 512
                        P.mm(self.bank(1 + n2), hT[:, kc, :], Win[:, kc, c0:c0 + 512], kc == 0, kc == 7, [hT, Win], ["ps%d" % (1 + n2)])
                qk = qkbf()
                P.copy("act", qk[:], self.psum[:, 512:1536], ["ps1", "ps2"], [qk])
                pb2, key2 = self.transposes(qk, 1024, 0, None, None)
                P.copy("dve", T[:].rearrange("p k t -> p (k t)"), pb2, [key2], [T])
            for n2 in range(2):
                for kc in range(8):
                    c0 = 2048 + n2 * 512
                    P.mm(self.bank(1 + n2), hT[:, kc, :], Win[:, kc, c0:c0 + 512], kc == 0, kc == 7, [hT, Win], ["ps%d" % (1 + n2)])
            P.copy("act", V[:, :, 0:64], self.psum[:, 512:1536].rearrange("p (h d) -> p h d", d=64), ["ps1", "ps2"], [V])
            chunks[j] = (Tq, Tk, V)

        def attend(jq):
            if jq in (2, 3):
                rels = [-2, -1, 0, 1, 2, 3]
                et = jq - 2
            elif jq in (32, 33):
                rels = [-3, -2, -1, 0, 1, 2]
                et = jq - 30
            else:
                rels = [-2, -1, 0, 1, 2]
                et = None
            if et is not None:
                for q4 in range(4):
                    P.dma("pool", biasE[:, q4 * 4:(q4 + 1) * 4], self.biasE[et, :, q4 * 4:(q4 + 1) * 4], "ld_biasE", reads=[], writes=[biasE])
                bias = biasE
            else:
                bias = biasI
            nr = len(rels)
            xt = xrot()
            P.dma("sp", xt[:], self.x1[jq * 128:(jq + 1) * 128, :], "ld_x1r%d" % (jq % 3), reads=["x1"], writes=[xt])
            Tq = chunks[jq][0]
            o = orot()
            pts = {}

            def emitS(hd):
                hp, hf = hd // 2, hd % 2
                slot = hd % 2
                skey = "psS1_%d" % slot
                sbase = (3 + 2 * slot) * 512
                for ri, rel in enumerate(rels):
                    Tk = chunks[jq + rel][1]
                    so = self.psum[:, sbase + ri * 128: sbase + (ri + 1) * 128]
                    P.mm(so, Tk[hf * 64:(hf + 1) * 64, hp, :], Tq[hf * 64:(hf + 1) * 64, hp, :], True, False, [Tk, Tq], [skey])
                    P.mm(so, self.ident[:], bias[:, hd, ri, :], False, True, [self.ident, bias], [skey])

            def emitExp(hd):
                slot = hd % 2
                sbase = (3 + 2 * slot) * 512
                pt = PT()
                pts[hd] = pt
                P.act(pt[:, 0:nr * 128], self.psum[:, sbase:sbase + nr * 128], AF.Exp, ["psS1_%d" % slot], [pt], scale=0.125)

            def emitPV(hd):
                pt = pts.pop(hd)
                okey = "psO1"
                obase = 7 * 512
                for ri, rel in enumerate(rels):
                    Vk = chunks[jq + rel][2]
                    P.mm(self.psum[:, obase:obase + 65], pt[:, ri * 128:(ri + 1) * 128], Vk[:, hd, 0:65], ri == 0, ri == nr - 1,
                         [pt, Vk], [okey])
                rl = self.small()
                P.op("dve", (lambda rl=rl, ob=obase: (lambda e: e.reciprocal(rl[:, 0:1], self.psum[:, ob + 64:ob + 65])))(), [okey], [rl])
                P.ts(o[:, hd * 64:(hd + 1) * 64], self.psum[:, obase:obase + 64], rl[:, 0:1], None, ALU.mult, None, [okey, rl], [o])

            emitS(0)
            for hd in range(16):
                if hd + 1 < 16:
                    emitS(hd + 1)
                emitExp(hd)
                emitPV(hd)
            oT = oTrot()
            pb, key = self.transposes(o, D, 0, None, None)
            P.copy("dve", oT[:].rearrange("p k t -> p (k t)"), pb, [key], [oT])
            for n2 in range(2):
                for kc in range(8):
                    P.mm(self.bank(1 + n2), oT[:, kc, :], Wout[:, kc, n2 * 512:(n2 + 1) * 512], kc == 0, kc == 7,
                         [oT, Wout], ["ps%d" % (1 + n2)])
            xout = xo()
            self.post_residual_2(1, xt, G_bc, xout, tmp())
            P.dma("pool", self.x2[(jq - 2) * 128:(jq - 1) * 128, :], xout[:], "st_x2%d" % (jq % 2), reads=[xout], writes=["x2"])

        nxt_q = 2
        for j in range(OWNT):
            produce(j)
            while nxt_q <= 33:
                need = min(OWNT - 1, nxt_q + (3 if nxt_q in (2, 3) else 2))
                if need > j:
                    break
                attend(nxt_q)
                nxt_q += 1
        P.pop()
        self.dbg_dump_dram("x2", self.x2, [4096, D], F32)

    def build(self):
        part = self.part
        steps = []
        if part in ("L0", "ALL"):
            steps.append(("mods", lambda: self.phase_mods([0, 1] if part == "L0" else [0, 1, 2, 3])))
            steps.append(("prep", self.phase_l0_prep))
            steps.append(("B", self.phase_l0_B))
            steps.append(("attnA", self.phase_l0_attnA))
            steps.append(("mlp0", lambda: self.phase_mlp(0, self.x_mid, self.x1, OWNT, "x1")))
        if part == "L1":
            steps.append(("mods", lambda: self.phase_mods([2, 3])))
        if part in ("L1", "ALL"):
            steps.append(("l1", self.phase_l1))
            steps.append(("mlp1", lambda: self.phase_mlp(1, self.x2, self.out, 32, "out")))
        for name, fn in steps:
            fn()
            if self.upto == name:
                break
        self.P.finish()


def _rope_cs(pos, dim):
    inv = (10000.0 ** (-np.arange(0, dim, 2, dtype=np.float32) / dim)).astype(np.float32)
    ang = pos.astype(np.float32)[:, None] * inv[None, :]
    return np.cos(ang).astype(np.float32), np.sin(ang).astype(np.float32)


def _rope_tab_axial(pos):
    pos = np.clip(pos, 0, NB - 1)
    cr, sr = _rope_cs(pos // 64, 32)
    cc, sc = _rope_cs(pos % 64, 32)
    return np.concatenate([cr, cr, cc, cc, -sr, sr, -sc, sc], 1).astype(np.float32)


def _rope_tab_1d(pos):
    pos = np.clip(pos, 0, NB - 1)
    c, s = _rope_cs(pos, 64)
    return np.concatenate([c, c, -s, s], 1).astype(np.float32)


def _pk(w, kchunks):
    K, N = w.shape
    return np.ascontiguousarray(w.reshape(kchunks, 128, N).transpose(1, 0, 2))


def _bias_tables(rpb, r0_list, rel_lists):
    outs = []
    ik = np.arange(128)
    for r0, rels in zip(r0_list, rel_lists):
        t = np.full((128, 16, len(rels), 128), NEG, np.float32)
        qrow = r0 + ik // 64
        qc = ik % 64
        rs = np.clip(qrow - 4, 0, 256 - 8)
        cs = np.clip(qc - 8, 0, 64 - 16)
        for ri, rel in enumerate(rels):
            krow = r0 + 2 * rel + ik // 64
            kc = ik % 64
            ok = ((krow[:, None] >= rs[None, :]) & (krow[:, None] < rs[None, :] + 8) &
                  (kc[:, None] >= cs[None, :]) & (kc[:, None] < cs[None, :] + 16) &
                  (krow[:, None] >= 0) & (krow[:, None] < 256) & (qrow[None, :] >= 0) & (qrow[None, :] < 256))
            dr = np.clip(krow[:, None] - qrow[None, :] + 7, 0, 14)
            dc = np.clip(kc[:, None] - qc[None, :] + 15, 0, 30)
            vals = rpb[:, dr, dc]
            t[:, :, ri, :] = np.where(ok[None], vals, np.float32(NEG)).transpose(1, 0, 2)
        outs.append(t)
    return outs


def _host_prep(inp):
    x = np.asarray(inp["x"], np.float32)
    shared = {}
    shared["ada_w"] = np.ascontiguousarray(
        np.asarray(inp["ada_w"], np.float32).reshape(4, 8, 128, 3072).transpose(0, 2, 1, 3))
    shared["ada_b"] = np.ascontiguousarray(np.asarray(inp["ada_b"], np.float32).reshape(4, 3072))
    shared["norm_g"] = np.ascontiguousarray(np.asarray(inp["norm_g"], np.float32).reshape(8, 1024))
    shared["ident"] = np.eye(128, dtype=np.float32)
    shared["w_up"] = np.stack([_pk(np.asarray(inp["mlp_w_up"][l], np.float32), 8) for l in range(2)])
    shared["w_down"] = np.stack([_pk(np.asarray(inp["mlp_w_down"][l], np.float32), 32) for l in range(2)])
    w_in = np.asarray(inp["ab_w_in"][0], np.float32)
    qa = w_in[:, 0:512].reshape(1024, 2, 4, 64).transpose(0, 2, 1, 3).reshape(1024, 512)
    shared["w_a"] = _pk(np.concatenate([qa, w_in[:, 512:768]], 1), 8)
    wb = []
    for g in range(3):
        cols = [w_in[:, 768 + part * 768 + g * 256: 768 + part * 768 + (g + 1) * 256] for part in range(3)]
        wb.append(_pk(np.concatenate(cols, 1), 8))
    shared["w_b"] = np.stack(wb)
    wo0 = np.asarray(inp["ab_w_out"][0], np.float32)
    shared["w_out0a"] = np.ascontiguousarray(wo0[0:512].reshape(8, 64, 1024).transpose(1, 0, 2))
    shared["w_out0b"] = _pk(wo0[512:768], 2)
    shared["gains"] = np.stack([np.asarray(inp["a_q_gain"][0], np.float32), np.asarray(inp["a_k_gain"][0], np.float32)])
    shared["ropeA_b"] = _rope_tab_axial(np.arange(NB))
    ik = np.arange(128)
    bm = np.full((128, 3, 128), NEG, np.float32)
    dk = ik[:, None] - ik[None, :]
    bm[:, 0, :] = np.where(dk >= 64, 0.0, NEG)
    bm[:, 1, :] = np.where(np.abs(dk) <= 64, 0.0, NEG)
    bm[:, 2, :] = np.where(dk <= -64, 0.0, NEG)
    shared["bmask"] = bm
    shared["w_in1"] = _pk(np.asarray(inp["c_w_in"][0], np.float32), 8)
    shared["w_out1"] = _pk(np.asarray(inp["c_w_out"][0], np.float32), 8)
    rpb = np.asarray(inp["c_rpb"][0], np.float32)
    shared["biasI"] = _bias_tables(rpb, [100], [[-2, -1, 0, 1, 2]])[0]
    per = []
    for core in range(8):
        b, q = core // 4, core % 4
        s = 4096 * q
        d = {}
        xp = np.zeros((WIN, D), np.float32)
        lo, hi = s - HALO - WPAD, s + 4096 + HALO + WPAD
        a, bnd = max(lo, 0), min(hi, NB)
        xp[a - lo:bnd - lo] = x[b, a:bnd]
        d["xw"] = xp
        d["xb"] = np.ascontiguousarray(x[b])
        d["cmod"] = np.ascontiguousarray(np.asarray(inp["c"], np.float32)[b].reshape(8, 128).T)
        wpos = np.arange(lo, hi)
        vw = ((wpos >= 0) & (wpos < NB)).astype(np.float32)[:, None]
        d["ropeB_w"] = np.concatenate([_rope_tab_1d(wpos), np.repeat(vw, 64, 1)], 1)
        d["ropeA_o"] = _rope_tab_axial(np.arange(s - HALO, s + 4096 + HALO))
        r_own0 = (s - HALO) // 64
        r0s = [r_own0 + 2 * jq for jq in (2, 3, 32, 33)]
        rl = [[-2, -1, 0, 1, 2, 3]] * 2 + [[-3, -2, -1, 0, 1, 2]] * 2
        d["biasE"] = np.stack(_bias_tables(rpb, r0s, rl))
        per.append(d)
    return shared, per


L0_KEYS = ["cmod", "ada_w", "ada_b", "norm_g", "ident", "w_up", "w_down", "xw", "xb", "w_a", "w_b", "w_out0a", "w_out0b", "gains",
           "ropeA_b", "ropeA_o", "ropeB_w", "bmask"]
L1_KEYS = ["cmod", "ada_w", "ada_b", "norm_g", "ident", "w_up", "w_down", "w_in1", "w_out1", "biasI", "biasE"]

MODE = "FUSED"


def _run(part, shared, per, extra=None, dbg=()):
    nc = bass.Bass("TRN2", target_bir_lowering=False)
    Builder(nc, part, dbg).build()
    keys = {"L0": L0_KEYS, "L1": L1_KEYS, "ALL": sorted(set(L0_KEYS + L1_KEYS))}[part]
    in_maps = []
    for core in range(8):
        m = {}
        for k in keys:
            m[k] = per[core][k] if k in per[core] else shared[k]
        if extra is not None:
            m.update(extra[core])
        in_maps.append(m)
    res = run_bass_kernel_spmd(nc, in_maps, core_ids=list(range(8)))
    return res.results


def kernel(**inputs):
    shared, per = _host_prep(inputs)
    if MODE == "SPLIT":
        r0 = _run("L0", shared, per)
        extra = [{"x1": np.asarray(r0[c]["x1"], np.float32)} for c in range(8)]
        r1 = _run("L1", shared, per, extra)
    else:
        r1 = _run("ALL", shared, per)
    out = np.empty((2, NB, D), np.float32)
    for core in range(8):
        b, q = core // 4, core % 4
        out[b, 4096 * q:4096 * (q + 1)] = np.asarray(r1[core]["out"], np.float32)
    return out
```

```python
import numpy as np
from contextlib import ExitStack
import concourse.bass as bass
import concourse.mybir as mybir
from concourse.bass_utils import run_bass_kernel_spmd

F32 = mybir.dt.float32
BF16 = mybir.dt.bfloat16
AF = mybir.ActivationFunctionType
ALU = mybir.AluOpType
AX = mybir.AxisListType

D = 1024
NB = 16384
NBT = 128
OWN = 4608
OWNT = 36
HALO = 256
WPAD = 2048
WIN = OWN + 2 * WPAD
WINT = WIN // 128
OWN0T = WPAD // 128
EPS = 1e-6
NEG = -30000.0
VS = 72


class Prog:
    ENGS = ("pe", "act", "dve", "pool", "sp")

    def __init__(self, nc):
        self.nc = nc
        self.root = ExitStack()
        self.scopes = []
        self.ops = []
        self.nalloc = 0
        self.freed = []
        self.alias = {}
        self.scope_names = []

    def _stack(self):
        return self.scopes[-1] if self.scopes else self.root

    def push(self):
        self.scopes.append(ExitStack())
        self.scope_names.append([])

    def pop(self):
        self.scopes.pop().close()
        self.freed.extend(self.scope_names.pop())

    def sb(self, name, shape, dtype):
        self.nalloc += 1
        nm = "%s_%d" % (name, self.nalloc)
        t = self._stack().enter_context(self.nc.sbuf_tensor(nm, list(shape), dtype))
        if self.scope_names:
            self.scope_names[-1].append(nm)
        if self.freed:
            self.alias[nm] = len(self.freed)
        return t

    def ps(self, name, shape, dtype):
        return self.root.enter_context(self.nc.psum_tensor(name, list(shape), dtype))

    def dram(self, name, shape, dtype, kind):
        return self.nc.dram_tensor(name, list(shape), dtype, kind=kind).ap()

    def rot(self, name, shape, dtype, n):
        ts = [self.sb("%s%d" % (name, i), shape, dtype) for i in range(n)]
        st = {"i": -1}

        def nxt():
            st["i"] += 1
            return ts[st["i"] % n]
        nxt.tensors = ts
        return nxt

    def op(self, eng, fn, reads, writes, dma=None):
        rd = [r if isinstance(r, str) else r.name for r in reads]
        wr = [w if isinstance(w, str) else w.name for w in writes]
        self.ops.append(dict(eng=eng, fn=fn, reads=rd, writes=wr, dma=dma))

    def dma(self, q, out, in_, sem, reads=None, writes=None):
        self.op(q, lambda e: e.dma_start(out=out, in_=in_),
                [in_] if reads is None else reads, [out] if writes is None else writes, dma=sem)

    def mm(self, out, lhsT, rhs, start, stop, reads, writes, **kw):
        self.op("pe", lambda e: e.matmul(out, lhsT=lhsT, rhs=rhs, start=start, stop=stop, **kw), reads, writes)

    def tr(self, out, in_, ident, reads, writes):
        self.op("pe", lambda e: e.transpose(out, in_, ident), reads, writes)

    def act(self, out, in_, func, reads, writes, **kw):
        self.op("act", lambda e: e.activation(out, in_, func, **kw), reads, writes)

    def copy(self, eng, out, in_, reads, writes):
        if eng == "act":
            self.op("act", lambda e: e.copy(out, in_), reads, writes)
        else:
            self.op(eng, lambda e: e.tensor_copy(out, in_), reads, writes)

    def tt(self, eng, out, in0, in1, op, reads, writes):
        self.op(eng, lambda e: e.tensor_tensor(out=out, in0=in0, in1=in1, op=op), reads, writes)

    def ts(self, out, in0, s1, s2, op0, op1, reads, writes):
        if op1 is None:
            self.op("dve", lambda e: e.tensor_scalar(out=out, in0=in0, scalar1=s1, scalar2=None, op0=op0), reads, writes)
        else:
            self.op("dve", lambda e: e.tensor_scalar(out=out, in0=in0, scalar1=s1, scalar2=s2, op0=op0, op1=op1), reads, writes)

    def stt(self, out, in0, scalar, in1, op0, op1, reads, writes):
        self.op("dve", lambda e: e.scalar_tensor_tensor(out=out, in0=in0, scalar=scalar, in1=in1, op0=op0, op1=op1), reads, writes)

    def memset(self, eng, out, val, writes):
        self.op(eng, lambda e: e.memset(out, val), [], writes)

    def finish(self):
        nc = self.nc
        ops = self.ops
        wstate, rstate = {}, {}
        seen = set()
        deps = [None] * len(ops)
        needed = [False] * len(ops)
        for i, o in enumerate(ops):
            sk = ("dma", o["dma"]) if o["dma"] else ("eng", o["eng"])
            o["sk"] = sk
            d = {}
            isdma = bool(o["dma"])
            ispe = o["eng"] == "pe"
            for t in o["reads"] + o["writes"]:
                if t in self.alias and t not in seen:
                    seen.add(t)
                    mr = rstate.setdefault(t, {})
                    for a in self.freed[:self.alias[t]]:
                        for stt_ in (wstate.get(a), rstate.get(a)):
                            if stt_:
                                for skp, j in stt_.items():
                                    if mr.get(skp, -1) < j:
                                        mr[skp] = j
            for t in o["reads"]:
                for skp, j in wstate.get(t, {}).items():
                    if skp == sk and (isdma or ispe):
                        continue
                    if d.get(skp, -1) < j:
                        d[skp] = j
            for t in o["writes"]:
                for skp, j in wstate.get(t, {}).items():
                    if skp == sk:
                        continue
                    if d.get(skp, -1) < j:
                        d[skp] = j
                for skp, j in rstate.get(t, {}).items():
                    if skp == sk:
                        continue
                    if d.get(skp, -1) < j:
                        d[skp] = j
            deps[i] = d
            for j in d.values():
                needed[j] = True
            for t in o["reads"]:
                rstate.setdefault(t, {})[sk] = i
            for t in o["writes"]:
                wstate.setdefault(t, {})[sk] = i
        cnt = {}
        val = [None] * len(ops)
        issued_at = [None] * len(ops)
        run = {}
        for i, o in enumerate(ops):
            sk = o["sk"]
            if o["dma"]:
                cnt[sk] = cnt.get(sk, 0) + 16
                val[i] = cnt[sk]
                run[sk] = val[i]
            elif needed[i]:
                cnt[sk] = cnt.get(sk, 0) + 1
                val[i] = cnt[sk]
            issued_at[i] = dict(run) if deps[i] and any(k[0] == "dma" for k in deps[i]) else None
        sems = {}
        for sk in sorted(cnt, key=str):
            sems[sk] = self.root.enter_context(nc.semaphore("s_%s_%s" % sk))
        self.n_sems = len(sems)
        self.cnt = dict(cnt)
        per = {e: [] for e in self.ENGS}
        for i, o in enumerate(ops):
            per[o["eng"]].append(i)

        def emit(engname, e):
            waited = {}
            for i in per[engname]:
                o = ops[i]
                for skp in sorted(deps[i], key=str):
                    v = val[deps[i][skp]]
                    if skp[0] == "dma":
                        v = issued_at[i][skp]
                    if waited.get(skp, 0) >= v:
                        continue
                    e.wait_ge(sems[skp], v)
                    waited[skp] = v
                ins = o["fn"](e)
                if o["dma"]:
                    ins.then_inc(sems[o["sk"]], 16)
                elif needed[i]:
                    ins.then_inc(sems[o["sk"]], 1)
            if engname == "sp":
                for sk in sorted(cnt, key=str):
                    if sk[0] == "dma" and waited.get(sk, 0) < cnt[sk]:
                        e.wait_ge(sems[sk], cnt[sk])

        with nc.Block() as block:
            @block.tensor
            def _(e):
                emit("pe", e)

            @block.scalar
            def _(e):
                emit("act", e)

            @block.vector
            def _(e):
                emit("dve", e)

            @block.gpsimd
            def _(e):
                emit("pool", e)

            @block.sync
            def _(e):
                emit("sp", e)
        while self.scopes:
            self.pop()
        self.root.close()


def rows_ap(t, row0, nrows, rstride, ncols, rowlen):
    return bass.AP(t.tensor, row0 * rowlen, [[rstride * rowlen, nrows], [1, ncols]])


class Builder:
    def __init__(self, nc, part, dbg=(), upto=None):
        self.nc = nc
        self.part = part
        self.dbg = set(dbg)
        self.upto = upto
        P = self.P = Prog(nc)
        L0 = part in ("L0", "ALL")
        L1 = part in ("L1", "ALL")
        I = "ExternalInput"
        self.cmod = P.dram("cmod", [128, 8], F32, I)
        self.ada_w = P.dram("ada_w", [4, 128, 8, 3072], F32, I)
        self.ada_b = P.dram("ada_b", [4, 3072], F32, I)
        self.norm_g = P.dram("norm_g", [8, 1024], F32, I)
        self.ident_in = P.dram("ident", [128, 128], F32, I)
        self.w_up = P.dram("w_up", [2, 128, 8, 4096], F32, I)
        self.w_down = P.dram("w_down", [2, 128, 32, 1024], F32, I)
        if L0:
            self.xw = P.dram("xw", [WIN, D], F32, I)
            self.xb = P.dram("xb", [NB, D], F32, I)
            self.w_a = P.dram("w_a", [128, 8, 768], F32, I)
            self.w_b = P.dram("w_b", [3, 128, 8, 768], F32, I)
            self.w_out0a = P.dram("w_out0a", [64, 8, 1024], F32, I)
            self.w_out0b = P.dram("w_out0b", [128, 2, 1024], F32, I)
            self.gains = P.dram("gains", [2, 64], F32, I)
            self.ropeA_b = P.dram("ropeA_b", [NB, 128], F32, I)
            self.ropeA_o = P.dram("ropeA_o", [OWN, 128], F32, I)
            self.ropeB_w = P.dram("ropeB_w", [WIN, 192], F32, I)
            self.bmask = P.dram("bmask", [128, 3, 128], F32, I)
        if L1:
            self.w_in1 = P.dram("w_in1", [128, 8, 3072], F32, I)
            self.w_out1 = P.dram("w_out1", [128, 8, 1024], F32, I)
            self.biasI = P.dram("biasI", [128, 16, 5, 128], F32, I)
            self.biasE = P.dram("biasE", [4, 128, 16, 6, 128], F32, I)
        if part == "L0":
            self.x1 = P.dram("x1", [OWN, D], F32, "ExternalOutput")
        elif part == "L1":
            self.x1 = P.dram("x1", [OWN, D], F32, I)
        else:
            self.x1 = P.dram("x1", [OWN, D], F32, "Internal")
        if L1:
            self.out = P.dram("out", [4096, D], F32, "ExternalOutput")
        self.modrows = P.dram("modrows", [12, D], F32, "Internal")
        if L0:
            self.h_w = P.dram("h_w", [WIN, D], BF16, "Internal")
            self.OB = P.dram("OB", [OWN, 3 * 260], F32, "Internal")
            self.x_mid = P.dram("x_mid", [OWN, D], F32, "Internal")
        if L1:
            self.x2 = P.dram("x2", [4096, D], F32, "Internal")
        self.dbg_out = {}
        self.psum = P.ps("psum", [128, 4096], F32)
        self.psum_bf = self.psum[:].bitcast(BF16)
        self.ident = P.sb("ident", [128, 128], BF16)
        P.dma("pool", self.ident[:], self.ident_in, "c_ident")
        self.m05 = P.sb("m05", [128, 16], F32)
        P.memset("pool", self.m05[:], -0.5, [self.m05])
        self.junk = P.sb("junk", [128, 1024], BF16)
        self.small = P.rot("small", [128, 16], F32, 12)

    def bank(self, b0, ncols=512, p0=0, p1=128):
        return self.psum[p0:p1, b0 * 512: b0 * 512 + ncols]

    def bank_bf(self, b0, ncols=1024):
        return self.psum_bf[:, b0 * 1024: b0 * 1024 + ncols]

    def dbg_dump_dram(self, name, src_ap, shape, dtype):
        if name in self.dbg:
            o = self.P.dram("dbg_" + name, shape, dtype, "ExternalOutput")
            self.P.dma("sp", o, src_ap, "dbg", reads=[src_ap.name], writes=["dbg_" + name])

    def dbg_dump_sb(self, name, t, shape, dtype):
        if name in self.dbg:
            o = self.P.dram("dbg_" + name, shape, dtype, "ExternalOutput")
            self.P.dma("sp", o, t, "dbg", reads=[t.name], writes=["dbg_" + name])

    def load_bc(self, m, which, name):
        t = self.P.sb(name, [128, D], F32)
        r = 3 * m + which
        self.P.dma("sp", t[:], self.modrows[r:r + 1, :].partition_broadcast(128), "ld_bc",
                   reads=["modrows"], writes=[t])
        return t

    def rstd_from_ss(self, ss, n, inv):
        P = self.P
        v = self.small()
        P.ts(v[:, 0:n], ss, inv, EPS, ALU.mult, ALU.add, [ss], [v])
        r = self.small()
        P.tt("pool", r[:, 0:n], v[:, 0:n], self.m05[:, 0:n], ALU.pow, [v, self.m05], [r])
        return r

    def prenorm(self, xt, A_bc, B_bc, h_out, tmp):
        P = self.P
        ss = self.small()
        P.memset("pool", ss[:, 0:1], 0.0, [ss])
        P.act(self.junk[:], xt[:], AF.Square, [xt], [self.junk, ss], accum_out=ss[:, 0:1])
        r = self.rstd_from_ss(ss[:, 0:1], 1, 1.0 / D)
        P.stt(tmp[:], xt[:], r[:, 0:1], A_bc[:], ALU.mult, ALU.mult, [xt, r, A_bc], [tmp])
        P.tt("pool", h_out[:], tmp[:], B_bc[:], ALU.add, [tmp, B_bc], [h_out])

    def transposes(self, src, ncol, bankno, dst_ap, dst_key, eng="act", src_key=None):
        P = self.P
        key = "ps%d" % bankno
        pb = self.bank_bf(bankno, ncol)
        sk = src_key if src_key is not None else src
        for c in range(ncol // 128):
            P.tr(pb[:, c * 128:(c + 1) * 128], src[:, c * 128:(c + 1) * 128], self.ident[:],
                 [sk, self.ident], [key])
        return pb, key

    def post_residual(self, ykey, yap, xt, G_bc, xout, tmp):
        P = self.P
        ss = self.small()
        P.memset("pool", ss[:, 0:1], 0.0, [ss])
        P.act(self.junk[:], yap, AF.Square, [ykey], [self.junk, ss], accum_out=ss[:, 0:1])
        r = self.rstd_from_ss(ss[:, 0:1], 1, 1.0 / D)
        P.stt(tmp[:], yap, r[:, 0:1], G_bc[:], ALU.mult, ALU.mult, [ykey, r, G_bc], [tmp])
        P.tt("pool", xout[:], tmp[:], xt[:], ALU.add, [tmp, xt], [xout])

    def qk_rope(self, src_ap, src_key, H, gain, rope_t, out_bf, axial, st):
        self.qk_rope_a(src_ap, src_key, H, st["qs"])
        self.qk_rope_b(st["qs"], H, gain, rope_t, out_bf, axial, st)

    def qk_rope_a(self, src_ap, src_key, H, qs):
        self.P.copy("act", qs[:, 0:H * 64], src_ap, [src_key], [qs])

    def qk_rope_b(self, qs, H, gain, rope_t, out_bf, axial, st):
        P = self.P
        W = H * 64
        t1, t2 = st["t1"], st["t2"]
        v3 = lambda t: t[:, 0:W].rearrange("p (h d) -> p h d", d=64)
        if gain is not None:
            P.tt("pool", t1[:, 0:W], qs[:, 0:W], qs[:, 0:W], ALU.mult, [qs], [t1])
            ssq = self.small()
            self.P.op("dve", lambda e: e.tensor_reduce(out=ssq[:, 0:H], in_=v3(t1), axis=AX.X, op=ALU.add), [t1], [ssq])
            r = self.rstd_from_ss(ssq[:, 0:H], H, 1.0 / 64)
            P.tt("dve", v3(t2), v3(qs), r[:, 0:H].unsqueeze(2).to_broadcast([128, H, 64]), ALU.mult, [qs, r], [t2])
            P.tt("pool", v3(qs), v3(t2), gain[:, 0:64].unsqueeze(1).to_broadcast([128, H, 64]), ALU.mult, [t2, gain], [qs])
        cosb = rope_t[:, 0:64].unsqueeze(1).to_broadcast([128, H, 64])
        P.tt("dve", v3(t1), v3(qs), cosb, ALU.mult, [qs, rope_t], [t1])
        if axial:
            hv = lambda t, off: bass.AP(t[:].tensor, off, [[t[:].ap[0][0], 128], [64, H], [32, 2], [1, 16]])
            sv = lambda off: bass.AP(rope_t[:].tensor, 64 + off, [[rope_t[:].ap[0][0], 128], [0, H], [32, 2], [1, 16]])
            hw = 16
        else:
            hv = lambda t, off: bass.AP(t[:].tensor, off, [[t[:].ap[0][0], 128], [64, H], [1, 32]])
            sv = lambda off: bass.AP(rope_t[:].tensor, 64 + off, [[rope_t[:].ap[0][0], 128], [0, H], [1, 32]])
            hw = 32
        P.tt("pool", hv(t2, 0), hv(qs, hw), sv(0), ALU.mult, [qs, rope_t], [t2])
        P.tt("pool", hv(t2, hw), hv(qs, 0), sv(hw), ALU.mult, [qs, rope_t], [t2])
        P.tt("dve", out_bf, t1[:, 0:W], t2[:, 0:W], ALU.add, [t1, t2], [out_bf.name if hasattr(out_bf, "name") else out_bf])

    def phase_mods(self, ms):
        P = self.P
        P.push()
        cT = P.sb("cT", [128, 8], F32)
        condT = P.sb("condT", [128, 8], F32)
        P.dma("sp", cT[:], self.cmod, "ld_c")
        P.act(condT[:], cT[:], AF.Silu, [cT], [condT])
        brow = P.sb("brow", [1, 3072], F32)
        grow = P.sb("grow", [1, 2, D], F32)
        mrow = P.sb("mrow", [1, 3072], F32)
        orow = P.sb("orow", [1, 3, D], F32)
        wch = P.rot("wch", [128, 8, 512], F32, 2)
        for m in ms:
            P.dma("sp", brow[:], self.ada_b[m:m + 1, :], "ld_b")
            P.dma("sp", grow[:], self.norm_g[2 * m:2 * m + 2, :].rearrange("(o r) d -> o r d", o=1), "ld_g")
            for n6 in range(6):
                w = wch()
                P.dma("sp", w[:], self.ada_w[m, :, :, n6 * 512:(n6 + 1) * 512], "ld_w%d" % (n6 % 2))
                for kc in range(8):
                    P.mm(self.bank(0, 512, 0, 1), condT[:, kc:kc + 1], w[:, kc, :], kc == 0, kc == 7,
                         [condT, w], ["ps0"])
                P.tt("dve", mrow[:, n6 * 512:(n6 + 1) * 512], self.bank(0, 512, 0, 1), brow[:, n6 * 512:(n6 + 1) * 512],
                     ALU.add, ["ps0", brow], [mrow])
            P.stt(orow[:, 0, :], mrow[:, D:2 * D], 1.0, grow[:, 0, :], ALU.add, ALU.mult, [mrow, grow], [orow])
            P.copy("dve", orow[:, 1, :], mrow[:, 0:D], [mrow], [orow])
            P.tt("dve", orow[:, 2, :], mrow[:, 2 * D:3 * D], grow[:, 1, :], ALU.mult, [mrow, grow], [orow])
            P.dma("sp", self.modrows[3 * m:3 * m + 3, :].rearrange("(o r) d -> o r d", o=1), orow[:], "st_mod",
                  reads=[orow], writes=["modrows"])
        P.pop()
        self.dbg_dump_dram("modrows", self.modrows, [12, D], F32)

    def phase_l0_prep(self):
        P = self.P
        P.push()
        self.KAT = P.sb("KAT", [128, NB], BF16)
        self.VA = P.sb("VA", [128, NBT, 2, VS], BF16)
        self.QAT = P.sb("QAT", [128, 4, OWN], BF16)
        P.memset("pool", self.VA[:, :, :, 64:65], 1.0, [self.VA])
        P.push()
        WA = P.sb("WA", [128, 8, 768], BF16)
        P.dma("pool", WA[:], self.w_a, "ld_WA")
        gains = P.sb("gains", [128, 2, 64], F32)
        P.dma("sp", gains[:], bass.AP(self.gains.tensor, 0, [[0, 128], [64, 2], [1, 64]]), "ld_gain", reads=[], writes=[gains])
        A_bc = self.load_bc(0, 0, "A_bc")
        B_bc = self.load_bc(0, 1, "B_bc")
        xrot = P.rot("xt", [128, D], F32, 3)
        tmp = P.rot("tmp", [128, D], F32, 2)
        hrot = P.rot("h", [128, D], BF16, 3)
        hTrot = P.rot("hT", [128, 8, 128], BF16, 2)
        rrot = P.rot("ropeT", [128, 128], F32, 3)
        st = dict(qs=P.sb("qs", [128, 512], F32), t1=P.sb("t1", [128, 512], F32), t2=P.sb("t2", [128, 512], F32))
        qbf = P.rot("qbf", [128, 512], BF16, 2)
        nld = [0]

        def load_x(src, row0):
            xt = xrot()
            k = nld[0] % 3
            nld[0] += 1
            P.dma("sp", xt[:], src[row0:row0 + 128, :], "ld_x%d" % k)
            return xt, k

        qsrot = P.rot("qsr", [128, 512], F32, 2)

        def skew(stages):
            prevB = None
            for A, Bst in stages:
                A()
                if prevB is not None:
                    prevB()
                prevB = Bst
            if prevB is not None:
                prevB()

        stages = []
        for ti in range(WINT):
            t = ti - OWN0T
            own = 0 <= t < OWNT
            box = {}

            def A(ti=ti, t=t, own=own, box=box):
                xt, k = load_x(self.xw, ti * 128)
                h = hrot()
                self.prenorm(xt, A_bc, B_bc, h, tmp())
                P.dma("pool", self.h_w[ti * 128:(ti + 1) * 128, :], h[:], "st_h%d" % k, reads=[h], writes=["h_w"])
                if own:
                    rt = rrot()
                    P.dma("sp", rt[:], self.ropeA_o[t * 128:(t + 1) * 128, :], "ld_r%d" % (t % 3))
                    hT = hTrot()
                    pb, key = self.transposes(h, D, 0, None, None)
                    P.copy("act", hT[:].rearrange("p k t -> p (k t)"), pb, [key], [hT])
                    for kc in range(8):
                        P.mm(self.bank(1), hT[:, kc, :], WA[:, kc, 0:512], kc == 0, kc == 7, [hT, WA], ["ps1"])
                    qs = qsrot()
                    self.qk_rope_a(self.bank(1), "ps1", 8, qs)
                    box["rt"], box["qs"] = rt, qs

            def Bst(t=t, own=own, box=box):
                if not own:
                    return
                qb = qbf()
                self.qk_rope_b(box["qs"], 8, gains[:, 0, :], box["rt"], qb[:], True, st)
                pb2, key2 = self.transposes(qb, 512, 2, None, None)
                P.copy("act", self.QAT[:, :, t * 128:(t + 1) * 128], pb2.rearrange("p (g t) -> p g t", g=4), [key2], [self.QAT])
            stages.append((A, Bst))
        skew(stages)
        stages = []
        for c in range(NBT):
            box = {}

            def A(c=c, box=box):
                xt, k = load_x(self.xb, c * 128)
                h = hrot()
                self.prenorm(xt, A_bc, B_bc, h, tmp())
                rt = rrot()
                P.dma("sp", rt[:], self.ropeA_b[c * 128:(c + 1) * 128, :], "ld_rb%d" % (c % 3))
                hT = hTrot()
                pb, key = self.transposes(h, D, 0, None, None)
                P.copy("act", hT[:].rearrange("p k t -> p (k t)"), pb, [key], [hT])
                for kc in range(8):
                    P.mm(self.bank(1, 256), hT[:, kc, :], WA[:, kc, 512:768], kc == 0, kc == 7, [hT, WA], ["ps1"])
                qs = qsrot()
                self.qk_rope_a(self.bank(1, 128), "ps1", 2, qs)
                P.copy("act", self.VA[:, c, :, 0:64], self.psum[:, 512 + 128:512 + 256].rearrange("p (h d) -> p h d", d=64),
                       ["ps1"], [self.VA])
                box["rt"], box["qs"] = rt, qs

            def Bst(c=c, box=box):
                kb = qbf()
                self.qk_rope_b(box["qs"], 2, gains[:, 1, :], box["rt"], kb[:, 0:128], True, st)
                pb2, key2 = self.transposes(kb, 128, 2, None, None)
                P.copy("dve", self.KAT[:, c * 128:(c + 1) * 128], pb2, [key2], [self.KAT])
            stages.append((A, Bst))
        skew(stages)
        P.pop()
        self.dbg_dump_dram("h_w", self.h_w, [WIN, D], BF16)
        self.dbg_dump_sb("KAT", self.KAT[:], [128, NB], BF16)
        self.dbg_dump_sb("VA", self.VA[:], [128, NBT, 2, VS], BF16)
        self.dbg_dump_sb("QAT", self.QAT[:], [128, 4, OWN], BF16)

    def phase_l0_B(self):
        P = self.P
        P.push()
        bmask = P.sb("bmask", [128, 3, 128], BF16)
        P.dma("pool", bmask[:], self.bmask, "ld_bmask")
        hrot = P.rot("hB", [128, D], BF16, 3)
        hTrot = P.rot("hTB", [128, 8, 128], BF16, 2)
        rrot = P.rot("ropeB", [128, 192], F32, 3)
        st = dict(t1=P.sb("t1B", [128, 512], F32), t2=P.sb("t2B", [128, 512], F32))
        qsrot = P.rot("qsrB", [128, 512], F32, 2)
        qkbf = P.rot("qkbf", [128, 512], BF16, 2)
        QKT = P.rot("QKT", [128, 4, 128], BF16, 4)
        VB = P.rot("VB", [128, 4, VS], BF16, 4)
        PT = P.rot("PTB", [128, 1536], BF16, 2)
        osb = P.rot("osb", [128, 260], F32, 2)
        WB = P.rot("WB", [128, 8, 768], BF16, 2)
        nchunk = [0]
        nblk = [0]
        for g, d in enumerate((1, 4, 16)):
            W = WB()
            P.dma("pool", W[:], self.w_b[g], "ld_WB%d" % (g % 2))
            own_u = OWN // d
            nb = (own_u + 127) // 128
            jmax = (own_u + 64 + 127) // 128 - 1
            for r in range(d):
                chunks = {}
                boxes = {}
                nxt = [0]

                def prodA(j, r=r, W=W, d=d, boxes=boxes):
                    n = nchunk[0]
                    nchunk[0] += 1
                    w0 = WPAD + r + d * 128 * j
                    h = hrot()
                    P.dma("sp", h[:], rows_ap(self.h_w, w0, 128, d, D, D), "ld_hB%d" % (n % 3), reads=["h_w"], writes=[h])
                    rt = rrot()
                    P.dma("sp", rt[:], rows_ap(self.ropeB_w, w0, 128, d, 192, 192), "ld_rB%d" % (n % 3), reads=[], writes=[rt])
                    vt = rt[:, 128:129]
                    hT = hTrot()
                    pb, key = self.transposes(h, D, 0, None, None)
                    P.copy("act", hT[:].rearrange("p k t -> p (k t)"), pb, [key], [hT])
                    for kc in range(8):
                        P.mm(self.bank(1), hT[:, kc, :], W[:, kc, 0:512], kc == 0, kc == 7, [hT, W], ["ps1"])
                    for kc in range(8):
                        P.mm(self.bank(2, 256), hT[:, kc, :], W[:, kc, 512:768], kc == 0, kc == 7, [hT, W], ["ps2"])
                    qs = qsrot()
                    self.qk_rope_a(self.bank(1), "ps1", 8, qs)
                    V = VB()
                    P.act(V[:, :, 0:64], self.bank(2, 256).rearrange("p (h d) -> p h d", d=64), AF.Identity, ["ps2", vt], [V],
                          scale=vt)
                    P.copy("dve", V[:, :, 64:65], vt.unsqueeze(1).to_broadcast([128, 4, 1]), [vt], [V])
                    boxes[j] = (rt, qs, V)

                def prodB(j, boxes=boxes, chunks=chunks):
                    rt, qs, V = boxes.pop(j)
                    qk = qkbf()
                    self.qk_rope_b(qs, 8, None, rt, qk[:], False, st)
                    T = QKT()
                    pb2, key2 = self.transposes(qk, 512, 3, None, None)
                    P.copy("dve", T[:].rearrange("p a t -> p (a t)"), pb2, [key2], [T])
                    chunks[j] = (T, V)

                def attend(jb, r=r, d=d, g=g, chunks=chunks, own_u=own_u):
                    rels = [rel for rel in (-1, 0, 1) if (jb + rel) in chunks]
                    nr = len(rels)
                    Tq = chunks[jb][0]
                    bi = nblk[0]
                    nblk[0] += 1
                    for hb in range(4):
                        pr, hf = hb // 2, hb % 2
                        for ri, rel in enumerate(rels):
                            Tk = chunks[jb + rel][0]
                            col = 4 * 512 + (hb * nr + ri) * 128
                            so = self.psum[:, col:col + 128]
                            P.mm(so, Tk[hf * 64:(hf + 1) * 64, 2 + pr, :], Tq[hf * 64:(hf + 1) * 64, pr, :], True, False,
                                 [Tk, Tq], ["psS"])
                            P.mm(so, self.ident[:], bmask[:, rel + 1, :], False, True, [self.ident, bmask], ["psS"])
                    pt = PT()
                    ncol = 4 * nr * 128
                    P.act(pt[:, 0:ncol], self.psum[:, 4 * 512:4 * 512 + ncol], AF.Exp, ["psS"], [pt], scale=0.125)
                    for hb in range(4):
                        for ri, rel in enumerate(rels):
                            Vk = chunks[jb + rel][1]
                            blk = (hb * nr + ri) * 128
                            P.mm(self.psum[:, 7 * 512 + hb * 65: 7 * 512 + hb * 65 + 65], pt[:, blk:blk + 128], Vk[:, hb, 0:65],
                                 ri == 0, ri == nr - 1, [pt, Vk], ["psO"])
                    o = osb()
                    P.copy("dve", o[:], self.psum[:, 7 * 512:7 * 512 + 260], ["psO"], [o])
                    nq = min(128, own_u - jb * 128)
                    t0 = r + d * 128 * jb
                    dst = bass.AP(self.OB.tensor, t0 * 780 + g * 260, [[d * 780, nq], [1, 260]])
                    P.dma("pool", dst, o[0:nq, :], "st_OB%d" % (bi % 2), reads=[o], writes=["OB"])

                def attend_ready(jdone):
                    while nxt[0] < nb and min(nxt[0] + 1, jmax) <= jdone:
                        attend(nxt[0])
                        nxt[0] += 1

                js = list(range(-1, jmax + 1))
                for idx, j in enumerate(js):
                    prodA(j)
                    if idx >= 1:
                        prodB(js[idx - 1])
                        attend_ready(js[idx - 1])
                prodB(js[-1])
                attend_ready(js[-1])
        P.pop()
        self.dbg_dump_dram("OB", self.OB, [OWN, 780], F32)

    def phase_l0_attnA(self):
        P = self.P
        P.push()
        WoutA = P.sb("Wout0A", [64, 8, D], BF16)
        WoutB = P.sb("Wout0B", [128, 2, D], BF16)
        P.dma("pool", WoutA[:], self.w_out0a, "ld_Wout0")
        P.dma("pool", WoutB[:], self.w_out0b, "ld_Wout0")
        onesf = P.sb("onesf", [128, 64], F32)
        P.memset("pool", onesf[:], 1.0, [onesf])
        G_bc = self.load_bc(0, 2, "G_bc0")
        PT = P.rot("PTA", [128, 1024], BF16, 3)
        obrot = P.rot("obin", [128, 780], F32, 2)
        xrot = P.rot("xtA", [128, D], F32, 2)
        orot = P.rot("otokB", [128, 256], BF16, 2)
        oTA = [P.rot("oTA%d" % kv, [64, 4, 128], BF16, 2) for kv in range(2)]
        oTBrot = P.rot("oTB", [128, 2, 128], BF16, 2)
        rlrow = P.rot("rlrow", [128, 512], F32, 2)
        rlbc = P.rot("rlbc", [64, 512], F32, 2)
        tmp = P.rot("tmpA", [128, D], F32, 2)
        xo = P.rot("xoA", [128, D], F32, 2)
        bsum = P.sb("bsum", [128, 260], F32)
        NS = NBT // 2
        for t in range(OWNT):
            q0 = t * 128
            xt = xrot()
            P.dma("sp", xt[:], self.xw[WPAD + q0:WPAD + q0 + 128, :], "ld_xA%d" % (t % 2))
            ob = obrot()
            P.dma("sp", ob[:], self.OB[q0:q0 + 128, :], "ld_ob%d" % (t % 2), reads=["OB"], writes=[ob])
            steps = [(kv, c2) for kv in range(2) for c2 in range(NS)]
            pts = {}
            oT_cur = [oTA[0](), oTA[1]()]
            rl_cur = {}

            def emitS(k):
                kv, c2 = steps[k]
                slot = k % 2
                skey = "psSA%d" % slot
                for u in range(2):
                    c = 2 * c2 + u
                    so = self.psum[:, slot * 1024 + u * 512: slot * 1024 + (u + 1) * 512]
                    P.mm(so, self.KAT[kv * 64:(kv + 1) * 64, c * 128:(c + 1) * 128],
                         self.QAT[kv * 64:(kv + 1) * 64, :, q0:q0 + 128], True, True, [self.KAT, self.QAT], [skey])

            def emitExp(k):
                slot = k % 2
                pt = PT()
                pts[k] = pt
                P.act(pt[:], self.psum[:, slot * 1024:(slot + 1) * 1024], AF.Exp, ["psSA%d" % slot], [pt], scale=0.125)

            def emitPV(k):
                kv, c2 = steps[k]
                okey = "psOA%d" % kv
                pt = pts.pop(k)
                for u in range(2):
                    c = 2 * c2 + u
                    P.mm(self.psum[0:65, (4 + kv) * 512:(5 + kv) * 512], self.VA[:, c, kv, 0:65], pt[:, u * 512:(u + 1) * 512],
                         c == 0, c == NBT - 1, [pt, self.VA], [okey])

            def norm_recip(kv):
                r = rlrow()
                rl_cur[kv] = r
                P.op("dve", (lambda r=r, kv=kv: (lambda e: e.reciprocal(r[64:65, :], self.psum[64:65, (4 + kv) * 512:(5 + kv) * 512])))(),
                     ["psOA%d" % kv], [r])

            def norm_apply(kv):
                r = rl_cur[kv]
                P.mm(self.psum[0:64, 6 * 512:7 * 512], onesf[64:65, 0:64], r[64:65, :], True, True, [onesf, r], ["ps6"])
                rb = rlbc()
                P.copy("dve", rb[:], self.psum[0:64, 6 * 512:7 * 512], ["ps6"], [rb])
                P.tt("dve", oT_cur[kv][:].rearrange("p g t -> p (g t)"), self.psum[0:64, (4 + kv) * 512:(5 + kv) * 512], rb[:],
                     ALU.mult, ["psOA%d" % kv, rb], [oT_cur[kv]])

            emitS(0)
            for k in range(len(steps)):
                if k + 1 < len(steps):
                    emitS(k + 1)
                emitExp(k)
                emitPV(k)
                if k == NS - 1:
                    norm_recip(0)
                if k == NS + 2:
                    norm_apply(0)
            norm_recip(1)
            o = orot()
            obv = ob[:].rearrange("p (g c) -> p g c", g=3)
            P.tt("pool", bsum[:], obv[:, 0, :], obv[:, 1, :], ALU.add, [ob], [bsum])
            P.tt("pool", bsum[:], bsum[:], obv[:, 2, :], ALU.add, [ob, bsum], [bsum])
            bv = bsum[:].rearrange("p (g d) -> p g d", d=65)
            rlb = self.small()
            P.op("dve", (lambda rlb=rlb, bv=bv: (lambda e: e.reciprocal(rlb[:, 0:4], bv[:, :, 64])))(), [bsum], [rlb])
            P.tt("dve", o[:].rearrange("p (g d) -> p g d", d=64), bv[:, :, 0:64],
                 rlb[:, 0:4].unsqueeze(2).to_broadcast([128, 4, 64]), ALU.mult, [bsum, rlb], [o])
            oTB = oTBrot()
            pb, key = self.transposes(o, 256, 7, None, None)
            P.copy("dve", oTB[:].rearrange("p k t -> p (k t)"), pb, [key], [oTB])
            norm_apply(1)
            for n2 in range(2):
                bkey = "ps%d" % (6 + n2)
                first = True
                for kv in range(2):
                    for g in range(4):
                        P.mm(self.bank(6 + n2), oT_cur[kv][:, g, :], WoutA[:, kv * 4 + g, n2 * 512:(n2 + 1) * 512], first, False,
                             [oT_cur[kv], WoutA], [bkey])
                        first = False
                for c in range(2):
                    P.mm(self.bank(6 + n2), oTB[:, c, :], WoutB[:, c, n2 * 512:(n2 + 1) * 512], False, c == 1, [oTB, WoutB], [bkey])
            xout = xo()
            self.post_residual_2(6, xt, G_bc, xout, tmp())
            P.dma("pool", self.x_mid[q0:q0 + 128, :], xout[:], "st_xm%d" % (t % 2), reads=[xout], writes=["x_mid"])
        P.pop()
        P.pop()
        self.dbg_dump_dram("x_mid", self.x_mid, [OWN, D], F32)

    def post_residual_2(self, b0, xt, G_bc, xout, tmp):
        yap = self.psum[:, b0 * 512:(b0 + 2) * 512]
        P = self.P
        keys = ["ps%d" % b0, "ps%d" % (b0 + 1)]
        ss = self.small()
        P.memset("pool", ss[:, 0:1], 0.0, [ss])
        P.act(self.junk[:], yap, AF.Square, keys, [self.junk, ss], accum_out=ss[:, 0:1])
        r = self.rstd_from_ss(ss[:, 0:1], 1, 1.0 / D)
        P.stt(tmp[:], yap, r[:, 0:1], G_bc[:], ALU.mult, ALU.mult, keys + [r, G_bc], [tmp])
        P.tt("pool", xout[:], tmp[:], xt[:], ALU.add, [tmp, xt], [xout])

    def phase_mlp(self, layer, src, dst, ntiles, tag):
        P = self.P
        m = 2 * layer + 1
        P.push()
        Wup = P.sb("Wup", [128, 8, 4096], BF16)
        Wdn = P.sb("Wdn", [128, 32, D], BF16)
        for q4 in range(4):
            P.dma("pool", Wup[:, :, q4 * 1024:(q4 + 1) * 1024], self.w_up[layer, :, :, q4 * 1024:(q4 + 1) * 1024], "ld_Wup",
                  reads=[], writes=[Wup])
            P.dma("pool", Wdn[:, q4 * 8:(q4 + 1) * 8, :], self.w_down[layer, :, q4 * 8:(q4 + 1) * 8, :], "ld_Wdn",
                  reads=[], writes=[Wdn])
        A_bc = self.load_bc(m, 0, "A_bcM")
        B_bc = self.load_bc(m, 1, "B_bcM")
        G_bc = self.load_bc(m, 2, "G_bcM")
        xrot = P.rot("xtM", [128, D], F32, 4)
        tmp = P.rot("tmpM", [128, D], F32, 2)
        hrot = P.rot("hM", [128, D], BF16, 2)
        hT = P.sb("hTM", [128, 8, 256], BF16)
        uT = P.sb("uTM", [128, 32, 256], BF16)
        rl = P.rot("relu", [128, 512], F32, 2)
        xo = P.rot("xoM", [128, D], F32, 2)
        srckey = src.name
        for gidx in range(ntiles // 2):
            xts = []
            for i in range(2):
                ti = gidx * 2 + i
                xt = xrot()
                P.dma("sp", xt[:], src[ti * 128:(ti + 1) * 128, :], "ld_xM%d" % (ti % 4), reads=[srckey], writes=[xt])
                xts.append(xt)
                h = hrot()
                self.prenorm(xt, A_bc, B_bc, h, tmp())
                pb, key = self.transposes(h, D, 4 + i, None, None)
                P.copy("act", hT[:, :, i * 128:(i + 1) * 128], pb.rearrange("p (k t) -> p k t", k=8), [key], [hT])
            for hp in range(16):
                bno = hp % 2
                bkey = "ps%d" % bno
                for u in range(2):
                    hc = 2 * hp + u
                    for kc in range(8):
                        P.mm(self.psum[:, bno * 512 + u * 256: bno * 512 + (u + 1) * 256], Wup[:, kc, hc * 128:(hc + 1) * 128],
                             hT[:, kc, :], kc == 0, kc == 7, [Wup, hT], [bkey])
                r = rl()
                P.act(r[:], self.bank(bno), AF.Relu, [bkey], [r])
                P.tt("dve", uT[:, 2 * hp:2 * hp + 2, :].rearrange("p a t -> p (a t)"), r[:], r[:], ALU.mult, [r], [uT])
            for i in range(2):
                ti = gidx * 2 + i
                b0 = 4 + 2 * i
                for n2 in range(2):
                    for hc in range(32):
                        P.mm(self.bank(b0 + n2), uT[:, hc, i * 128:(i + 1) * 128], Wdn[:, hc, n2 * 512:(n2 + 1) * 512],
                             hc == 0, hc == 31, [uT, Wdn], ["ps%d" % (b0 + n2)])
                xout = xo()
                self.post_residual_2(b0, xts[i], G_bc, xout, tmp())
                P.dma("pool", dst[ti * 128:(ti + 1) * 128, :], xout[:], "st_%s%d" % (tag, ti % 2), reads=[xout], writes=[dst.name])
        P.pop()

    def phase_l1(self):
        P = self.P
        P.push()
        Win = P.sb("Win1", [128, 8, 3072], BF16)
        for q3 in range(3):
            P.dma("pool", Win[:, :, q3 * 1024:(q3 + 1) * 1024], self.w_in1[:, :, q3 * 1024:(q3 + 1) * 1024], "ld_Win1",
                  reads=[], writes=[Win])
        Wout = P.sb("Wout1", [128, 8, D], BF16)
        P.dma("pool", Wout[:], self.w_out1, "ld_Wout1")
        biasI = P.sb("biasI", [128, 16, 5, 128], BF16)
        for q4 in range(4):
            P.dma("pool", biasI[:, q4 * 4:(q4 + 1) * 4], self.biasI[:, q4 * 4:(q4 + 1) * 4], "ld_biasI", reads=[], writes=[biasI])
        biasE = P.sb("biasE", [128, 16, 6, 128], BF16)
        A_bc = self.load_bc(2, 0, "A_bc1")
        B_bc = self.load_bc(2, 1, "B_bc1")
        G_bc = self.load_bc(2, 2, "G_bc1")
        xrot = P.rot("xt1", [128, D], F32, 2)
        tmp = P.rot("tmp1", [128, D], F32, 1)
        hrot = P.rot("h1", [128, D], BF16, 1)
        hTrot = P.rot("hT1", [128, 8, 128], BF16, 2)
        qkbf = P.rot("qkbf1", [128, 1024], BF16, 1)
        NS = 7
        QT = P.rot("QT1", [128, 8, 128], BF16, NS)
        KT = P.rot("KT1", [128, 8, 128], BF16, NS)
        VC = P.rot("VC1", [128, 16, VS], BF16, NS)
        for v in VC.tensors:
            P.memset("pool", v[:, :, 64:65], 1.0, [v])
        PT = P.rot("PT1", [128, 768], BF16, 3)
        orot = P.rot("otok1", [128, D], BF16, 2)
        oTrot = P.rot("oT1", [128, 8, 128], BF16, 2)
        xo = P.rot("xo1", [128, D], F32, 1)
        chunks = {}
        npass = [0]

        def produce(j):
            xt = xrot()
            P.dma("sp", xt[:], self.x1[j * 128:(j + 1) * 128, :], "ld_x1%d" % (j % 3), reads=["x1"], writes=[xt])
            h = hrot()
            self.prenorm(xt, A_bc, B_bc, h, tmp())
            hT = hTrot()
            pb, key = self.transposes(h, D, 0, None, None)
            P.copy("dve", hT[:].rearrange("p k t -> p (k t)"), pb, [key], [hT])
            Tq, Tk, V = QT(), KT(), VC()
            for part, T in ((0, Tq), (1, Tk)):
                for n2 in range(2):
                    for kc in range(8):
                        c0 = part * 1024 + n2 * 512
                        P.mm(self.bank(1 + n2), hT[:, kc, :], Win[:, kc, c0:c0 + 512], kc == 0, kc == 7, [hT, Win], ["ps%d" % (1 + n2)])
                qk = qkbf()
                P.copy("act", qk[:], self.psum[:, 512:1536], ["ps1", "ps2"], [qk])
                pb2, key2 = self.transposes(qk, 1024, 0, None, None)
                P.copy("dve", T[:].rearrange("p k t -> p (k t)"), pb2, [key2], [T])
            for n2 in range(2):
                for kc in range(8):
                    c0 = 2048 + n2 * 512
                    P.mm(self.bank(1 + n2), hT[:, kc, :], Win[:, kc, c0:c0 + 512], kc == 0, kc == 7, [hT, Win], ["ps%d" % (1 + n2)])
            P.copy("act", V[:, :, 0:64], self.psum[:, 512:1536].rearrange("p (h d) -> p h d", d=64), ["ps1", "ps2"], [V])
            chunks[j] = (Tq, Tk, V)

        def attend(jq):
            if jq in (2, 3):
                rels = [-2, -1, 0, 1, 2, 3]
                et = jq - 2
            elif jq in (32, 33):
                rels = [-3, -2, -1, 0, 1, 2]
                et = jq - 30
            else:
                rels = [-2, -1, 0, 1, 2]
                et = None
            if et is not None:
                for q4 in range(4):
                    P.dma("pool", biasE[:, q4 * 4:(q4 + 1) * 4], self.biasE[et, :, q4 * 4:(q4 + 1) * 4], "ld_biasE", reads=[], writes=[biasE])
                bias = biasE
            else:
                bias = biasI
            nr = len(rels)
            xt = xrot()
            P.dma("sp", xt[:], self.x1[jq * 128:(jq + 1) * 128, :], "ld_x1r%d" % (jq % 3), reads=["x1"], writes=[xt])
            Tq = chunks[jq][0]
            o = orot()
            pts = {}

            def emitS(hd):
                hp, hf = hd // 2, hd % 2
                slot = hd % 2
                skey = "psS1_%d" % slot
                sbase = (3 + 2 * slot) * 512
                for ri, rel in enumerate(rels):
                    Tk = chunks[jq + rel][1]
                    so = self.psum[:, sbase + ri * 128: sbase + (ri + 1) * 128]
                    P.mm(so, Tk[hf * 64:(hf + 1) * 64, hp, :], Tq[hf * 64:(hf + 1) * 64, hp, :], True, False, [Tk, Tq], [skey])
                    P.mm(so, self.ident[:], bias[:, hd, ri, :], False, True, [self.ident, bias], [skey])

            def emitExp(hd):
                slot = hd % 2
                sbase = (3 + 2 * slot) * 512
                pt = PT()
                pts[hd] = pt
                P.act(pt[:, 0:nr * 128], self.psum[:, sbase:sbase + nr * 128], AF.Exp, ["psS1_%d" % slot], [pt], scale=0.125)

            def emitPV(hd):
                pt = pts.pop(hd)
                okey = "psO1"
                obase = 7 * 512
                for ri, rel in enumerate(rels):
                    Vk = chunks[jq + rel][2]
                    P.mm(self.psum[:, obase:obase + 65], pt[:, ri * 128:(ri + 1) * 128], Vk[:, hd, 0:65], ri == 0, ri == nr - 1,
                         [pt, Vk], [okey])
                rl = self.small()
                P.op("dve", (lambda rl=rl, ob=obase: (lambda e: e.reciprocal(rl[:, 0:1], self.psum[:, ob + 64:ob + 65])))(), [okey], [rl])
                P.ts(o[:, hd * 64:(hd + 1) * 64], self.psum[:, obase:obase + 64], rl[:, 0:1], None, ALU.mult, None, [okey, rl], [o])

            emitS(0)
            for hd in range(16):
                if hd + 1 < 16:
                    emitS(hd + 1)
                emitExp(hd)
                emitPV(hd)
            oT = oTrot()
            pb, key = self.transposes(o, D, 0, None, None)
            P.copy("dve", oT[:].rearrange("p k t -> p (k t)"), pb, [key], [oT])
            for n2 in range(2):
                for kc in range(8):
                    P.mm(self.bank(1 + n2), oT[:, kc, :], Wout[:, kc, n2 * 512:(n2 + 1) * 512], kc == 0, kc == 7,
                         [oT, Wout], ["ps%d" % (1 + n2)])
            xout = xo()
            self.post_residual_2(1, xt, G_bc, xout, tmp())
            P.dma("pool", self.x2[(jq - 2) * 128:(jq - 1) * 128, :], xout[:], "st_x2%d" % (jq % 2), reads=[xout], writes=["x2"])

        nxt_q = 2
        for j in range(OWNT):
            produce(j)
            while nxt_q <= 33:
                need = min(OWNT - 1, nxt_q + (3 if nxt_q in (2, 3) else 2))
                if need > j:
                    break
                attend(nxt_q)
                nxt_q += 1
        P.pop()
        self.dbg_dump_dram("x2", self.x2, [4096, D], F32)

    def build(self):
        part = self.part
        steps = []
        if part in ("L0", "ALL"):
            steps.append(("mods", lambda: self.phase_mods([0, 1] if part == "L0" else [0, 1, 2, 3])))
            steps.append(("prep", self.phase_l0_prep))
            steps.append(("B", self.phase_l0_B))
            steps.append(("attnA", self.phase_l0_attnA))
            steps.append(("mlp0", lambda: self.phase_mlp(0, self.x_mid, self.x1, OWNT, "x1")))
        if part == "L1":
            steps.append(("mods", lambda: self.phase_mods([2, 3])))
        if part in ("L1", "ALL"):
            steps.append(("l1", self.phase_l1))
            steps.append(("mlp1", lambda: self.phase_mlp(1, self.x2, self.out, 32, "out")))
        for name, fn in steps:
            fn()
            if self.upto == name:
                break
        self.P.finish()


def _rope_cs(pos, dim):
    inv = (10000.0 ** (-np.arange(0, dim, 2, dtype=np.float32) / dim)).astype(np.float32)
    ang = pos.astype(np.float32)[:, None] * inv[None, :]
    return np.cos(ang).astype(np.float32), np.sin(ang).astype(np.float32)


def _rope_tab_axial(pos):
    pos = np.clip(pos, 0, NB - 1)
    cr, sr = _rope_cs(pos // 64, 32)
    cc, sc = _rope_cs(pos % 64, 32)
    return np.concatenate([cr, cr, cc, cc, -sr, sr, -sc, sc], 1).astype(np.float32)


def _rope_tab_1d(pos):
    pos = np.clip(pos, 0, NB - 1)
    c, s = _rope_cs(pos, 64)
    return np.concatenate([c, c, -s, s], 1).astype(np.float32)


def _pk(w, kchunks):
    K, N = w.shape
    return np.ascontiguousarray(w.reshape(kchunks, 128, N).transpose(1, 0, 2))


def _bias_tables(rpb, r0_list, rel_lists):
    outs = []
    ik = np.arange(128)
    for r0, rels in zip(r0_list, rel_lists):
        t = np.full((128, 16, len(rels), 128), NEG, np.float32)
        qrow = r0 + ik // 64
        qc = ik % 64
        rs = np.clip(qrow - 4, 0, 256 - 8)
        cs = np.clip(qc - 8, 0, 64 - 16)
        for ri, rel in enumerate(rels):
            krow = r0 + 2 * rel + ik // 64
            kc = ik % 64
            ok = ((krow[:, None] >= rs[None, :]) & (krow[:, None] < rs[None, :] + 8) &
                  (kc[:, None] >= cs[None, :]) & (kc[:, None] < cs[None, :] + 16) &
                  (krow[:, None] >= 0) & (krow[:, None] < 256) & (qrow[None, :] >= 0) & (qrow[None, :] < 256))
            dr = np.clip(krow[:, None] - qrow[None, :] + 7, 0, 14)
            dc = np.clip(kc[:, None] - qc[None, :] + 15, 0, 30)
            vals = rpb[:, dr, dc]
            t[:, :, ri, :] = np.where(ok[None], vals, np.float32(NEG)).transpose(1, 0, 2)
        outs.append(t)
    return outs


def _host_prep(inp):
    x = np.asarray(inp["x"], np.float32)
    shared = {}
    shared["ada_w"] = np.ascontiguousarray(
        np.asarray(inp["ada_w"], np.float32).reshape(4, 8, 128, 3072).transpose(0, 2, 1, 3))
    shared["ada_b"] = np.ascontiguousarray(np.asarray(inp["ada_b"], np.float32).reshape(4, 3072))
    shared["norm_g"] = np.ascontiguousarray(np.asarray(inp["norm_g"], np.float32).reshape(8, 1024))
    shared["ident"] = np.eye(128, dtype=np.float32)
    shared["w_up"] = np.stack([_pk(np.asarray(inp["mlp_w_up"][l], np.float32), 8) for l in range(2)])
    shared["w_down"] = np.stack([_pk(np.asarray(inp["mlp_w_down"][l], np.float32), 32) for l in range(2)])
    w_in = np.asarray(inp["ab_w_in"][0], np.float32)
    qa = w_in[:, 0:512].reshape(1024, 2, 4, 64).transpose(0, 2, 1, 3).reshape(1024, 512)
    shared["w_a"] = _pk(np.concatenate([qa, w_in[:, 512:768]], 1), 8)
    wb = []
    for g in range(3):
        cols = [w_in[:, 768 + part * 768 + g * 256: 768 + part * 768 + (g + 1) * 256] for part in range(3)]
        wb.append(_pk(np.concatenate(cols, 1), 8))
    shared["w_b"] = np.stack(wb)
    wo0 = np.asarray(inp["ab_w_out"][0], np.float32)
    shared["w_out0a"] = np.ascontiguousarray(wo0[0:512].reshape(8, 64, 1024).transpose(1, 0, 2))
    shared["w_out0b"] = _pk(wo0[512:768], 2)
    shared["gains"] = np.stack([np.asarray(inp["a_q_gain"][0], np.float32), np.asarray(inp["a_k_gain"][0], np.float32)])
    shared["ropeA_b"] = _rope_tab_axial(np.arange(NB))
    ik = np.arange(128)
    bm = np.full((128, 3, 128), NEG, np.float32)
    dk = ik[:, None] - ik[None, :]
    bm[:, 0, :] = np.where(dk >= 64, 0.0, NEG)
    bm[:, 1, :] = np.where(np.abs(dk) <= 64, 0.0, NEG)
    bm[:, 2, :] = np.where(dk <= -64, 0.0, NEG)
    shared["bmask"] = bm
    shared["w_in1"] = _pk(np.asarray(inp["c_w_in"][0], np.float32), 8)
    shared["w_out1"] = _pk(np.asarray(inp["c_w_out"][0], np.float32), 8)
    rpb = np.asarray(inp["c_rpb"][0], np.float32)
    shared["biasI"] = _bias_tables(rpb, [100], [[-2, -1, 0, 1, 2]])[0]
    per = []
    for core in range(8):
        b, q = core // 4, core % 4
        s = 4096 * q
        d = {}
        xp = np.zeros((WIN, D), np.float32)
        lo, hi = s - HALO - WPAD, s + 4096 + HALO + WPAD
        a, bnd = max(lo, 0), min(hi, NB)
        xp[a - lo:bnd - lo] = x[b, a:bnd]
        d["xw"] = xp
        d["xb"] = np.ascontiguousarray(x[b])
        d["cmod"] = np.ascontiguousarray(np.asarray(inp["c"], np.float32)[b].reshape(8, 128).T)
        wpos = np.arange(lo, hi)
        vw = ((wpos >= 0) & (wpos < NB)).astype(np.float32)[:, None]
        d["ropeB_w"] = np.concatenate([_rope_tab_1d(wpos), np.repeat(vw, 64, 1)], 1)
        d["ropeA_o"] = _rope_tab_axial(np.arange(s - HALO, s + 4096 + HALO))
        r_own0 = (s - HALO) // 64
        r0s = [r_own0 + 2 * jq for jq in (2, 3, 32, 33)]
        rl = [[-2, -1, 0, 1, 2, 3]] * 2 + [[-3, -2, -1, 0, 1, 2]] * 2
        d["biasE"] = np.stack(_bias_tables(rpb, r0s, rl))
        per.append(d)
    return shared, per


L0_KEYS = ["cmod", "ada_w", "ada_b", "norm_g", "ident", "w_up", "w_down", "xw", "xb", "w_a", "w_b", "w_out0a", "w_out0b", "gains",
           "ropeA_b", "ropeA_o", "ropeB_w", "bmask"]
L1_KEYS = ["cmod", "ada_w", "ada_b", "norm_g", "ident", "w_up", "w_down", "w_in1", "w_out1", "biasI", "biasE"]

MODE = "FUSED"


def _run(part, shared, per, extra=None, dbg=()):
    nc = bass.Bass("TRN2", target_bir_lowering=False)
    Builder(nc, part, dbg).build()
    keys = {"L0": L0_KEYS, "L1": L1_KEYS, "ALL": sorted(set(L0_KEYS + L1_KEYS))}[part]
    in_maps = []
    for core in range(8):
        m = {}
        for k in keys:
            m[k] = per[core][k] if k in per[core] else shared[k]
        if extra is not None:
            m.update(extra[core])
        in_maps.append(m)
    res = run_bass_kernel_spmd(nc, in_maps, core_ids=list(range(8)))
    return res.results


def kernel(**inputs):
    shared, per = _host_prep(inputs)
    if MODE == "SPLIT":
        r0 = _run("L0", shared, per)
        extra = [{"x1": np.asarray(r0[c]["x1"], np.float32)} for c in range(8)]
        r1 = _run("L1", shared, per, extra)
    else:
        r1 = _run("ALL", shared, per)
    out = np.empty((2, NB, D), np.float32)
    for core in range(8):
        b, q = core // 4, core % 4
        out[b, 4096 * q:4096 * (q + 1)] = np.asarray(r1[core]["out"], np.float32)
    return out
```

```python
import numpy as np
from contextlib import ExitStack
import concourse.bass as bass
import concourse.mybir as mybir
from concourse.bass_utils import run_bass_kernel_spmd

F32 = mybir.dt.float32
BF16 = mybir.dt.bfloat16
AF = mybir.ActivationFunctionType
ALU = mybir.AluOpType
AX = mybir.AxisListType

D = 1024
NB = 16384
NBT = 128
OWN = 4608
OWNT = 36
HALO = 256
WPAD = 2048
WIN = OWN + 2 * WPAD
WINT = WIN // 128
OWN0T = WPAD // 128
EPS = 1e-6
NEG = -30000.0
VS = 72


class Prog:
    ENGS = ("pe", "act", "dve", "pool", "sp")

    def __init__(self, nc):
        self.nc = nc
        self.root = ExitStack()
        self.scopes = []
        self.ops = []
        self.nalloc = 0
        self.freed = []
        self.alias = {}
        self.scope_names = []

    def _stack(self):
        return self.scopes[-1] if self.scopes else self.root

    def push(self):
        self.scopes.append(ExitStack())
        self.scope_names.append([])

    def pop(self):
        self.scopes.pop().close()
        self.freed.extend(self.scope_names.pop())

    def sb(self, name, shape, dtype):
        self.nalloc += 1
        nm = "%s_%d" % (name, self.nalloc)
        t = self._stack().enter_context(self.nc.sbuf_tensor(nm, list(shape), dtype))
        if self.scope_names:
            self.scope_names[-1].append(nm)
        if self.freed:
            self.alias[nm] = len(self.freed)
        return t

    def ps(self, name, shape, dtype):
        return self.root.enter_context(self.nc.psum_tensor(name, list(shape), dtype))

    def dram(self, name, shape, dtype, kind):
        return self.nc.dram_tensor(name, list(shape), dtype, kind=kind).ap()

    def rot(self, name, shape, dtype, n):
        ts = [self.sb("%s%d" % (name, i), shape, dtype) for i in range(n)]
        st = {"i": -1}

        def nxt():
            st["i"] += 1
            return ts[st["i"] % n]
        nxt.tensors = ts
        return nxt

    def op(self, eng, fn, reads, writes, dma=None):
        rd = [r if isinstance(r, str) else r.name for r in reads]
        wr = [w if isinstance(w, str) else w.name for w in writes]
        self.ops.append(dict(eng=eng, fn=fn, reads=rd, writes=wr, dma=dma))

    def dma(self, q, out, in_, sem, reads=None, writes=None):
        self.op(q, lambda e: e.dma_start(out=out, in_=in_),
                [in_] if reads is None else reads, [out] if writes is None else writes, dma=sem)

    def mm(self, out, lhsT, rhs, start, stop, reads, writes, **kw):
        self.op("pe", lambda e: e.matmul(out, lhsT=lhsT, rhs=rhs, start=start, stop=stop, **kw), reads, writes)

    def tr(self, out, in_, ident, reads, writes):
        self.op("pe", lambda e: e.transpose(out, in_, ident), reads, writes)

    def act(self, out, in_, func, reads, writes, **kw):
        self.op("act", lambda e: e.activation(out, in_, func, **kw), reads, writes)

    def copy(self, eng, out, in_, reads, writes):
        if eng == "act":
            self.op("act", lambda e: e.copy(out, in_), reads, writes)
        else:
            self.op(eng, lambda e: e.tensor_copy(out, in_), reads, writes)

    def tt(self, eng, out, in0, in1, op, reads, writes):
        self.op(eng, lambda e: e.tensor_tensor(out=out, in0=in0, in1=in1, op=op), reads, writes)

    def ts(self, out, in0, s1, s2, op0, op1, reads, writes):
        if op1 is None:
            self.op("dve", lambda e: e.tensor_scalar(out=out, in0=in0, scalar1=s1, scalar2=None, op0=op0), reads, writes)
        else:
            self.op("dve", lambda e: e.tensor_scalar(out=out, in0=in0, scalar1=s1, scalar2=s2, op0=op0, op1=op1), reads, writes)

    def stt(self, out, in0, scalar, in1, op0, op1, reads, writes):
        self.op("dve", lambda e: e.scalar_tensor_tensor(out=out, in0=in0, scalar=scalar, in1=in1, op0=op0, op1=op1), reads, writes)

    def memset(self, eng, out, val, writes):
        self.op(eng, lambda e: e.memset(out, val), [], writes)

    def finish(self):
        nc = self.nc
        ops = self.ops
        wstate, rstate = {}, {}
        seen = set()
        deps = [None] * len(ops)
        needed = [False] * len(ops)
        for i, o in enumerate(ops):
            sk = ("dma", o["dma"]) if o["dma"] else ("eng", o["eng"])
            o["sk"] = sk
            d = {}
            isdma = bool(o["dma"])
            ispe = o["eng"] == "pe"
            for t in o["reads"] + o["writes"]:
                if t in self.alias and t not in seen:
                    seen.add(t)
                    mr = rstate.setdefault(t, {})
                    for a in self.freed[:self.alias[t]]:
                        for stt_ in (wstate.get(a), rstate.get(a)):
                            if stt_:
                                for skp, j in stt_.items():
                                    if mr.get(skp, -1) < j:
                                        mr[skp] = j
            for t in o["reads"]:
                for skp, j in wstate.get(t, {}).items():
                    if skp == sk and (isdma or ispe):
                        continue
                    if d.get(skp, -1) < j:
                        d[skp] = j
            for t in o["writes"]:
                for skp, j in wstate.get(t, {}).items():
                    if skp == sk:
                        continue
                    if d.get(skp, -1) < j:
                        d[skp] = j
                for skp, j in rstate.get(t, {}).items():
                    if skp == sk:
                        continue
                    if d.get(skp, -1) < j:
                        d[skp] = j
            deps[i] = d
            for j in d.values():
                needed[j] = True
            for t in o["reads"]:
                rstate.setdefault(t, {})[sk] = i
            for t in o["writes"]:
                wstate.setdefault(t, {})[sk] = i
        cnt = {}
        val = [None] * len(ops)
        issued_at = [None] * len(ops)
        run = {}
        for i, o in enumerate(ops):
            sk = o["sk"]
            if o["dma"]:
                cnt[sk] = cnt.get(sk, 0) + 16
                val[i] = cnt[sk]
                run[sk] = val[i]
            elif needed[i]:
                cnt[sk] = cnt.get(sk, 0) + 1
                val[i] = cnt[sk]
            issued_at[i] = dict(run) if deps[i] and any(k[0] == "dma" for k in deps[i]) else None
        sems = {}
        for sk in sorted(cnt, key=str):
            sems[sk] = self.root.enter_context(nc.semaphore("s_%s_%s" % sk))
        self.n_sems = len(sems)
        self.cnt = dict(cnt)
        per = {e: [] for e in self.ENGS}
        for i, o in enumerate(ops):
            per[o["eng"]].append(i)

        def emit(engname, e):
            waited = {}
            for i in per[engname]:
                o = ops[i]
                for skp in sorted(deps[i], key=str):
                    v = val[deps[i][skp]]
                    if skp[0] == "dma":
                        v = issued_at[i][skp]
                    if waited.get(skp, 0) >= v:
                        continue
                    e.wait_ge(sems[skp], v)
                    waited[skp] = v
                ins = o["fn"](e)
                if o["dma"]:
                    ins.then_inc(sems[o["sk"]], 16)
                elif needed[i]:
                    ins.then_inc(sems[o["sk"]], 1)
            if engname == "sp":
                for sk in sorted(cnt, key=str):
                    if sk[0] == "dma" and waited.get(sk, 0) < cnt[sk]:
                        e.wait_ge(sems[sk], cnt[sk])

        with nc.Block() as block:
            @block.tensor
            def _(e):
                emit("pe", e)

            @block.scalar
            def _(e):
                emit("act", e)

            @block.vector
            def _(e):
                emit("dve", e)

            @block.gpsimd
            def _(e):
                emit("pool", e)

            @block.sync
            def _(e):
                emit("sp", e)
        while self.scopes:
            self.pop()
        self.root.close()


def rows_ap(t, row0, nrows, rstride, ncols, rowlen):
    return bass.AP(t.tensor, row0 * rowlen, [[rstride * rowlen, nrows], [1, ncols]])


class Builder:
    def __init__(self, nc, part, dbg=(), upto=None):
        self.nc = nc
        self.part = part
        self.dbg = set(dbg)
        self.upto = upto
        P = self.P = Prog(nc)
        L0 = part in ("L0", "ALL")
        L1 = part in ("L1", "ALL")
        I = "ExternalInput"
        self.cmod = P.dram("cmod", [128, 8], F32, I)
        self.ada_w = P.dram("ada_w", [4, 128, 8, 3072], F32, I)
        self.ada_b = P.dram("ada_b", [4, 3072], F32, I)
        self.norm_g = P.dram("norm_g", [8, 1024], F32, I)
        self.ident_in = P.dram("ident", [128, 128], F32, I)
        self.w_up = P.dram("w_up", [2, 128, 8, 4096], F32, I)
        self.w_down = P.dram("w_down", [2, 128, 32, 1024], F32, I)
        if L0:
            self.xw = P.dram("xw", [WIN, D], F32, I)
            self.xb = P.dram("xb", [NB, D], F32, I)
            self.w_a = P.dram("w_a", [128, 8, 768], F32, I)
            self.w_b = P.dram("w_b", [3, 128, 8, 768], F32, I)
            self.w_out0 = P.dram("w_out0", [128, 6, 1024], F32, I)
            self.gains = P.dram("gains", [2, 64], F32, I)
            self.ropeA_b = P.dram("ropeA_b", [NB, 128], F32, I)
            self.ropeA_o = P.dram("ropeA_o", [OWN, 128], F32, I)
            self.ropeB_w = P.dram("ropeB_w", [WIN, 192], F32, I)
            self.bmask = P.dram("bmask", [128, 3, 128], F32, I)
        if L1:
            self.w_in1 = P.dram("w_in1", [128, 8, 3072], F32, I)
            self.w_out1 = P.dram("w_out1", [128, 8, 1024], F32, I)
            self.biasI = P.dram("biasI", [128, 16, 5, 128], F32, I)
            self.biasE = P.dram("biasE", [4, 128, 16, 6, 128], F32, I)
        if part == "L0":
            self.x1 = P.dram("x1", [OWN, D], F32, "ExternalOutput")
        elif part == "L1":
            self.x1 = P.dram("x1", [OWN, D], F32, I)
        else:
            self.x1 = P.dram("x1", [OWN, D], F32, "Internal")
        if L1:
            self.out = P.dram("out", [4096, D], F32, "ExternalOutput")
        self.modrows = P.dram("modrows", [12, D], F32, "Internal")
        if L0:
            self.h_w = P.dram("h_w", [WIN, D], BF16, "Internal")
            self.OB = P.dram("OB", [OWN, 3 * 260], F32, "Internal")
            self.x_mid = P.dram("x_mid", [OWN, D], F32, "Internal")
        if L1:
            self.x2 = P.dram("x2", [4096, D], F32, "Internal")
        self.dbg_out = {}
        self.psum = P.ps("psum", [128, 4096], F32)
        self.psum_bf = self.psum[:].bitcast(BF16)
        self.ident = P.sb("ident", [128, 128], BF16)
        P.dma("pool", self.ident[:], self.ident_in, "c_ident")
        self.m05 = P.sb("m05", [128, 16], F32)
        P.memset("pool", self.m05[:], -0.5, [self.m05])
        self.junk = P.sb("junk", [128, 1024], BF16)
        self.small = P.rot("small", [128, 16], F32, 12)

    def bank(self, b0, ncols=512, p0=0, p1=128):
        return self.psum[p0:p1, b0 * 512: b0 * 512 + ncols]

    def bank_bf(self, b0, ncols=1024):
        return self.psum_bf[:, b0 * 1024: b0 * 1024 + ncols]

    def dbg_dump_dram(self, name, src_ap, shape, dtype):
        if name in self.dbg:
            o = self.P.dram("dbg_" + name, shape, dtype, "ExternalOutput")
            self.P.dma("sp", o, src_ap, "dbg", reads=[src_ap.name], writes=["dbg_" + name])

    def dbg_dump_sb(self, name, t, shape, dtype):
        if name in self.dbg:
            o = self.P.dram("dbg_" + name, shape, dtype, "ExternalOutput")
            self.P.dma("sp", o, t, "dbg", reads=[t.name], writes=["dbg_" + name])

    def load_bc(self, m, which, name):
        t = self.P.sb(name, [128, D], F32)
        r = 3 * m + which
        self.P.dma("sp", t[:], self.modrows[r:r + 1, :].partition_broadcast(128), "ld_bc",
                   reads=["modrows"], writes=[t])
        return t

    def rstd_from_ss(self, ss, n, inv):
        P = self.P
        v = self.small()
        P.ts(v[:, 0:n], ss, inv, EPS, ALU.mult, ALU.add, [ss], [v])
        r = self.small()
        P.tt("pool", r[:, 0:n], v[:, 0:n], self.m05[:, 0:n], ALU.pow, [v, self.m05], [r])
        return r

    def prenorm(self, xt, A_bc, B_bc, h_out, tmp):
        P = self.P
        ss = self.small()
        P.memset("pool", ss[:, 0:1], 0.0, [ss])
        P.act(self.junk[:], xt[:], AF.Square, [xt], [self.junk, ss], accum_out=ss[:, 0:1])
        r = self.rstd_from_ss(ss[:, 0:1], 1, 1.0 / D)
        P.stt(tmp[:], xt[:], r[:, 0:1], A_bc[:], ALU.mult, ALU.mult, [xt, r, A_bc], [tmp])
        P.tt("pool", h_out[:], tmp[:], B_bc[:], ALU.add, [tmp, B_bc], [h_out])

    def transposes(self, src, ncol, bankno, dst_ap, dst_key, eng="act", src_key=None):
        P = self.P
        key = "ps%d" % bankno
        pb = self.bank_bf(bankno, ncol)
        sk = src_key if src_key is not None else src
        for c in range(ncol // 128):
            P.tr(pb[:, c * 128:(c + 1) * 128], src[:, c * 128:(c + 1) * 128], self.ident[:],
                 [sk, self.ident], [key])
        return pb, key

    def post_residual(self, ykey, yap, xt, G_bc, xout, tmp):
        P = self.P
        ss = self.small()
        P.memset("pool", ss[:, 0:1], 0.0, [ss])
        P.act(self.junk[:], yap, AF.Square, [ykey], [self.junk, ss], accum_out=ss[:, 0:1])
        r = self.rstd_from_ss(ss[:, 0:1], 1, 1.0 / D)
        P.stt(tmp[:], yap, r[:, 0:1], G_bc[:], ALU.mult, ALU.mult, [ykey, r, G_bc], [tmp])
        P.tt("pool", xout[:], tmp[:], xt[:], ALU.add, [tmp, xt], [xout])

    def qk_rope(self, src_ap, src_key, H, gain, rope_t, out_bf, axial, st):
        self.qk_rope_a(src_ap, src_key, H, st["qs"])
        self.qk_rope_b(st["qs"], H, gain, rope_t, out_bf, axial, st)

    def qk_rope_a(self, src_ap, src_key, H, qs):
        self.P.copy("act", qs[:, 0:H * 64], src_ap, [src_key], [qs])

    def qk_rope_b(self, qs, H, gain, rope_t, out_bf, axial, st):
        P = self.P
        W = H * 64
        t1, t2 = st["t1"], st["t2"]
        v3 = lambda t: t[:, 0:W].rearrange("p (h d) -> p h d", d=64)
        if gain is not None:
            P.tt("pool", t1[:, 0:W], qs[:, 0:W], qs[:, 0:W], ALU.mult, [qs], [t1])
            ssq = self.small()
            self.P.op("dve", lambda e: e.tensor_reduce(out=ssq[:, 0:H], in_=v3(t1), axis=AX.X, op=ALU.add), [t1], [ssq])
            r = self.rstd_from_ss(ssq[:, 0:H], H, 1.0 / 64)
            P.tt("dve", v3(t2), v3(qs), r[:, 0:H].unsqueeze(2).to_broadcast([128, H, 64]), ALU.mult, [qs, r], [t2])
            P.tt("pool", v3(qs), v3(t2), gain[:, 0:64].unsqueeze(1).to_broadcast([128, H, 64]), ALU.mult, [t2, gain], [qs])
        cosb = rope_t[:, 0:64].unsqueeze(1).to_broadcast([128, H, 64])
        P.tt("dve", v3(t1), v3(qs), cosb, ALU.mult, [qs, rope_t], [t1])
        if axial:
            hv = lambda t, off: bass.AP(t[:].tensor, off, [[t[:].ap[0][0], 128], [64, H], [32, 2], [1, 16]])
            sv = lambda off: bass.AP(rope_t[:].tensor, 64 + off, [[rope_t[:].ap[0][0], 128], [0, H], [32, 2], [1, 16]])
            hw = 16
        else:
            hv = lambda t, off: bass.AP(t[:].tensor, off, [[t[:].ap[0][0], 128], [64, H], [1, 32]])
            sv = lambda off: bass.AP(rope_t[:].tensor, 64 + off, [[rope_t[:].ap[0][0], 128], [0, H], [1, 32]])
            hw = 32
        P.tt("pool", hv(t2, 0), hv(qs, hw), sv(0), ALU.mult, [qs, rope_t], [t2])
        P.tt("pool", hv(t2, hw), hv(qs, 0), sv(hw), ALU.mult, [qs, rope_t], [t2])
        P.tt("dve", out_bf, t1[:, 0:W], t2[:, 0:W], ALU.add, [t1, t2], [out_bf.name if hasattr(out_bf, "name") else out_bf])

    def phase_mods(self, ms):
        P = self.P
        P.push()
        cT = P.sb("cT", [128, 8], F32)
        condT = P.sb("condT", [128, 8], F32)
        P.dma("sp", cT[:], self.cmod, "ld_c")
        P.act(condT[:], cT[:], AF.Silu, [cT], [condT])
        brow = P.sb("brow", [1, 3072], F32)
        grow = P.sb("grow", [1, 2, D], F32)
        mrow = P.sb("mrow", [1, 3072], F32)
        orow = P.sb("orow", [1, 3, D], F32)
        wch = P.rot("wch", [128, 8, 512], F32, 2)
        for m in ms:
            P.dma("sp", brow[:], self.ada_b[m:m + 1, :], "ld_b")
            P.dma("sp", grow[:], self.norm_g[2 * m:2 * m + 2, :].rearrange("(o r) d -> o r d", o=1), "ld_g")
            for n6 in range(6):
                w = wch()
                P.dma("sp", w[:], self.ada_w[m, :, :, n6 * 512:(n6 + 1) * 512], "ld_w%d" % (n6 % 2))
                for kc in range(8):
                    P.mm(self.bank(0, 512, 0, 1), condT[:, kc:kc + 1], w[:, kc, :], kc == 0, kc == 7,
                         [condT, w], ["ps0"])
                P.tt("dve", mrow[:, n6 * 512:(n6 + 1) * 512], self.bank(0, 512, 0, 1), brow[:, n6 * 512:(n6 + 1) * 512],
                     ALU.add, ["ps0", brow], [mrow])
            P.stt(orow[:, 0, :], mrow[:, D:2 * D], 1.0, grow[:, 0, :], ALU.add, ALU.mult, [mrow, grow], [orow])
            P.copy("dve", orow[:, 1, :], mrow[:, 0:D], [mrow], [orow])
            P.tt("dve", orow[:, 2, :], mrow[:, 2 * D:3 * D], grow[:, 1, :], ALU.mult, [mrow, grow], [orow])
            P.dma("sp", self.modrows[3 * m:3 * m + 3, :].rearrange("(o r) d -> o r d", o=1), orow[:], "st_mod",
                  reads=[orow], writes=["modrows"])
        P.pop()
        self.dbg_dump_dram("modrows", self.modrows, [12, D], F32)

    def phase_l0_prep(self):
        P = self.P
        P.push()
        self.KAT = P.sb("KAT", [128, NB], BF16)
        self.VA = P.sb("VA", [128, NBT, 2, VS], BF16)
        self.QAT = P.sb("QAT", [128, 4, OWN], BF16)
        P.memset("pool", self.VA[:, :, :, 64:65], 1.0, [self.VA])
        P.push()
        WA = P.sb("WA", [128, 8, 768], BF16)
        P.dma("pool", WA[:], self.w_a, "ld_WA")
        gains = P.sb("gains", [128, 2, 64], F32)
        P.dma("sp", gains[:], bass.AP(self.gains.tensor, 0, [[0, 128], [64, 2], [1, 64]]), "ld_gain", reads=[], writes=[gains])
        A_bc = self.load_bc(0, 0, "A_bc")
        B_bc = self.load_bc(0, 1, "B_bc")
        xrot = P.rot("xt", [128, D], F32, 3)
        tmp = P.rot("tmp", [128, D], F32, 2)
        hrot = P.rot("h", [128, D], BF16, 3)
        hTrot = P.rot("hT", [128, 8, 128], BF16, 2)
        rrot = P.rot("ropeT", [128, 128], F32, 3)
        st = dict(qs=P.sb("qs", [128, 512], F32), t1=P.sb("t1", [128, 512], F32), t2=P.sb("t2", [128, 512], F32))
        qbf = P.rot("qbf", [128, 512], BF16, 2)
        nld = [0]

        def load_x(src, row0):
            xt = xrot()
            k = nld[0] % 3
            nld[0] += 1
            P.dma("sp", xt[:], src[row0:row0 + 128, :], "ld_x%d" % k)
            return xt, k

        qsrot = P.rot("qsr", [128, 512], F32, 2)

        def skew(stages):
            prevB = None
            for A, Bst in stages:
                A()
                if prevB is not None:
                    prevB()
                prevB = Bst
            if prevB is not None:
                prevB()

        stages = []
        for ti in range(WINT):
            t = ti - OWN0T
            own = 0 <= t < OWNT
            box = {}

            def A(ti=ti, t=t, own=own, box=box):
                xt, k = load_x(self.xw, ti * 128)
                h = hrot()
                self.prenorm(xt, A_bc, B_bc, h, tmp())
                P.dma("pool", self.h_w[ti * 128:(ti + 1) * 128, :], h[:], "st_h%d" % k, reads=[h], writes=["h_w"])
                if own:
                    rt = rrot()
                    P.dma("sp", rt[:], self.ropeA_o[t * 128:(t + 1) * 128, :], "ld_r%d" % (t % 3))
                    hT = hTrot()
                    pb, key = self.transposes(h, D, 0, None, None)
                    P.copy("act", hT[:].rearrange("p k t -> p (k t)"), pb, [key], [hT])
                    for kc in range(8):
                        P.mm(self.bank(1), hT[:, kc, :], WA[:, kc, 0:512], kc == 0, kc == 7, [hT, WA], ["ps1"])
                    qs = qsrot()
                    self.qk_rope_a(self.bank(1), "ps1", 8, qs)
                    box["rt"], box["qs"] = rt, qs

            def Bst(t=t, own=own, box=box):
                if not own:
                    return
                qb = qbf()
                self.qk_rope_b(box["qs"], 8, gains[:, 0, :], box["rt"], qb[:], True, st)
                pb2, key2 = self.transposes(qb, 512, 2, None, None)
                P.copy("act", self.QAT[:, :, t * 128:(t + 1) * 128], pb2.rearrange("p (g t) -> p g t", g=4), [key2], [self.QAT])
            stages.append((A, Bst))
        skew(stages)
        stages = []
        for c in range(NBT):
            box = {}

            def A(c=c, box=box):
                xt, k = load_x(self.xb, c * 128)
                h = hrot()
                self.prenorm(xt, A_bc, B_bc, h, tmp())
                rt = rrot()
                P.dma("sp", rt[:], self.ropeA_b[c * 128:(c + 1) * 128, :], "ld_rb%d" % (c % 3))
                hT = hTrot()
                pb, key = self.transposes(h, D, 0, None, None)
                P.copy("act", hT[:].rearrange("p k t -> p (k t)"), pb, [key], [hT])
                for kc in range(8):
                    P.mm(self.bank(1, 256), hT[:, kc, :], WA[:, kc, 512:768], kc == 0, kc == 7, [hT, WA], ["ps1"])
                qs = qsrot()
                self.qk_rope_a(self.bank(1, 128), "ps1", 2, qs)
                P.copy("act", self.VA[:, c, :, 0:64], self.psum[:, 512 + 128:512 + 256].rearrange("p (h d) -> p h d", d=64),
                       ["ps1"], [self.VA])
                box["rt"], box["qs"] = rt, qs

            def Bst(c=c, box=box):
                kb = qbf()
                self.qk_rope_b(box["qs"], 2, gains[:, 1, :], box["rt"], kb[:, 0:128], True, st)
                pb2, key2 = self.transposes(kb, 128, 2, None, None)
                P.copy("dve", self.KAT[:, c * 128:(c + 1) * 128], pb2, [key2], [self.KAT])
            stages.append((A, Bst))
        skew(stages)
        P.pop()
        self.dbg_dump_dram("h_w", self.h_w, [WIN, D], BF16)
        self.dbg_dump_sb("KAT", self.KAT[:], [128, NB], BF16)
        self.dbg_dump_sb("VA", self.VA[:], [128, NBT, 2, VS], BF16)
        self.dbg_dump_sb("QAT", self.QAT[:], [128, 4, OWN], BF16)

    def phase_l0_B(self):
        P = self.P
        P.push()
        bmask = P.sb("bmask", [128, 3, 128], BF16)
        P.dma("pool", bmask[:], self.bmask, "ld_bmask")
        hrot = P.rot("hB", [128, D], BF16, 3)
        hTrot = P.rot("hTB", [128, 8, 128], BF16, 2)
        rrot = P.rot("ropeB", [128, 192], F32, 3)
        st = dict(t1=P.sb("t1B", [128, 512], F32), t2=P.sb("t2B", [128, 512], F32))
        qsrot = P.rot("qsrB", [128, 512], F32, 2)
        qkbf = P.rot("qkbf", [128, 512], BF16, 2)
        QKT = P.rot("QKT", [128, 4, 128], BF16, 4)
        VB = P.rot("VB", [128, 4, VS], BF16, 4)
        PT = P.rot("PTB", [128, 1536], BF16, 2)
        osb = P.rot("osb", [128, 260], F32, 2)
        WB = P.rot("WB", [128, 8, 768], BF16, 2)
        nchunk = [0]
        nblk = [0]
        for g, d in enumerate((1, 4, 16)):
            W = WB()
            P.dma("pool", W[:], self.w_b[g], "ld_WB%d" % (g % 2))
            own_u = OWN // d
            nb = (own_u + 127) // 128
            jmax = (own_u + 64 + 127) // 128 - 1
            for r in range(d):
                chunks = {}
                boxes = {}
                nxt = [0]

                def prodA(j, r=r, W=W, d=d, boxes=boxes):
                    n = nchunk[0]
                    nchunk[0] += 1
                    w0 = WPAD + r + d * 128 * j
                    h = hrot()
                    P.dma("sp", h[:], rows_ap(self.h_w, w0, 128, d, D, D), "ld_hB%d" % (n % 3), reads=["h_w"], writes=[h])
                    rt = rrot()
                    P.dma("sp", rt[:], rows_ap(self.ropeB_w, w0, 128, d, 192, 192), "ld_rB%d" % (n % 3), reads=[], writes=[rt])
                    vt = rt[:, 128:129]
                    hT = hTrot()
                    pb, key = self.transposes(h, D, 0, None, None)
                    P.copy("act", hT[:].rearrange("p k t -> p (k t)"), pb, [key], [hT])
                    for kc in range(8):
                        P.mm(self.bank(1), hT[:, kc, :], W[:, kc, 0:512], kc == 0, kc == 7, [hT, W], ["ps1"])
                    for kc in range(8):
                        P.mm(self.bank(2, 256), hT[:, kc, :], W[:, kc, 512:768], kc == 0, kc == 7, [hT, W], ["ps2"])
                    qs = qsrot()
                    self.qk_rope_a(self.bank(1), "ps1", 8, qs)
                    V = VB()
                    P.act(V[:, :, 0:64], self.bank(2, 256).rearrange("p (h d) -> p h d", d=64), AF.Identity, ["ps2", vt], [V],
                          scale=vt)
                    P.copy("dve", V[:, :, 64:65], vt.unsqueeze(1).to_broadcast([128, 4, 1]), [vt], [V])
                    boxes[j] = (rt, qs, V)

                def prodB(j, boxes=boxes, chunks=chunks):
                    rt, qs, V = boxes.pop(j)
                    qk = qkbf()
                    self.qk_rope_b(qs, 8, None, rt, qk[:], False, st)
                    T = QKT()
                    pb2, key2 = self.transposes(qk, 512, 3, None, None)
                    P.copy("dve", T[:].rearrange("p a t -> p (a t)"), pb2, [key2], [T])
                    chunks[j] = (T, V)

                def attend(jb, r=r, d=d, g=g, chunks=chunks, own_u=own_u):
                    rels = [rel for rel in (-1, 0, 1) if (jb + rel) in chunks]
                    nr = len(rels)
                    Tq = chunks[jb][0]
                    bi = nblk[0]
                    nblk[0] += 1
                    for hb in range(4):
                        pr, hf = hb // 2, hb % 2
                        for ri, rel in enumerate(rels):
                            Tk = chunks[jb + rel][0]
                            col = 4 * 512 + (hb * nr + ri) * 128
                            so = self.psum[:, col:col + 128]
                            P.mm(so, Tk[hf * 64:(hf + 1) * 64, 2 + pr, :], Tq[hf * 64:(hf + 1) * 64, pr, :], True, False,
                                 [Tk, Tq], ["psS"])
                            P.mm(so, self.ident[:], bmask[:, rel + 1, :], False, True, [self.ident, bmask], ["psS"])
                    pt = PT()
                    ncol = 4 * nr * 128
                    P.act(pt[:, 0:ncol], self.psum[:, 4 * 512:4 * 512 + ncol], AF.Exp, ["psS"], [pt], scale=0.125)
                    for hb in range(4):
                        for ri, rel in enumerate(rels):
                            Vk = chunks[jb + rel][1]
                            blk = (hb * nr + ri) * 128
                            P.mm(self.psum[:, 7 * 512 + hb * 65: 7 * 512 + hb * 65 + 65], pt[:, blk:blk + 128], Vk[:, hb, 0:65],
                                 ri == 0, ri == nr - 1, [pt, Vk], ["psO"])
                    o = osb()
                    P.copy("dve", o[:], self.psum[:, 7 * 512:7 * 512 + 260], ["psO"], [o])
                    nq = min(128, own_u - jb * 128)
                    t0 = r + d * 128 * jb
                    dst = bass.AP(self.OB.tensor, t0 * 780 + g * 260, [[d * 780, nq], [1, 260]])
                    P.dma("pool", dst, o[0:nq, :], "st_OB%d" % (bi % 2), reads=[o], writes=["OB"])

                def attend_ready(jdone):
                    while nxt[0] < nb and min(nxt[0] + 1, jmax) <= jdone:
                        attend(nxt[0])
                        nxt[0] += 1

                js = list(range(-1, jmax + 1))
                for idx, j in enumerate(js):
                    prodA(j)
                    if idx >= 1:
                        prodB(js[idx - 1])
                        attend_ready(js[idx - 1])
                prodB(js[-1])
                attend_ready(js[-1])
        P.pop()
        self.dbg_dump_dram("OB", self.OB, [OWN, 780], F32)

    def phase_l0_attnA(self):
        P = self.P
        P.push()
        Wout = P.sb("Wout0", [128, 6, D], BF16)
        P.dma("pool", Wout[:], self.w_out0, "ld_Wout0")
        G_bc = self.load_bc(0, 2, "G_bc0")
        PT = P.rot("PTA", [128, 1024], BF16, 3)
        obrot = P.rot("obin", [128, 780], F32, 2)
        xrot = P.rot("xtA", [128, D], F32, 2)
        orot = P.rot("otok", [128, 768], BF16, 2)
        oTrot = P.rot("oT", [128, 6, 128], BF16, 2)
        tmp = P.rot("tmpA", [128, D], F32, 2)
        xo = P.rot("xoA", [128, D], F32, 2)
        bsum = P.sb("bsum", [128, 260], F32)
        qzrot = P.rot("qz", [128, 2, 4, 128], BF16, 2)
        for z in qzrot.tensors:
            P.memset("pool", z[:], 0.0, [z])
        for t in range(OWNT):
            q0 = t * 128
            xt = xrot()
            P.dma("sp", xt[:], self.xw[WPAD + q0:WPAD + q0 + 128, :], "ld_xA%d" % (t % 2))
            ob = obrot()
            P.dma("sp", ob[:], self.OB[q0:q0 + 128, :], "ld_ob%d" % (t % 2), reads=["OB"], writes=[ob])
            first = {0: True, 1: True}
            o = orot()
            qz = qzrot()
            P.copy("pool", qz[0:64, 0, :, :], self.QAT[0:64, :, q0:q0 + 128], [self.QAT], [qz])
            P.copy("pool", qz[64:128, 1, :, :], self.QAT[64:128, :, q0:q0 + 128], [self.QAT], [qz])
            steps = [(kv, c2) for kv in range(2) for c2 in range(NBT // 2)]
            pts = {}

            def emitS(k):
                kv, c2 = steps[k]
                slot = k % 2
                skey = "psSA%d" % slot
                for u in range(2):
                    c = 2 * c2 + u
                    so = self.psum[:, slot * 1024 + u * 512: slot * 1024 + (u + 1) * 512]
                    P.mm(so, self.KAT[:, c * 128:(c + 1) * 128], qz[:, kv, :, :], True, True, [self.KAT, qz], [skey])

            def emitExp(k):
                slot = k % 2
                pt = PT()
                pts[k] = pt
                P.act(pt[:], self.psum[:, slot * 1024:(slot + 1) * 1024], AF.Exp, ["psSA%d" % slot], [pt], scale=0.125)

            def emitPV(k):
                kv, c2 = steps[k]
                okey = "psOA%d" % kv
                pt = pts.pop(k)
                for u in range(2):
                    c = 2 * c2 + u
                    for gq in range(4):
                        oo = self.psum[:, (4 + kv) * 512 + gq * 65:(4 + kv) * 512 + gq * 65 + 65]
                        P.mm(oo, pt[:, u * 512 + gq * 128: u * 512 + (gq + 1) * 128], self.VA[:, c, kv, 0:65],
                             first[kv], False, [pt, self.VA], [okey], skip_group_check=True)
                        first[kv] = False

            def normalize(kv):
                okey = "psOA%d" % kv
                ov = self.psum[:, (4 + kv) * 512:(4 + kv) * 512 + 260].rearrange("p (g d) -> p g d", d=65)
                rl = self.small()
                P.op("dve", (lambda rl=rl, ov=ov: (lambda e: e.reciprocal(rl[:, 0:4], ov[:, :, 64])))(), [okey], [rl])
                P.tt("dve", o[:, kv * 256:(kv + 1) * 256].rearrange("p (g d) -> p g d", d=64), ov[:, :, 0:64],
                     rl[:, 0:4].unsqueeze(2).to_broadcast([128, 4, 64]), ALU.mult, [okey, rl], [o])

            emitS(0)
            for k in range(len(steps)):
                if k + 1 < len(steps):
                    emitS(k + 1)
                emitExp(k)
                emitPV(k)
                if k == NBT // 2 - 1:
                    normalize(0)
            normalize(1)
            obv = ob[:].rearrange("p (g c) -> p g c", g=3)
            P.tt("pool", bsum[:], obv[:, 0, :], obv[:, 1, :], ALU.add, [ob], [bsum])
            P.tt("pool", bsum[:], bsum[:], obv[:, 2, :], ALU.add, [ob, bsum], [bsum])
            bv = bsum[:].rearrange("p (g d) -> p g d", d=65)
            rlb = self.small()
            P.op("dve", (lambda rlb=rlb, bv=bv: (lambda e: e.reciprocal(rlb[:, 0:4], bv[:, :, 64])))(), [bsum], [rlb])
            P.tt("dve", o[:, 512:768].rearrange("p (g d) -> p g d", d=64), bv[:, :, 0:64],
                 rlb[:, 0:4].unsqueeze(2).to_broadcast([128, 4, 64]), ALU.mult, [bsum, rlb], [o])
            oT = oTrot()
            pb, key = self.transposes(o, 768, 6, None, None)
            P.copy("dve", oT[:].rearrange("p k t -> p (k t)"), pb, [key], [oT])
            for n2 in range(2):
                for kc in range(6):
                    P.mm(self.bank(6 + n2), oT[:, kc, :], Wout[:, kc, n2 * 512:(n2 + 1) * 512], kc == 0, kc == 5,
                         [oT, Wout], ["ps%d" % (6 + n2)])
            xout = xo()
            self.post_residual_2(6, xt, G_bc, xout, tmp())
            P.dma("pool", self.x_mid[q0:q0 + 128, :], xout[:], "st_xm%d" % (t % 2), reads=[xout], writes=["x_mid"])
        P.pop()
        P.pop()
        self.dbg_dump_dram("x_mid", self.x_mid, [OWN, D], F32)

    def post_residual_2(self, b0, xt, G_bc, xout, tmp):
        yap = self.psum[:, b0 * 512:(b0 + 2) * 512]
        P = self.P
        keys = ["ps%d" % b0, "ps%d" % (b0 + 1)]
        ss = self.small()
        P.memset("pool", ss[:, 0:1], 0.0, [ss])
        P.act(self.junk[:], yap, AF.Square, keys, [self.junk, ss], accum_out=ss[:, 0:1])
        r = self.rstd_from_ss(ss[:, 0:1], 1, 1.0 / D)
        P.stt(tmp[:], yap, r[:, 0:1], G_bc[:], ALU.mult, ALU.mult, keys + [r, G_bc], [tmp])
        P.tt("pool", xout[:], tmp[:], xt[:], ALU.add, [tmp, xt], [xout])

    def phase_mlp(self, layer, src, dst, ntiles, tag):
        P = self.P
        m = 2 * layer + 1
        P.push()
        Wup = P.sb("Wup", [128, 8, 4096], BF16)
        Wdn = P.sb("Wdn", [128, 32, D], BF16)
        for q4 in range(4):
            P.dma("pool", Wup[:, :, q4 * 1024:(q4 + 1) * 1024], self.w_up[layer, :, :, q4 * 1024:(q4 + 1) * 1024], "ld_Wup",
                  reads=[], writes=[Wup])
            P.dma("pool", Wdn[:, q4 * 8:(q4 + 1) * 8, :], self.w_down[layer, :, q4 * 8:(q4 + 1) * 8, :], "ld_Wdn",
                  reads=[], writes=[Wdn])
        A_bc = self.load_bc(m, 0, "A_bcM")
        B_bc = self.load_bc(m, 1, "B_bcM")
        G_bc = self.load_bc(m, 2, "G_bcM")
        xrot = P.rot("xtM", [128, D], F32, 4)
        tmp = P.rot("tmpM", [128, D], F32, 2)
        hrot = P.rot("hM", [128, D], BF16, 2)
        hT = P.sb("hTM", [128, 8, 256], BF16)
        uT = P.sb("uTM", [128, 32, 256], BF16)
        rl = P.rot("relu", [128, 512], F32, 2)
        xo = P.rot("xoM", [128, D], F32, 2)
        srckey = src.name
        for gidx in range(ntiles // 2):
            xts = []
            for i in range(2):
                ti = gidx * 2 + i
                xt = xrot()
                P.dma("sp", xt[:], src[ti * 128:(ti + 1) * 128, :], "ld_xM%d" % (ti % 4), reads=[srckey], writes=[xt])
                xts.append(xt)
                h = hrot()
                self.prenorm(xt, A_bc, B_bc, h, tmp())
                pb, key = self.transposes(h, D, 4 + i, None, None)
                P.copy("act", hT[:, :, i * 128:(i + 1) * 128], pb.rearrange("p (k t) -> p k t", k=8), [key], [hT])
            for hp in range(16):
                bno = hp % 2
                bkey = "ps%d" % bno
                for u in range(2):
                    hc = 2 * hp + u
                    for kc in range(8):
                        P.mm(self.psum[:, bno * 512 + u * 256: bno * 512 + (u + 1) * 256], Wup[:, kc, hc * 128:(hc + 1) * 128],
                             hT[:, kc, :], kc == 0, kc == 7, [Wup, hT], [bkey])
                r = rl()
                P.act(r[:], self.bank(bno), AF.Relu, [bkey], [r])
                P.tt("dve", uT[:, 2 * hp:2 * hp + 2, :].rearrange("p a t -> p (a t)"), r[:], r[:], ALU.mult, [r], [uT])
            for i in range(2):
                ti = gidx * 2 + i
                b0 = 4 + 2 * i
                for n2 in range(2):
                    for hc in range(32):
                        P.mm(self.bank(b0 + n2), uT[:, hc, i * 128:(i + 1) * 128], Wdn[:, hc, n2 * 512:(n2 + 1) * 512],
                             hc == 0, hc == 31, [uT, Wdn], ["ps%d" % (b0 + n2)])
                xout = xo()
                self.post_residual_2(b0, xts[i], G_bc, xout, tmp())
                P.dma("pool", dst[ti * 128:(ti + 1) * 128, :], xout[:], "st_%s%d" % (tag, ti % 2), reads=[xout], writes=[dst.name])
        P.pop()

    def phase_l1(self):
        P = self.P
        P.push()
        Win = P.sb("Win1", [128, 8, 3072], BF16)
        for q3 in range(3):
            P.dma("pool", Win[:, :, q3 * 1024:(q3 + 1) * 1024], self.w_in1[:, :, q3 * 1024:(q3 + 1) * 1024], "ld_Win1",
                  reads=[], writes=[Win])
        Wout = P.sb("Wout1", [128, 8, D], BF16)
        P.dma("pool", Wout[:], self.w_out1, "ld_Wout1")
        biasI = P.sb("biasI", [128, 16, 5, 128], BF16)
        for q4 in range(4):
            P.dma("pool", biasI[:, q4 * 4:(q4 + 1) * 4], self.biasI[:, q4 * 4:(q4 + 1) * 4], "ld_biasI", reads=[], writes=[biasI])
        biasE = P.sb("biasE", [128, 16, 6, 128], BF16)
        A_bc = self.load_bc(2, 0, "A_bc1")
        B_bc = self.load_bc(2, 1, "B_bc1")
        G_bc = self.load_bc(2, 2, "G_bc1")
        xrot = P.rot("xt1", [128, D], F32, 2)
        tmp = P.rot("tmp1", [128, D], F32, 1)
        hrot = P.rot("h1", [128, D], BF16, 1)
        hTrot = P.rot("hT1", [128, 8, 128], BF16, 2)
        qkbf = P.rot("qkbf1", [128, 1024], BF16, 1)
        NS = 7
        QT = P.rot("QT1", [128, 8, 128], BF16, NS)
        KT = P.rot("KT1", [128, 8, 128], BF16, NS)
        VC = P.rot("VC1", [128, 16, VS], BF16, NS)
        for v in VC.tensors:
            P.memset("pool", v[:, :, 64:65], 1.0, [v])
        PT = P.rot("PT1", [128, 768], BF16, 3)
        orot = P.rot("otok1", [128, D], BF16, 2)
        oTrot = P.rot("oT1", [128, 8, 128], BF16, 2)
        xo = P.rot("xo1", [128, D], F32, 1)
        chunks = {}
        npass = [0]

        def produce(j):
            xt = xrot()
            P.dma("sp", xt[:], self.x1[j * 128:(j + 1) * 128, :], "ld_x1%d" % (j % 3), reads=["x1"], writes=[xt])
            h = hrot()
            self.prenorm(xt, A_bc, B_bc, h, tmp())
            hT = hTrot()
            pb, key = self.transposes(h, D, 0, None, None)
            P.copy("dve", hT[:].rearrange("p k t -> p (k t)"), pb, [key], [hT])
            Tq, Tk, V = QT(), KT(), VC()
            for part, T in ((0, Tq), (1, Tk)):
                for n2 in range(2):
                    for kc in range(8):
                        c0 = part * 1024 + n2 * 512
                        P.mm(self.bank(1 + n2), hT[:, kc, :], Win[:, kc, c0:c0 + 512], kc == 0, kc == 7, [hT, Win], ["ps%d" % (1 + n2)])
                qk = qkbf()
                P.copy("act", qk[:], self.psum[:, 512:1536], ["ps1", "ps2"], [qk])
                pb2, key2 = self.transposes(qk, 1024, 0, None, None)
                P.copy("dve", T[:].rearrange("p k t -> p (k t)"), pb2, [key2], [T])
            for n2 in range(2):
                for kc in range(8):
                    c0 = 2048 + n2 * 512
                    P.mm(self.bank(1 + n2), hT[:, kc, :], Win[:, kc, c0:c0 + 512], kc == 0, kc == 7, [hT, Win], ["ps%d" % (1 + n2)])
            P.copy("act", V[:, :, 0:64], self.psum[:, 512:1536].rearrange("p (h d) -> p h d", d=64), ["ps1", "ps2"], [V])
            chunks[j] = (Tq, Tk, V)

        def attend(jq):
            if jq in (2, 3):
                rels = [-2, -1, 0, 1, 2, 3]
                et = jq - 2
            elif jq in (32, 33):
                rels = [-3, -2, -1, 0, 1, 2]
                et = jq - 30
            else:
                rels = [-2, -1, 0, 1, 2]
                et = None
            if et is not None:
                for q4 in range(4):
                    P.dma("pool", biasE[:, q4 * 4:(q4 + 1) * 4], self.biasE[et, :, q4 * 4:(q4 + 1) * 4], "ld_biasE", reads=[], writes=[biasE])
                bias = biasE
            else:
                bias = biasI
            nr = len(rels)
            xt = xrot()
            P.dma("sp", xt[:], self.x1[jq * 128:(jq + 1) * 128, :], "ld_x1r%d" % (jq % 3), reads=["x1"], writes=[xt])
            Tq = chunks[jq][0]
            o = orot()
            pts = {}

            def emitS(hd):
                hp, hf = hd // 2, hd % 2
                slot = hd % 2
                skey = "psS1_%d" % slot
                sbase = (3 + 2 * slot) * 512
                for ri, rel in enumerate(rels):
                    Tk = chunks[jq + rel][1]
                    so = self.psum[:, sbase + ri * 128: sbase + (ri + 1) * 128]
                    P.mm(so, Tk[hf * 64:(hf + 1) * 64, hp, :], Tq[hf * 64:(hf + 1) * 64, hp, :], True, False, [Tk, Tq], [skey])
                    P.mm(so, self.ident[:], bias[:, hd, ri, :], False, True, [self.ident, bias], [skey])

            def emitExp(hd):
                slot = hd % 2
                sbase = (3 + 2 * slot) * 512
                pt = PT()
                pts[hd] = pt
                P.act(pt[:, 0:nr * 128], self.psum[:, sbase:sbase + nr * 128], AF.Exp, ["psS1_%d" % slot], [pt], scale=0.125)

            def emitPV(hd):
                pt = pts.pop(hd)
                okey = "psO1"
                obase = 7 * 512
                for ri, rel in enumerate(rels):
                    Vk = chunks[jq + rel][2]
                    P.mm(self.psum[:, obase:obase + 65], pt[:, ri * 128:(ri + 1) * 128], Vk[:, hd, 0:65], ri == 0, ri == nr - 1,
                         [pt, Vk], [okey])
                rl = self.small()
                P.op("dve", (lambda rl=rl, ob=obase: (lambda e: e.reciprocal(rl[:, 0:1], self.psum[:, ob + 64:ob + 65])))(), [okey], [rl])
                P.ts(o[:, hd * 64:(hd + 1) * 64], self.psum[:, obase:obase + 64], rl[:, 0:1], None, ALU.mult, None, [okey, rl], [o])

            emitS(0)
            for hd in range(16):
                if hd + 1 < 16:
                    emitS(hd + 1)
                emitExp(hd)
                emitPV(hd)
            oT = oTrot()
            pb, key = self.transposes(o, D, 0, None, None)
            P.copy("dve", oT[:].rearrange("p k t -> p (k t)"), pb, [key], [oT])
            for n2 in range(2):
                for kc in range(8):
                    P.mm(self.bank(1 + n2), oT[:, kc, :], Wout[:, kc, n2 * 512:(n2 + 1) * 512], kc == 0, kc == 7,
                         [oT, Wout], ["ps%d" % (1 + n2)])
            xout = xo()
            self.post_residual_2(1, xt, G_bc, xout, tmp())
            P.dma("pool", self.x2[(jq - 2) * 128:(jq - 1) * 128, :], xout[:], "st_x2%d" % (jq % 2), reads=[xout], writes=["x2"])

        nxt_q = 2
        for j in range(OWNT):
            produce(j)
            while nxt_q <= 33:
                need = min(OWNT - 1, nxt_q + (3 if nxt_q in (2, 3) else 2))
                if need > j:
                    break
                attend(nxt_q)
                nxt_q += 1
        P.pop()
        self.dbg_dump_dram("x2", self.x2, [4096, D], F32)

    def build(self):
        part = self.part
        steps = []
        if part in ("L0", "ALL"):
            steps.append(("mods", lambda: self.phase_mods([0, 1] if part == "L0" else [0, 1, 2, 3])))
            steps.append(("prep", self.phase_l0_prep))
            steps.append(("B", self.phase_l0_B))
            steps.append(("attnA", self.phase_l0_attnA))
            steps.append(("mlp0", lambda: self.phase_mlp(0, self.x_mid, self.x1, OWNT, "x1")))
        if part == "L1":
            steps.append(("mods", lambda: self.phase_mods([2, 3])))
        if part in ("L1", "ALL"):
            steps.append(("l1", self.phase_l1))
            steps.append(("mlp1", lambda: self.phase_mlp(1, self.x2, self.out, 32, "out")))
        for name, fn in steps:
            fn()
            if self.upto == name:
                break
        self.P.finish()


def _rope_cs(pos, dim):
    inv = (10000.0 ** (-np.arange(0, dim, 2, dtype=np.float32) / dim)).astype(np.float32)
    ang = pos.astype(np.float32)[:, None] * inv[None, :]
    return np.cos(ang).astype(np.float32), np.sin(ang).astype(np.float32)


def _rope_tab_axial(pos):
    pos = np.clip(pos, 0, NB - 1)
    cr, sr = _rope_cs(pos // 64, 32)
    cc, sc = _rope_cs(pos % 64, 32)
    return np.concatenate([cr, cr, cc, cc, -sr, sr, -sc, sc], 1).astype(np.float32)


def _rope_tab_1d(pos):
    pos = np.clip(pos, 0, NB - 1)
    c, s = _rope_cs(pos, 64)
    return np.concatenate([c, c, -s, s], 1).astype(np.float32)


def _pk(w, kchunks):
    K, N = w.shape
    return np.ascontiguousarray(w.reshape(kchunks, 128, N).transpose(1, 0, 2))


def _bias_tables(rpb, r0_list, rel_lists):
    outs = []
    ik = np.arange(128)
    for r0, rels in zip(r0_list, rel_lists):
        t = np.full((128, 16, len(rels), 128), NEG, np.float32)
        qrow = r0 + ik // 64
        qc = ik % 64
        rs = np.clip(qrow - 4, 0, 256 - 8)
        cs = np.clip(qc - 8, 0, 64 - 16)
        for ri, rel in enumerate(rels):
            krow = r0 + 2 * rel + ik // 64
            kc = ik % 64
            ok = ((krow[:, None] >= rs[None, :]) & (krow[:, None] < rs[None, :] + 8) &
                  (kc[:, None] >= cs[None, :]) & (kc[:, None] < cs[None, :] + 16) &
                  (krow[:, None] >= 0) & (krow[:, None] < 256) & (qrow[None, :] >= 0) & (qrow[None, :] < 256))
            dr = np.clip(krow[:, None] - qrow[None, :] + 7, 0, 14)
            dc = np.clip(kc[:, None] - qc[None, :] + 15, 0, 30)
            vals = rpb[:, dr, dc]
            t[:, :, ri, :] = np.where(ok[None], vals, np.float32(NEG)).transpose(1, 0, 2)
        outs.append(t)
    return outs


def _host_prep(inp):
    x = np.asarray(inp["x"], np.float32)
    shared = {}
    shared["ada_w"] = np.ascontiguousarray(
        np.asarray(inp["ada_w"], np.float32).reshape(4, 8, 128, 3072).transpose(0, 2, 1, 3))
    shared["ada_b"] = np.ascontiguousarray(np.asarray(inp["ada_b"], np.float32).reshape(4, 3072))
    shared["norm_g"] = np.ascontiguousarray(np.asarray(inp["norm_g"], np.float32).reshape(8, 1024))
    shared["ident"] = np.eye(128, dtype=np.float32)
    shared["w_up"] = np.stack([_pk(np.asarray(inp["mlp_w_up"][l], np.float32), 8) for l in range(2)])
    shared["w_down"] = np.stack([_pk(np.asarray(inp["mlp_w_down"][l], np.float32), 32) for l in range(2)])
    w_in = np.asarray(inp["ab_w_in"][0], np.float32)
    qa = w_in[:, 0:512].reshape(1024, 2, 4, 64).transpose(0, 2, 1, 3).reshape(1024, 512)
    shared["w_a"] = _pk(np.concatenate([qa, w_in[:, 512:768]], 1), 8)
    wb = []
    for g in range(3):
        cols = [w_in[:, 768 + part * 768 + g * 256: 768 + part * 768 + (g + 1) * 256] for part in range(3)]
        wb.append(_pk(np.concatenate(cols, 1), 8))
    shared["w_b"] = np.stack(wb)
    shared["w_out0"] = _pk(np.asarray(inp["ab_w_out"][0], np.float32), 6)
    shared["gains"] = np.stack([np.asarray(inp["a_q_gain"][0], np.float32), np.asarray(inp["a_k_gain"][0], np.float32)])
    shared["ropeA_b"] = _rope_tab_axial(np.arange(NB))
    ik = np.arange(128)
    bm = np.full((128, 3, 128), NEG, np.float32)
    dk = ik[:, None] - ik[None, :]
    bm[:, 0, :] = np.where(dk >= 64, 0.0, NEG)
    bm[:, 1, :] = np.where(np.abs(dk) <= 64, 0.0, NEG)
    bm[:, 2, :] = np.where(dk <= -64, 0.0, NEG)
    shared["bmask"] = bm
    shared["w_in1"] = _pk(np.asarray(inp["c_w_in"][0], np.float32), 8)
    shared["w_out1"] = _pk(np.asarray(inp["c_w_out"][0], np.float32), 8)
    rpb = np.asarray(inp["c_rpb"][0], np.float32)
    shared["biasI"] = _bias_tables(rpb, [100], [[-2, -1, 0, 1, 2]])[0]
    per = []
    for core in range(8):
        b, q = core // 4, core % 4
        s = 4096 * q
        d = {}
        xp = np.zeros((WIN, D), np.float32)
        lo, hi = s - HALO - WPAD, s + 4096 + HALO + WPAD
        a, bnd = max(lo, 0), min(hi, NB)
        xp[a - lo:bnd - lo] = x[b, a:bnd]
        d["xw"] = xp
        d["xb"] = np.ascontiguousarray(x[b])
        d["cmod"] = np.ascontiguousarray(np.asarray(inp["c"], np.float32)[b].reshape(8, 128).T)
        wpos = np.arange(lo, hi)
        vw = ((wpos >= 0) & (wpos < NB)).astype(np.float32)[:, None]
        d["ropeB_w"] = np.concatenate([_rope_tab_1d(wpos), np.repeat(vw, 64, 1)], 1)
        d["ropeA_o"] = _rope_tab_axial(np.arange(s - HALO, s + 4096 + HALO))
        r_own0 = (s - HALO) // 64
        r0s = [r_own0 + 2 * jq for jq in (2, 3, 32, 33)]
        rl = [[-2, -1, 0, 1, 2, 3]] * 2 + [[-3, -2, -1, 0, 1, 2]] * 2
        d["biasE"] = np.stack(_bias_tables(rpb, r0s, rl))
        per.append(d)
    return shared, per


L0_KEYS = ["cmod", "ada_w", "ada_b", "norm_g", "ident", "w_up", "w_down", "xw", "xb", "w_a", "w_b", "w_out0", "gains",
           "ropeA_b", "ropeA_o", "ropeB_w", "bmask"]
L1_KEYS = ["cmod", "ada_w", "ada_b", "norm_g", "ident", "w_up", "w_down", "w_in1", "w_out1", "biasI", "biasE"]

MODE = "FUSED"


def _run(part, shared, per, extra=None, dbg=()):
    nc = bass.Bass("TRN2", target_bir_lowering=False)
    Builder(nc, part, dbg).build()
    keys = {"L0": L0_KEYS, "L1": L1_KEYS, "ALL": sorted(set(L0_KEYS + L1_KEYS))}[part]
    in_maps = []
    for core in range(8):
        m = {}
        for k in keys:
            m[k] = per[core][k] if k in per[core] else shared[k]
        if extra is not None:
            m.update(extra[core])
        in_maps.append(m)
    res = run_bass_kernel_spmd(nc, in_maps, core_ids=list(range(8)))
    return res.results


def kernel(**inputs):
    shared, per = _host_prep(inputs)
    if MODE == "SPLIT":
        r0 = _run("L0", shared, per)
        extra = [{"x1": np.asarray(r0[c]["x1"], np.float32)} for c in range(8)]
        r1 = _run("L1", shared, per, extra)
    else:
        r1 = _run("ALL", shared, per)
    out = np.empty((2, NB, D), np.float32)
    for core in range(8):
        b, q = core // 4, core % 4
        out[b, 4096 * q:4096 * (q + 1)] = np.asarray(r1[core]["out"], np.float32)
    return out
```

```python
import numpy as np
from contextlib import ExitStack
import concourse.bass as bass
import concourse.mybir as mybir
from concourse.bass_utils import run_bass_kernel_spmd

F32 = mybir.dt.float32
BF16 = mybir.dt.bfloat16
AF = mybir.ActivationFunctionType
ALU = mybir.AluOpType
AX = mybir.AxisListType

D = 1024
NB = 16384
NBT = 128
OWN = 4608
OWNT = 36
HALO = 256
WPAD = 2048
WIN = OWN + 2 * WPAD
WINT = WIN // 128
OWN0T = WPAD // 128
EPS = 1e-6
NEG = -30000.0
VS = 72


class Prog:
    ENGS = ("pe", "act", "dve", "pool", "sp")

    def __init__(self, nc):
        self.nc = nc
        self.root = ExitStack()
        self.scopes = []
        self.ops = []
        self.nalloc = 0
        self.freed = []
        self.alias = {}
        self.scope_names = []

    def _stack(self):
        return self.scopes[-1] if self.scopes else self.root

    def push(self):
        self.scopes.append(ExitStack())
        self.scope_names.append([])

    def pop(self):
        self.scopes.pop().close()
        self.freed.extend(self.scope_names.pop())

    def sb(self, name, shape, dtype):
        self.nalloc += 1
        nm = "%s_%d" % (name, self.nalloc)
        t = self._stack().enter_context(self.nc.sbuf_tensor(nm, list(shape), dtype))
        if self.scope_names:
            self.scope_names[-1].append(nm)
        if self.freed:
            self.alias[nm] = len(self.freed)
        return t

    def ps(self, name, shape, dtype):
        return self.root.enter_context(self.nc.psum_tensor(name, list(shape), dtype))

    def dram(self, name, shape, dtype, kind):
        return self.nc.dram_tensor(name, list(shape), dtype, kind=kind).ap()

    def rot(self, name, shape, dtype, n):
        ts = [self.sb("%s%d" % (name, i), shape, dtype) for i in range(n)]
        st = {"i": -1}

        def nxt():
            st["i"] += 1
            return ts[st["i"] % n]
        nxt.tensors = ts
        return nxt

    def op(self, eng, fn, reads, writes, dma=None):
        rd = [r if isinstance(r, str) else r.name for r in reads]
        wr = [w if isinstance(w, str) else w.name for w in writes]
        self.ops.append(dict(eng=eng, fn=fn, reads=rd, writes=wr, dma=dma))

    def dma(self, q, out, in_, sem, reads=None, writes=None):
        self.op(q, lambda e: e.dma_start(out=out, in_=in_),
                [in_] if reads is None else reads, [out] if writes is None else writes, dma=sem)

    def mm(self, out, lhsT, rhs, start, stop, reads, writes, **kw):
        self.op("pe", lambda e: e.matmul(out, lhsT=lhsT, rhs=rhs, start=start, stop=stop, **kw), reads, writes)

    def tr(self, out, in_, ident, reads, writes):
        self.op("pe", lambda e: e.transpose(out, in_, ident), reads, writes)

    def act(self, out, in_, func, reads, writes, **kw):
        self.op("act", lambda e: e.activation(out, in_, func, **kw), reads, writes)

    def copy(self, eng, out, in_, reads, writes):
        if eng == "act":
            self.op("act", lambda e: e.copy(out, in_), reads, writes)
        else:
            self.op(eng, lambda e: e.tensor_copy(out, in_), reads, writes)

    def tt(self, eng, out, in0, in1, op, reads, writes):
        self.op(eng, lambda e: e.tensor_tensor(out=out, in0=in0, in1=in1, op=op), reads, writes)

    def ts(self, out, in0, s1, s2, op0, op1, reads, writes):
        if op1 is None:
            self.op("dve", lambda e: e.tensor_scalar(out=out, in0=in0, scalar1=s1, scalar2=None, op0=op0), reads, writes)
        else:
            self.op("dve", lambda e: e.tensor_scalar(out=out, in0=in0, scalar1=s1, scalar2=s2, op0=op0, op1=op1), reads, writes)

    def stt(self, out, in0, scalar, in1, op0, op1, reads, writes):
        self.op("dve", lambda e: e.scalar_tensor_tensor(out=out, in0=in0, scalar=scalar, in1=in1, op0=op0, op1=op1), reads, writes)

    def memset(self, eng, out, val, writes):
        self.op(eng, lambda e: e.memset(out, val), [], writes)

    def finish(self):
        nc = self.nc
        ops = self.ops
        wstate, rstate = {}, {}
        seen = set()
        deps = [None] * len(ops)
        needed = [False] * len(ops)
        for i, o in enumerate(ops):
            sk = ("dma", o["dma"]) if o["dma"] else ("eng", o["eng"])
            o["sk"] = sk
            d = {}
            isdma = bool(o["dma"])
            ispe = o["eng"] == "pe"
            for t in o["reads"] + o["writes"]:
                if t in self.alias and t not in seen:
                    seen.add(t)
                    mr = rstate.setdefault(t, {})
                    for a in self.freed[:self.alias[t]]:
                        for stt_ in (wstate.get(a), rstate.get(a)):
                            if stt_:
                                for skp, j in stt_.items():
                                    if mr.get(skp, -1) < j:
                                        mr[skp] = j
            for t in o["reads"]:
                for skp, j in wstate.get(t, {}).items():
                    if skp == sk and (isdma or ispe):
                        continue
                    if d.get(skp, -1) < j:
                        d[skp] = j
            for t in o["writes"]:
                for skp, j in wstate.get(t, {}).items():
                    if skp == sk:
                        continue
                    if d.get(skp, -1) < j:
                        d[skp] = j
                for skp, j in rstate.get(t, {}).items():
                    if skp == sk:
                        continue
                    if d.get(skp, -1) < j:
                        d[skp] = j
            deps[i] = d
            for j in d.values():
                needed[j] = True
            for t in o["reads"]:
                rstate.setdefault(t, {})[sk] = i
            for t in o["writes"]:
                wstate.setdefault(t, {})[sk] = i
        cnt = {}
        val = [None] * len(ops)
        issued_at = [None] * len(ops)
        run = {}
        for i, o in enumerate(ops):
            sk = o["sk"]
            if o["dma"]:
                cnt[sk] = cnt.get(sk, 0) + 16
                val[i] = cnt[sk]
                run[sk] = val[i]
            elif needed[i]:
                cnt[sk] = cnt.get(sk, 0) + 1
                val[i] = cnt[sk]
            issued_at[i] = dict(run) if deps[i] and any(k[0] == "dma" for k in deps[i]) else None
        sems = {}
        for sk in sorted(cnt, key=str):
            sems[sk] = self.root.enter_context(nc.semaphore("s_%s_%s" % sk))
        self.n_sems = len(sems)
        self.cnt = dict(cnt)
        per = {e: [] for e in self.ENGS}
        for i, o in enumerate(ops):
            per[o["eng"]].append(i)

        def emit(engname, e):
            waited = {}
            for i in per[engname]:
                o = ops[i]
                for skp in sorted(deps[i], key=str):
                    v = val[deps[i][skp]]
                    if skp[0] == "dma":
                        v = issued_at[i][skp]
                    if waited.get(skp, 0) >= v:
                        continue
                    e.wait_ge(sems[skp], v)
                    waited[skp] = v
                ins = o["fn"](e)
                if o["dma"]:
                    ins.then_inc(sems[o["sk"]], 16)
                elif needed[i]:
                    ins.then_inc(sems[o["sk"]], 1)
            if engname == "sp":
                for sk in sorted(cnt, key=str):
                    if sk[0] == "dma" and waited.get(sk, 0) < cnt[sk]:
                        e.wait_ge(sems[sk], cnt[sk])

        with nc.Block() as block:
            @block.tensor
            def _(e):
                emit("pe", e)

            @block.scalar
            def _(e):
                emit("act", e)

            @block.vector
            def _(e):
                emit("dve", e)

            @block.gpsimd
            def _(e):
                emit("pool", e)

            @block.sync
            def _(e):
                emit("sp", e)
        while self.scopes:
            self.pop()
        self.root.close()


def rows_ap(t, row0, nrows, rstride, ncols, rowlen):
    return bass.AP(t.tensor, row0 * rowlen, [[rstride * rowlen, nrows], [1, ncols]])


class Builder:
    def __init__(self, nc, part, dbg=(), upto=None):
        self.nc = nc
        self.part = part
        self.dbg = set(dbg)
        self.upto = upto
        P = self.P = Prog(nc)
        L0 = part in ("L0", "ALL")
        L1 = part in ("L1", "ALL")
        I = "ExternalInput"
        self.cmod = P.dram("cmod", [128, 8], F32, I)
        self.ada_w = P.dram("ada_w", [4, 128, 8, 3072], F32, I)
        self.ada_b = P.dram("ada_b", [4, 3072], F32, I)
        self.norm_g = P.dram("norm_g", [8, 1024], F32, I)
        self.ident_in = P.dram("ident", [128, 128], F32, I)
        self.w_up = P.dram("w_up", [2, 128, 8, 4096], F32, I)
        self.w_down = P.dram("w_down", [2, 128, 32, 1024], F32, I)
        if L0:
            self.xw = P.dram("xw", [WIN, D], F32, I)
            self.xb = P.dram("xb", [NB, D], F32, I)
            self.w_a = P.dram("w_a", [128, 8, 768], F32, I)
            self.w_b = P.dram("w_b", [3, 128, 8, 768], F32, I)
            self.w_out0 = P.dram("w_out0", [128, 6, 1024], F32, I)
            self.gains = P.dram("gains", [2, 64], F32, I)
            self.ropeA_b = P.dram("ropeA_b", [NB, 128], F32, I)
            self.ropeA_o = P.dram("ropeA_o", [OWN, 128], F32, I)
            self.ropeB_w = P.dram("ropeB_w", [WIN, 192], F32, I)
            self.bmask = P.dram("bmask", [128, 3, 128], F32, I)
        if L1:
            self.w_in1 = P.dram("w_in1", [128, 8, 3072], F32, I)
            self.w_out1 = P.dram("w_out1", [128, 8, 1024], F32, I)
            self.biasI = P.dram("biasI", [128, 16, 5, 128], F32, I)
            self.biasE = P.dram("biasE", [4, 128, 16, 6, 128], F32, I)
        if part == "L0":
            self.x1 = P.dram("x1", [OWN, D], F32, "ExternalOutput")
        elif part == "L1":
            self.x1 = P.dram("x1", [OWN, D], F32, I)
        else:
            self.x1 = P.dram("x1", [OWN, D], F32, "Internal")
        if L1:
            self.out = P.dram("out", [4096, D], F32, "ExternalOutput")
        self.modrows = P.dram("modrows", [12, D], F32, "Internal")
        if L0:
            self.h_w = P.dram("h_w", [WIN, D], BF16, "Internal")
            self.OB = P.dram("OB", [OWN, 3 * 260], F32, "Internal")
            self.x_mid = P.dram("x_mid", [OWN, D], F32, "Internal")
        if L1:
            self.x2 = P.dram("x2", [4096, D], F32, "Internal")
        self.dbg_out = {}
        self.psum = P.ps("psum", [128, 4096], F32)
        self.psum_bf = self.psum[:].bitcast(BF16)
        self.ident = P.sb("ident", [128, 128], BF16)
        P.dma("pool", self.ident[:], self.ident_in, "c_ident")
        self.m05 = P.sb("m05", [128, 16], F32)
        P.memset("pool", self.m05[:], -0.5, [self.m05])
        self.junk = P.sb("junk", [128, 1024], BF16)
        self.small = P.rot("small", [128, 16], F32, 12)

    def bank(self, b0, ncols=512, p0=0, p1=128):
        return self.psum[p0:p1, b0 * 512: b0 * 512 + ncols]

    def bank_bf(self, b0, ncols=1024):
        return self.psum_bf[:, b0 * 1024: b0 * 1024 + ncols]

    def dbg_dump_dram(self, name, src_ap, shape, dtype):
        if name in self.dbg:
            o = self.P.dram("dbg_" + name, shape, dtype, "ExternalOutput")
            self.P.dma("sp", o, src_ap, "dbg", reads=[src_ap.name], writes=["dbg_" + name])

    def dbg_dump_sb(self, name, t, shape, dtype):
        if name in self.dbg:
            o = self.P.dram("dbg_" + name, shape, dtype, "ExternalOutput")
            self.P.dma("sp", o, t, "dbg", reads=[t.name], writes=["dbg_" + name])

    def load_bc(self, m, which, name):
        t = self.P.sb(name, [128, D], F32)
        r = 3 * m + which
        self.P.dma("sp", t[:], self.modrows[r:r + 1, :].partition_broadcast(128), "ld_bc",
                   reads=["modrows"], writes=[t])
        return t

    def rstd_from_ss(self, ss, n, inv):
        P = self.P
        v = self.small()
        P.ts(v[:, 0:n], ss, inv, EPS, ALU.mult, ALU.add, [ss], [v])
        r = self.small()
        P.tt("pool", r[:, 0:n], v[:, 0:n], self.m05[:, 0:n], ALU.pow, [v, self.m05], [r])
        return r

    def prenorm(self, xt, A_bc, B_bc, h_out, tmp):
        P = self.P
        ss = self.small()
        P.memset("pool", ss[:, 0:1], 0.0, [ss])
        P.act(self.junk[:], xt[:], AF.Square, [xt], [self.junk, ss], accum_out=ss[:, 0:1])
        r = self.rstd_from_ss(ss[:, 0:1], 1, 1.0 / D)
        P.stt(tmp[:], xt[:], r[:, 0:1], A_bc[:], ALU.mult, ALU.mult, [xt, r, A_bc], [tmp])
        P.tt("pool", h_out[:], tmp[:], B_bc[:], ALU.add, [tmp, B_bc], [h_out])

    def transposes(self, src, ncol, bankno, dst_ap, dst_key, eng="act", src_key=None):
        P = self.P
        key = "ps%d" % bankno
        pb = self.bank_bf(bankno, ncol)
        sk = src_key if src_key is not None else src
        for c in range(ncol // 128):
            P.tr(pb[:, c * 128:(c + 1) * 128], src[:, c * 128:(c + 1) * 128], self.ident[:],
                 [sk, self.ident], [key])
        return pb, key

    def post_residual(self, ykey, yap, xt, G_bc, xout, tmp):
        P = self.P
        ss = self.small()
        P.memset("pool", ss[:, 0:1], 0.0, [ss])
        P.act(self.junk[:], yap, AF.Square, [ykey], [self.junk, ss], accum_out=ss[:, 0:1])
        r = self.rstd_from_ss(ss[:, 0:1], 1, 1.0 / D)
        P.stt(tmp[:], yap, r[:, 0:1], G_bc[:], ALU.mult, ALU.mult, [ykey, r, G_bc], [tmp])
        P.tt("pool", xout[:], tmp[:], xt[:], ALU.add, [tmp, xt], [xout])

    def qk_rope(self, src_ap, src_key, H, gain, rope_t, out_bf, axial, st):
        self.qk_rope_a(src_ap, src_key, H, st["qs"])
        self.qk_rope_b(st["qs"], H, gain, rope_t, out_bf, axial, st)

    def qk_rope_a(self, src_ap, src_key, H, qs):
        self.P.copy("act", qs[:, 0:H * 64], src_ap, [src_key], [qs])

    def qk_rope_b(self, qs, H, gain, rope_t, out_bf, axial, st):
        P = self.P
        W = H * 64
        t1, t2 = st["t1"], st["t2"]
        v3 = lambda t: t[:, 0:W].rearrange("p (h d) -> p h d", d=64)
        if gain is not None:
            P.tt("pool", t1[:, 0:W], qs[:, 0:W], qs[:, 0:W], ALU.mult, [qs], [t1])
            ssq = self.small()
            self.P.op("dve", lambda e: e.tensor_reduce(out=ssq[:, 0:H], in_=v3(t1), axis=AX.X, op=ALU.add), [t1], [ssq])
            r = self.rstd_from_ss(ssq[:, 0:H], H, 1.0 / 64)
            P.tt("dve", v3(t2), v3(qs), r[:, 0:H].unsqueeze(2).to_broadcast([128, H, 64]), ALU.mult, [qs, r], [t2])
            P.tt("pool", v3(qs), v3(t2), gain[:, 0:64].unsqueeze(1).to_broadcast([128, H, 64]), ALU.mult, [t2, gain], [qs])
        cosb = rope_t[:, 0:64].unsqueeze(1).to_broadcast([128, H, 64])
        P.tt("dve", v3(t1), v3(qs), cosb, ALU.mult, [qs, rope_t], [t1])
        if axial:
            hv = lambda t, off: bass.AP(t[:].tensor, off, [[t[:].ap[0][0], 128], [64, H], [32, 2], [1, 16]])
            sv = lambda off: bass.AP(rope_t[:].tensor, 64 + off, [[rope_t[:].ap[0][0], 128], [0, H], [32, 2], [1, 16]])
            hw = 16
        else:
            hv = lambda t, off: bass.AP(t[:].tensor, off, [[t[:].ap[0][0], 128], [64, H], [1, 32]])
            sv = lambda off: bass.AP(rope_t[:].tensor, 64 + off, [[rope_t[:].ap[0][0], 128], [0, H], [1, 32]])
            hw = 32
        P.tt("pool", hv(t2, 0), hv(qs, hw), sv(0), ALU.mult, [qs, rope_t], [t2])
        P.tt("pool", hv(t2, hw), hv(qs, 0), sv(hw), ALU.mult, [qs, rope_t], [t2])
        P.tt("dve", out_bf, t1[:, 0:W], t2[:, 0:W], ALU.add, [t1, t2], [out_bf.name if hasattr(out_bf, "name") else out_bf])

    def phase_mods(self, ms):
        P = self.P
        P.push()
        cT = P.sb("cT", [128, 8], F32)
        condT = P.sb("condT", [128, 8], F32)
        P.dma("sp", cT[:], self.cmod, "ld_c")
        P.act(condT[:], cT[:], AF.Silu, [cT], [condT])
        brow = P.sb("brow", [1, 3072], F32)
        grow = P.sb("grow", [1, 2, D], F32)
        mrow = P.sb("mrow", [1, 3072], F32)
        orow = P.sb("orow", [1, 3, D], F32)
        wch = P.rot("wch", [128, 8, 512], F32, 2)
        for m in ms:
            P.dma("sp", brow[:], self.ada_b[m:m + 1, :], "ld_b")
            P.dma("sp", grow[:], self.norm_g[2 * m:2 * m + 2, :].rearrange("(o r) d -> o r d", o=1), "ld_g")
            for n6 in range(6):
                w = wch()
                P.dma("sp", w[:], self.ada_w[m, :, :, n6 * 512:(n6 + 1) * 512], "ld_w%d" % (n6 % 2))
                for kc in range(8):
                    P.mm(self.bank(0, 512, 0, 1), condT[:, kc:kc + 1], w[:, kc, :], kc == 0, kc == 7,
                         [condT, w], ["ps0"])
                P.tt("dve", mrow[:, n6 * 512:(n6 + 1) * 512], self.bank(0, 512, 0, 1), brow[:, n6 * 512:(n6 + 1) * 512],
                     ALU.add, ["ps0", brow], [mrow])
            P.stt(orow[:, 0, :], mrow[:, D:2 * D], 1.0, grow[:, 0, :], ALU.add, ALU.mult, [mrow, grow], [orow])
            P.copy("dve", orow[:, 1, :], mrow[:, 0:D], [mrow], [orow])
            P.tt("dve", orow[:, 2, :], mrow[:, 2 * D:3 * D], grow[:, 1, :], ALU.mult, [mrow, grow], [orow])
            P.dma("sp", self.modrows[3 * m:3 * m + 3, :].rearrange("(o r) d -> o r d", o=1), orow[:], "st_mod",
                  reads=[orow], writes=["modrows"])
        P.pop()
        self.dbg_dump_dram("modrows", self.modrows, [12, D], F32)

    def phase_l0_prep(self):
        P = self.P
        P.push()
        self.KAT = P.sb("KAT", [128, NB], BF16)
        self.VA = P.sb("VA", [128, NBT, 2, VS], BF16)
        self.QAT = P.sb("QAT", [128, 4, OWN], BF16)
        P.memset("pool", self.VA[:, :, :, 64:65], 1.0, [self.VA])
        P.push()
        WA = P.sb("WA", [128, 8, 768], BF16)
        P.dma("pool", WA[:], self.w_a, "ld_WA")
        gains = P.sb("gains", [128, 2, 64], F32)
        P.dma("sp", gains[:], bass.AP(self.gains.tensor, 0, [[0, 128], [64, 2], [1, 64]]), "ld_gain", reads=[], writes=[gains])
        A_bc = self.load_bc(0, 0, "A_bc")
        B_bc = self.load_bc(0, 1, "B_bc")
        xrot = P.rot("xt", [128, D], F32, 3)
        tmp = P.rot("tmp", [128, D], F32, 2)
        hrot = P.rot("h", [128, D], BF16, 3)
        hTrot = P.rot("hT", [128, 8, 128], BF16, 2)
        rrot = P.rot("ropeT", [128, 128], F32, 3)
        st = dict(qs=P.sb("qs", [128, 512], F32), t1=P.sb("t1", [128, 512], F32), t2=P.sb("t2", [128, 512], F32))
        qbf = P.rot("qbf", [128, 512], BF16, 2)
        nld = [0]

        def load_x(src, row0):
            xt = xrot()
            k = nld[0] % 3
            nld[0] += 1
            P.dma("sp", xt[:], src[row0:row0 + 128, :], "ld_x%d" % k)
            return xt, k

        qsrot = P.rot("qsr", [128, 512], F32, 2)

        def skew(stages):
            prevB = None
            for A, Bst in stages:
                A()
                if prevB is not None:
                    prevB()
                prevB = Bst
            if prevB is not None:
                prevB()

        stages = []
        for ti in range(WINT):
            t = ti - OWN0T
            own = 0 <= t < OWNT
            box = {}

            def A(ti=ti, t=t, own=own, box=box):
                xt, k = load_x(self.xw, ti * 128)
                h = hrot()
                self.prenorm(xt, A_bc, B_bc, h, tmp())
                P.dma("pool", self.h_w[ti * 128:(ti + 1) * 128, :], h[:], "st_h%d" % k, reads=[h], writes=["h_w"])
                if own:
                    rt = rrot()
                    P.dma("sp", rt[:], self.ropeA_o[t * 128:(t + 1) * 128, :], "ld_r%d" % (t % 3))
                    hT = hTrot()
                    pb, key = self.transposes(h, D, 0, None, None)
                    P.copy("act", hT[:].rearrange("p k t -> p (k t)"), pb, [key], [hT])
                    for kc in range(8):
                        P.mm(self.bank(1), hT[:, kc, :], WA[:, kc, 0:512], kc == 0, kc == 7, [hT, WA], ["ps1"])
                    qs = qsrot()
                    self.qk_rope_a(self.bank(1), "ps1", 8, qs)
                    box["rt"], box["qs"] = rt, qs

            def Bst(t=t, own=own, box=box):
                if not own:
                    return
                qb = qbf()
                self.qk_rope_b(box["qs"], 8, gains[:, 0, :], box["rt"], qb[:], True, st)
                pb2, key2 = self.transposes(qb, 512, 2, None, None)
                P.copy("act", self.QAT[:, :, t * 128:(t + 1) * 128], pb2.rearrange("p (g t) -> p g t", g=4), [key2], [self.QAT])
            stages.append((A, Bst))
        skew(stages)
        stages = []
        for c in range(NBT):
            box = {}

            def A(c=c, box=box):
                xt, k = load_x(self.xb, c * 128)
                h = hrot()
                self.prenorm(xt, A_bc, B_bc, h, tmp())
                rt = rrot()
                P.dma("sp", rt[:], self.ropeA_b[c * 128:(c + 1) * 128, :], "ld_rb%d" % (c % 3))
                hT = hTrot()
                pb, key = self.transposes(h, D, 0, None, None)
                P.copy("act", hT[:].rearrange("p k t -> p (k t)"), pb, [key], [hT])
                for kc in range(8):
                    P.mm(self.bank(1, 256), hT[:, kc, :], WA[:, kc, 512:768], kc == 0, kc == 7, [hT, WA], ["ps1"])
                qs = qsrot()
                self.qk_rope_a(self.bank(1, 128), "ps1", 2, qs)
                P.copy("act", self.VA[:, c, :, 0:64], self.psum[:, 512 + 128:512 + 256].rearrange("p (h d) -> p h d", d=64),
                       ["ps1"], [self.VA])
                box["rt"], box["qs"] = rt, qs

            def Bst(c=c, box=box):
                kb = qbf()
                self.qk_rope_b(box["qs"], 2, gains[:, 1, :], box["rt"], kb[:, 0:128], True, st)
                pb2, key2 = self.transposes(kb, 128, 2, None, None)
                P.copy("dve", self.KAT[:, c * 128:(c + 1) * 128], pb2, [key2], [self.KAT])
            stages.append((A, Bst))
        skew(stages)
        P.pop()
        self.dbg_dump_dram("h_w", self.h_w, [WIN, D], BF16)
        self.dbg_dump_sb("KAT", self.KAT[:], [128, NB], BF16)
        self.dbg_dump_sb("VA", self.VA[:], [128, NBT, 2, VS], BF16)
        self.dbg_dump_sb("QAT", self.QAT[:], [128, 4, OWN], BF16)

    def phase_l0_B(self):
        P = self.P
        P.push()
        bmask = P.sb("bmask", [128, 3, 128], BF16)
        P.dma("pool", bmask[:], self.bmask, "ld_bmask")
        hrot = P.rot("hB", [128, D], BF16, 3)
        hTrot = P.rot("hTB", [128, 8, 128], BF16, 2)
        rrot = P.rot("ropeB", [128, 192], F32, 3)
        st = dict(t1=P.sb("t1B", [128, 512], F32), t2=P.sb("t2B", [128, 512], F32))
        qsrot = P.rot("qsrB", [128, 512], F32, 2)
        qkbf = P.rot("qkbf", [128, 512], BF16, 2)
        QKT = P.rot("QKT", [128, 4, 128], BF16, 4)
        QZrot = P.rot("QZB", [128, 2, 2, 128], BF16, 4)
        for z in QZrot.tensors:
            P.memset("pool", z[:], 0.0, [z])
        VB = P.rot("VB", [128, 4, VS], BF16, 4)
        PT = P.rot("PTB", [128, 1536], BF16, 2)
        osb = P.rot("osb", [128, 260], F32, 2)
        WB = P.rot("WB", [128, 8, 768], BF16, 2)
        nchunk = [0]
        nblk = [0]
        for g, d in enumerate((1, 4, 16)):
            W = WB()
            P.dma("pool", W[:], self.w_b[g], "ld_WB%d" % (g % 2))
            own_u = OWN // d
            nb = (own_u + 127) // 128
            jmax = (own_u + 64 + 127) // 128 - 1
            for r in range(d):
                chunks = {}
                boxes = {}
                nxt = [0]

                def prodA(j, r=r, W=W, d=d, boxes=boxes):
                    n = nchunk[0]
                    nchunk[0] += 1
                    w0 = WPAD + r + d * 128 * j
                    h = hrot()
                    P.dma("sp", h[:], rows_ap(self.h_w, w0, 128, d, D, D), "ld_hB%d" % (n % 3), reads=["h_w"], writes=[h])
                    rt = rrot()
                    P.dma("sp", rt[:], rows_ap(self.ropeB_w, w0, 128, d, 192, 192), "ld_rB%d" % (n % 3), reads=[], writes=[rt])
                    vt = rt[:, 128:129]
                    hT = hTrot()
                    pb, key = self.transposes(h, D, 0, None, None)
                    P.copy("act", hT[:].rearrange("p k t -> p (k t)"), pb, [key], [hT])
                    for kc in range(8):
                        P.mm(self.bank(1), hT[:, kc, :], W[:, kc, 0:512], kc == 0, kc == 7, [hT, W], ["ps1"])
                    for kc in range(8):
                        P.mm(self.bank(2, 256), hT[:, kc, :], W[:, kc, 512:768], kc == 0, kc == 7, [hT, W], ["ps2"])
                    qs = qsrot()
                    self.qk_rope_a(self.bank(1), "ps1", 8, qs)
                    V = VB()
                    P.act(V[:, :, 0:64], self.bank(2, 256).rearrange("p (h d) -> p h d", d=64), AF.Identity, ["ps2", vt], [V],
                          scale=vt)
                    P.copy("dve", V[:, :, 64:65], vt.unsqueeze(1).to_broadcast([128, 4, 1]), [vt], [V])
                    boxes[j] = (rt, qs, V)

                def prodB(j, boxes=boxes, chunks=chunks):
                    rt, qs, V = boxes.pop(j)
                    qk = qkbf()
                    self.qk_rope_b(qs, 8, None, rt, qk[:], False, st)
                    T = QKT()
                    pb2, key2 = self.transposes(qk, 512, 3, None, None)
                    P.copy("dve", T[:].rearrange("p a t -> p (a t)"), pb2, [key2], [T])
                    QZ = QZrot()
                    P.copy("pool", QZ[0:64, :, 0, :], T[0:64, 0:2, :], [T], [QZ])
                    P.copy("pool", QZ[64:128, :, 1, :], T[64:128, 0:2, :], [T], [QZ])
                    chunks[j] = (T, V, QZ)

                def attend(jb, r=r, d=d, g=g, chunks=chunks, own_u=own_u):
                    rels = [rel for rel in (-1, 0, 1) if (jb + rel) in chunks]
                    nr = len(rels)
                    QZq = chunks[jb][2]
                    bi = nblk[0]
                    nblk[0] += 1
                    for hb in range(4):
                        pr, hf = hb // 2, hb % 2
                        for ri, rel in enumerate(rels):
                            Tk = chunks[jb + rel][0]
                            col = 4 * 512 + (hb * nr + ri) * 128
                            so = self.psum[:, col:col + 128]
                            P.mm(so, Tk[:, 2 + pr, :], QZq[:, pr, hf, :], True, False, [Tk, QZq], ["psS"])
                            P.mm(so, self.ident[:], bmask[:, rel + 1, :], False, True, [self.ident, bmask], ["psS"])
                    pt = PT()
                    ncol = 4 * nr * 128
                    P.act(pt[:, 0:ncol], self.psum[:, 4 * 512:4 * 512 + ncol], AF.Exp, ["psS"], [pt], scale=0.125)
                    for hb in range(4):
                        for ri, rel in enumerate(rels):
                            Vk = chunks[jb + rel][1]
                            blk = (hb * nr + ri) * 128
                            P.mm(self.psum[:, 7 * 512 + hb * 65: 7 * 512 + hb * 65 + 65], pt[:, blk:blk + 128], Vk[:, hb, 0:65],
                                 ri == 0, ri == nr - 1, [pt, Vk], ["psO"])
                    o = osb()
                    P.copy("dve", o[:], self.psum[:, 7 * 512:7 * 512 + 260], ["psO"], [o])
                    nq = min(128, own_u - jb * 128)
                    t0 = r + d * 128 * jb
                    dst = bass.AP(self.OB.tensor, t0 * 780 + g * 260, [[d * 780, nq], [1, 260]])
                    P.dma("pool", dst, o[0:nq, :], "st_OB%d" % (bi % 2), reads=[o], writes=["OB"])

                def attend_ready(jdone):
                    while nxt[0] < nb and min(nxt[0] + 1, jmax) <= jdone:
                        attend(nxt[0])
                        nxt[0] += 1

                js = list(range(-1, jmax + 1))
                for idx, j in enumerate(js):
                    prodA(j)
                    if idx >= 1:
                        prodB(js[idx - 1])
                        attend_ready(js[idx - 1])
                prodB(js[-1])
                attend_ready(js[-1])
        P.pop()
        self.dbg_dump_dram("OB", self.OB, [OWN, 780], F32)

    def phase_l0_attnA(self):
        P = self.P
        P.push()
        Wout = P.sb("Wout0", [128, 6, D], BF16)
        P.dma("pool", Wout[:], self.w_out0, "ld_Wout0")
        G_bc = self.load_bc(0, 2, "G_bc0")
        PT = P.rot("PTA", [128, 1024], BF16, 3)
        obrot = P.rot("obin", [128, 780], F32, 2)
        xrot = P.rot("xtA", [128, D], F32, 2)
        orot = P.rot("otok", [128, 768], BF16, 2)
        oTrot = P.rot("oT", [128, 6, 128], BF16, 2)
        tmp = P.rot("tmpA", [128, D], F32, 2)
        xo = P.rot("xoA", [128, D], F32, 2)
        bsum = P.sb("bsum", [128, 260], F32)
        qzrot = P.rot("qz", [128, 2, 4, 128], BF16, 2)
        for z in qzrot.tensors:
            P.memset("pool", z[:], 0.0, [z])
        for t in range(OWNT):
            q0 = t * 128
            xt = xrot()
            P.dma("sp", xt[:], self.xw[WPAD + q0:WPAD + q0 + 128, :], "ld_xA%d" % (t % 2))
            ob = obrot()
            P.dma("sp", ob[:], self.OB[q0:q0 + 128, :], "ld_ob%d" % (t % 2), reads=["OB"], writes=[ob])
            first = {0: True, 1: True}
            o = orot()
            qz = qzrot()
            P.copy("pool", qz[0:64, 0, :, :], self.QAT[0:64, :, q0:q0 + 128], [self.QAT], [qz])
            P.copy("pool", qz[64:128, 1, :, :], self.QAT[64:128, :, q0:q0 + 128], [self.QAT], [qz])
            steps = [(kv, c2) for kv in range(2) for c2 in range(NBT // 2)]
            pts = {}

            def emitS(k):
                kv, c2 = steps[k]
                slot = k % 2
                skey = "psSA%d" % slot
                for u in range(2):
                    c = 2 * c2 + u
                    so = self.psum[:, slot * 1024 + u * 512: slot * 1024 + (u + 1) * 512]
                    P.mm(so, self.KAT[:, c * 128:(c + 1) * 128], qz[:, kv, :, :], True, True, [self.KAT, qz], [skey])

            def emitExp(k):
                slot = k % 2
                pt = PT()
                pts[k] = pt
                P.act(pt[:], self.psum[:, slot * 1024:(slot + 1) * 1024], AF.Exp, ["psSA%d" % slot], [pt], scale=0.125)

            def emitPV(k):
                kv, c2 = steps[k]
                okey = "psOA%d" % kv
                pt = pts.pop(k)
                for u in range(2):
                    c = 2 * c2 + u
                    for gq in range(4):
                        oo = self.psum[:, (4 + kv) * 512 + gq * 65:(4 + kv) * 512 + gq * 65 + 65]
                        P.mm(oo, pt[:, u * 512 + gq * 128: u * 512 + (gq + 1) * 128], self.VA[:, c, kv, 0:65],
                             first[kv], False, [pt, self.VA], [okey], skip_group_check=True)
                        first[kv] = False

            def normalize(kv):
                okey = "psOA%d" % kv
                ov = self.psum[:, (4 + kv) * 512:(4 + kv) * 512 + 260].rearrange("p (g d) -> p g d", d=65)
                rl = self.small()
                P.op("dve", (lambda rl=rl, ov=ov: (lambda e: e.reciprocal(rl[:, 0:4], ov[:, :, 64])))(), [okey], [rl])
                P.tt("dve", o[:, kv * 256:(kv + 1) * 256].rearrange("p (g d) -> p g d", d=64), ov[:, :, 0:64],
                     rl[:, 0:4].unsqueeze(2).to_broadcast([128, 4, 64]), ALU.mult, [okey, rl], [o])

            emitS(0)
            for k in range(len(steps)):
                if k + 1 < len(steps):
                    emitS(k + 1)
                emitExp(k)
                emitPV(k)
                if k == NBT // 2 - 1:
                    normalize(0)
            normalize(1)
            obv = ob[:].rearrange("p (g c) -> p g c", g=3)
            P.tt("pool", bsum[:], obv[:, 0, :], obv[:, 1, :], ALU.add, [ob], [bsum])
            P.tt("pool", bsum[:], bsum[:], obv[:, 2, :], ALU.add, [ob, bsum], [bsum])
            bv = bsum[:].rearrange("p (g d) -> p g d", d=65)
            rlb = self.small()
            P.op("dve", (lambda rlb=rlb, bv=bv: (lambda e: e.reciprocal(rlb[:, 0:4], bv[:, :, 64])))(), [bsum], [rlb])
            P.tt("dve", o[:, 512:768].rearrange("p (g d) -> p g d", d=64), bv[:, :, 0:64],
                 rlb[:, 0:4].unsqueeze(2).to_broadcast([128, 4, 64]), ALU.mult, [bsum, rlb], [o])
            oT = oTrot()
            pb, key = self.transposes(o, 768, 6, None, None)
            P.copy("dve", oT[:].rearrange("p k t -> p (k t)"), pb, [key], [oT])
            for n2 in range(2):
                for kc in range(6):
                    P.mm(self.bank(6 + n2), oT[:, kc, :], Wout[:, kc, n2 * 512:(n2 + 1) * 512], kc == 0, kc == 5,
                         [oT, Wout], ["ps%d" % (6 + n2)])
            xout = xo()
            self.post_residual_2(6, xt, G_bc, xout, tmp())
            P.dma("pool", self.x_mid[q0:q0 + 128, :], xout[:], "st_xm%d" % (t % 2), reads=[xout], writes=["x_mid"])
        P.pop()
        P.pop()
        self.dbg_dump_dram("x_mid", self.x_mid, [OWN, D], F32)

    def post_residual_2(self, b0, xt, G_bc, xout, tmp):
        yap = self.psum[:, b0 * 512:(b0 + 2) * 512]
        P = self.P
        keys = ["ps%d" % b0, "ps%d" % (b0 + 1)]
        ss = self.small()
        P.memset("pool", ss[:, 0:1], 0.0, [ss])
        P.act(self.junk[:], yap, AF.Square, keys, [self.junk, ss], accum_out=ss[:, 0:1])
        r = self.rstd_from_ss(ss[:, 0:1], 1, 1.0 / D)
        P.stt(tmp[:], yap, r[:, 0:1], G_bc[:], ALU.mult, ALU.mult, keys + [r, G_bc], [tmp])
        P.tt("pool", xout[:], tmp[:], xt[:], ALU.add, [tmp, xt], [xout])

    def phase_mlp(self, layer, src, dst, ntiles, tag):
        P = self.P
        m = 2 * layer + 1
        P.push()
        Wup = P.sb("Wup", [128, 8, 4096], BF16)
        Wdn = P.sb("Wdn", [128, 32, D], BF16)
        for q4 in range(4):
            P.dma("pool", Wup[:, :, q4 * 1024:(q4 + 1) * 1024], self.w_up[layer, :, :, q4 * 1024:(q4 + 1) * 1024], "ld_Wup",
                  reads=[], writes=[Wup])
            P.dma("pool", Wdn[:, q4 * 8:(q4 + 1) * 8, :], self.w_down[layer, :, q4 * 8:(q4 + 1) * 8, :], "ld_Wdn",
                  reads=[], writes=[Wdn])
        A_bc = self.load_bc(m, 0, "A_bcM")
        B_bc = self.load_bc(m, 1, "B_bcM")
        G_bc = self.load_bc(m, 2, "G_bcM")
        xrot = P.rot("xtM", [128, D], F32, 4)
        tmp = P.rot("tmpM", [128, D], F32, 2)
        hrot = P.rot("hM", [128, D], BF16, 2)
        hT = P.sb("hTM", [128, 8, 256], BF16)
        uT = P.sb("uTM", [128, 32, 256], BF16)
        rl = P.rot("relu", [128, 512], F32, 3)
        xo = P.rot("xoM", [128, D], F32, 2)
        srckey = src.name
        for gidx in range(ntiles // 2):
            xts = []
            for i in range(2):
                ti = gidx * 2 + i
                xt = xrot()
                P.dma("sp", xt[:], src[ti * 128:(ti + 1) * 128, :], "ld_xM%d" % (ti % 4), reads=[srckey], writes=[xt])
                xts.append(xt)
                h = hrot()
                self.prenorm(xt, A_bc, B_bc, h, tmp())
                pb, key = self.transposes(h, D, 3, None, None)
                P.copy("act", hT[:, :, i * 128:(i + 1) * 128], pb.rearrange("p (k t) -> p k t", k=8), [key], [hT])
            for hp in range(16):
                bno = hp % 3
                bkey = "ps%d" % bno
                for u in range(2):
                    hc = 2 * hp + u
                    for kc in range(8):
                        P.mm(self.psum[:, bno * 512 + u * 256: bno * 512 + (u + 1) * 256], Wup[:, kc, hc * 128:(hc + 1) * 128],
                             hT[:, kc, :], kc == 0, kc == 7, [Wup, hT], [bkey])
                r = rl()
                P.act(r[:], self.bank(bno), AF.Relu, [bkey], [r])
                P.tt("dve", uT[:, 2 * hp:2 * hp + 2, :].rearrange("p a t -> p (a t)"), r[:], r[:], ALU.mult, [r], [uT])
            for i in range(2):
                ti = gidx * 2 + i
                b0 = 4 + 2 * i
                for n2 in range(2):
                    for hc in range(32):
                        P.mm(self.bank(b0 + n2), uT[:, hc, i * 128:(i + 1) * 128], Wdn[:, hc, n2 * 512:(n2 + 1) * 512],
                             hc == 0, hc == 31, [uT, Wdn], ["ps%d" % (b0 + n2)])
                xout = xo()
                self.post_residual_2(b0, xts[i], G_bc, xout, tmp())
                P.dma("pool", dst[ti * 128:(ti + 1) * 128, :], xout[:], "st_%s%d" % (tag, ti % 2), reads=[xout], writes=[dst.name])
        P.pop()

    def phase_l1(self):
        P = self.P
        P.push()
        Win = P.sb("Win1", [128, 8, 3072], BF16)
        for q3 in range(3):
            P.dma("pool", Win[:, :, q3 * 1024:(q3 + 1) * 1024], self.w_in1[:, :, q3 * 1024:(q3 + 1) * 1024], "ld_Win1",
                  reads=[], writes=[Win])
        Wout = P.sb("Wout1", [128, 8, D], BF16)
        P.dma("pool", Wout[:], self.w_out1, "ld_Wout1")
        biasI = P.sb("biasI", [128, 16, 5, 128], BF16)
        for q4 in range(4):
            P.dma("pool", biasI[:, q4 * 4:(q4 + 1) * 4], self.biasI[:, q4 * 4:(q4 + 1) * 4], "ld_biasI", reads=[], writes=[biasI])
        biasE = P.sb("biasE", [128, 16, 6, 128], BF16)
        A_bc = self.load_bc(2, 0, "A_bc1")
        B_bc = self.load_bc(2, 1, "B_bc1")
        G_bc = self.load_bc(2, 2, "G_bc1")
        xrot = P.rot("xt1", [128, D], F32, 2)
        tmp = P.rot("tmp1", [128, D], F32, 1)
        hrot = P.rot("h1", [128, D], BF16, 1)
        hTrot = P.rot("hT1", [128, 8, 128], BF16, 2)
        qkbf = P.rot("qkbf1", [128, 1024], BF16, 1)
        NS = 7
        QT = P.rot("QT1", [128, 8, 128], BF16, NS)
        KT = P.rot("KT1", [128, 8, 128], BF16, NS)
        VC = P.rot("VC1", [128, 16, VS], BF16, NS)
        for v in VC.tensors:
            P.memset("pool", v[:, :, 64:65], 1.0, [v])
        PT = P.rot("PT1", [128, 768], BF16, 3)
        orot = P.rot("otok1", [128, D], BF16, 2)
        oTrot = P.rot("oT1", [128, 8, 128], BF16, 2)
        xo = P.rot("xo1", [128, D], F32, 1)
        chunks = {}
        npass = [0]

        def produce(j):
            xt = xrot()
            P.dma("sp", xt[:], self.x1[j * 128:(j + 1) * 128, :], "ld_x1%d" % (j % 3), reads=["x1"], writes=[xt])
            h = hrot()
            self.prenorm(xt, A_bc, B_bc, h, tmp())
            hT = hTrot()
            pb, key = self.transposes(h, D, 0, None, None)
            P.copy("dve", hT[:].rearrange("p k t -> p (k t)"), pb, [key], [hT])
            Tq, Tk, V = QT(), KT(), VC()
            for part, T in ((0, Tq), (1, Tk)):
                for n2 in range(2):
                    for kc in range(8):
                        c0 = part * 1024 + n2 * 512
                        P.mm(self.bank(1 + n2), hT[:, kc, :], Win[:, kc, c0:c0 + 512], kc == 0, kc == 7, [hT, Win], ["ps%d" % (1 + n2)])
                qk = qkbf()
                P.copy("act", qk[:], self.psum[:, 512:1536], ["ps1", "ps2"], [qk])
                pb2, key2 = self.transposes(qk, 1024, 0, None, None)
                P.copy("dve", T[:].rearrange("p k t -> p (k t)"), pb2, [key2], [T])
            for n2 in range(2):
                for kc in range(8):
                    c0 = 2048 + n2 * 512
                    P.mm(self.bank(1 + n2), hT[:, kc, :], Win[:, kc, c0:c0 + 512], kc == 0, kc == 7, [hT, Win], ["ps%d" % (1 + n2)])
            P.copy("act", V[:, :, 0:64], self.psum[:, 512:1536].rearrange("p (h d) -> p h d", d=64), ["ps1", "ps2"], [V])
            chunks[j] = (Tq, Tk, V)

        def attend(jq):
            if jq in (2, 3):
                rels = [-2, -1, 0, 1, 2, 3]
                et = jq - 2
            elif jq in (32, 33):
                rels = [-3, -2, -1, 0, 1, 2]
                et = jq - 30
            else:
                rels = [-2, -1, 0, 1, 2]
                et = None
            if et is not None:
                for q4 in range(4):
                    P.dma("pool", biasE[:, q4 * 4:(q4 + 1) * 4], self.biasE[et, :, q4 * 4:(q4 + 1) * 4], "ld_biasE", reads=[], writes=[biasE])
                bias = biasE
            else:
                bias = biasI
            nr = len(rels)
            xt = xrot()
            P.dma("sp", xt[:], self.x1[jq * 128:(jq + 1) * 128, :], "ld_x1r%d" % (jq % 3), reads=["x1"], writes=[xt])
            Tq = chunks[jq][0]
            o = orot()
            pts = {}

            def emitS(hd):
                hp, hf = hd // 2, hd % 2
                slot = hd % 2
                skey = "psS1_%d" % slot
                sbase = (3 + 2 * slot) * 512
                for ri, rel in enumerate(rels):
                    Tk = chunks[jq + rel][1]
                    so = self.psum[:, sbase + ri * 128: sbase + (ri + 1) * 128]
                    P.mm(so, Tk[hf * 64:(hf + 1) * 64, hp, :], Tq[hf * 64:(hf + 1) * 64, hp, :], True, False, [Tk, Tq], [skey])
                    P.mm(so, self.ident[:], bias[:, hd, ri, :], False, True, [self.ident, bias], [skey])

            def emitExp(hd):
                slot = hd % 2
                sbase = (3 + 2 * slot) * 512
                pt = PT()
                pts[hd] = pt
                P.act(pt[:, 0:nr * 128], self.psum[:, sbase:sbase + nr * 128], AF.Exp, ["psS1_%d" % slot], [pt], scale=0.125)

            def emitPV(hd):
                pt = pts.pop(hd)
                okey = "psO1"
                obase = 7 * 512
                for ri, rel in enumerate(rels):
                    Vk = chunks[jq + rel][2]
                    P.mm(self.psum[:, obase:obase + 65], pt[:, ri * 128:(ri + 1) * 128], Vk[:, hd, 0:65], ri == 0, ri == nr - 1,
                         [pt, Vk], [okey])
                rl = self.small()
                P.op("dve", (lambda rl=rl, ob=obase: (lambda e: e.reciprocal(rl[:, 0:1], self.psum[:, ob + 64:ob + 65])))(), [okey], [rl])
                P.ts(o[:, hd * 64:(hd + 1) * 64], self.psum[:, obase:obase + 64], rl[:, 0:1], None, ALU.mult, None, [okey, rl], [o])

            emitS(0)
            for hd in range(16):
                if hd + 1 < 16:
                    emitS(hd + 1)
                emitExp(hd)
                emitPV(hd)
            oT = oTrot()
            pb, key = self.transposes(o, D, 0, None, None)
            P.copy("dve", oT[:].rearrange("p k t -> p (k t)"), pb, [key], [oT])
            for n2 in range(2):
                for kc in range(8):
                    P.mm(self.bank(1 + n2), oT[:, kc, :], Wout[:, kc, n2 * 512:(n2 + 1) * 512], kc == 0, kc == 7,
                         [oT, Wout], ["ps%d" % (1 + n2)])
            xout = xo()
            self.post_residual_2(1, xt, G_bc, xout, tmp())
            P.dma("pool", self.x2[(jq - 2) * 128:(jq - 1) * 128, :], xout[:], "st_x2%d" % (jq % 2), reads=[xout], writes=["x2"])

        nxt_q = 2
        for j in range(OWNT):
            produce(j)
            while nxt_q <= 33:
                need = min(OWNT - 1, nxt_q + (3 if nxt_q in (2, 3) else 2))
                if need > j:
                    break
                attend(nxt_q)
                nxt_q += 1
        P.pop()
        self.dbg_dump_dram("x2", self.x2, [4096, D], F32)

    def build(self):
        part = self.part
        steps = []
        if part in ("L0", "ALL"):
            steps.append(("mods", lambda: self.phase_mods([0, 1] if part == "L0" else [0, 1, 2, 3])))
            steps.append(("prep", self.phase_l0_prep))
            steps.append(("B", self.phase_l0_B))
            steps.append(("attnA", self.phase_l0_attnA))
            steps.append(("mlp0", lambda: self.phase_mlp(0, self.x_mid, self.x1, OWNT, "x1")))
        if part == "L1":
            steps.append(("mods", lambda: self.phase_mods([2, 3])))
        if part in ("L1", "ALL"):
            steps.append(("l1", self.phase_l1))
            steps.append(("mlp1", lambda: self.phase_mlp(1, self.x2, self.out, 32, "out")))
        for name, fn in steps:
            fn()
            if self.upto == name:
                break
        self.P.finish()


def _rope_cs(pos, dim):
    inv = (10000.0 ** (-np.arange(0, dim, 2, dtype=np.float32) / dim)).astype(np.float32)
    ang = pos.astype(np.float32)[:, None] * inv[None, :]
    return np.cos(ang).astype(np.float32), np.sin(ang).astype(np.float32)


def _rope_tab_axial(pos):
    pos = np.clip(pos, 0, NB - 1)
    cr, sr = _rope_cs(pos // 64, 32)
    cc, sc = _rope_cs(pos % 64, 32)
    return np.concatenate([cr, cr, cc, cc, -sr, sr, -sc, sc], 1).astype(np.float32)


def _rope_tab_1d(pos):
    pos = np.clip(pos, 0, NB - 1)
    c, s = _rope_cs(pos, 64)
    return np.concatenate([c, c, -s, s], 1).astype(np.float32)


def _pk(w, kchunks):
    K, N = w.shape
    return np.ascontiguousarray(w.reshape(kchunks, 128, N).transpose(1, 0, 2))


def _bias_tables(rpb, r0_list, rel_lists):
    outs = []
    ik = np.arange(128)
    for r0, rels in zip(r0_list, rel_lists):
        t = np.full((128, 16, len(rels), 128), NEG, np.float32)
        qrow = r0 + ik // 64
        qc = ik % 64
        rs = np.clip(qrow - 4, 0, 256 - 8)
        cs = np.clip(qc - 8, 0, 64 - 16)
        for ri, rel in enumerate(rels):
            krow = r0 + 2 * rel + ik // 64
            kc = ik % 64
            ok = ((krow[:, None] >= rs[None, :]) & (krow[:, None] < rs[None, :] + 8) &
                  (kc[:, None] >= cs[None, :]) & (kc[:, None] < cs[None, :] + 16) &
                  (krow[:, None] >= 0) & (krow[:, None] < 256) & (qrow[None, :] >= 0) & (qrow[None, :] < 256))
            dr = np.clip(krow[:, None] - qrow[None, :] + 7, 0, 14)
            dc = np.clip(kc[:, None] - qc[None, :] + 15, 0, 30)
            vals = rpb[:, dr, dc]
            t[:, :, ri, :] = np.where(ok[None], vals, np.float32(NEG)).transpose(1, 0, 2)
        outs.append(t)
    return outs


def _host_prep(inp):
    x = np.asarray(inp["x"], np.float32)
    shared = {}
    shared["ada_w"] = np.ascontiguousarray(
        np.asarray(inp["ada_w"], np.float32).reshape(4, 8, 128, 3072).transpose(0, 2, 1, 3))
    shared["ada_b"] = np.ascontiguousarray(np.asarray(inp["ada_b"], np.float32).reshape(4, 3072))
    shared["norm_g"] = np.ascontiguousarray(np.asarray(inp["norm_g"], np.float32).reshape(8, 1024))
    shared["ident"] = np.eye(128, dtype=np.float32)
    shared["w_up"] = np.stack([_pk(np.asarray(inp["mlp_w_up"][l], np.float32), 8) for l in range(2)])
    shared["w_down"] = np.stack([_pk(np.asarray(inp["mlp_w_down"][l], np.float32), 32) for l in range(2)])
    w_in = np.asarray(inp["ab_w_in"][0], np.float32)
    qa = w_in[:, 0:512].reshape(1024, 2, 4, 64).transpose(0, 2, 1, 3).reshape(1024, 512)
    shared["w_a"] = _pk(np.concatenate([qa, w_in[:, 512:768]], 1), 8)
    wb = []
    for g in range(3):
        cols = [w_in[:, 768 + part * 768 + g * 256: 768 + part * 768 + (g + 1) * 256] for part in range(3)]
        wb.append(_pk(np.concatenate(cols, 1), 8))
    shared["w_b"] = np.stack(wb)
    shared["w_out0"] = _pk(np.asarray(inp["ab_w_out"][0], np.float32), 6)
    shared["gains"] = np.stack([np.asarray(inp["a_q_gain"][0], np.float32), np.asarray(inp["a_k_gain"][0], np.float32)])
    shared["ropeA_b"] = _rope_tab_axial(np.arange(NB))
    ik = np.arange(128)
    bm = np.full((128, 3, 128), NEG, np.float32)
    dk = ik[:, None] - ik[None, :]
    bm[:, 0, :] = np.where(dk >= 64, 0.0, NEG)
    bm[:, 1, :] = np.where(np.abs(dk) <= 64, 0.0, NEG)
    bm[:, 2, :] = np.where(dk <= -64, 0.0, NEG)
    shared["bmask"] = bm
    shared["w_in1"] = _pk(np.asarray(inp["c_w_in"][0], np.float32), 8)
    shared["w_out1"] = _pk(np.asarray(inp["c_w_out"][0], np.float32), 8)
    rpb = np.asarray(inp["c_rpb"][0], np.float32)
    shared["biasI"] = _bias_tables(rpb, [100], [[-2, -1, 0, 1, 2]])[0]
    per = []
    for core in range(8):
        b, q = core // 4, core % 4
        s = 4096 * q
        d = {}
        xp = np.zeros((WIN, D), np.float32)
        lo, hi = s - HALO - WPAD, s + 4096 + HALO + WPAD
        a, bnd = max(lo, 0), min(hi, NB)
        xp[a - lo:bnd - lo] = x[b, a:bnd]
        d["xw"] = xp
        d["xb"] = np.ascontiguousarray(x[b])
        d["cmod"] = np.ascontiguousarray(np.asarray(inp["c"], np.float32)[b].reshape(8, 128).T)
        wpos = np.arange(lo, hi)
        vw = ((wpos >= 0) & (wpos < NB)).astype(np.float32)[:, None]
        d["ropeB_w"] = np.concatenate([_rope_tab_1d(wpos), np.repeat(vw, 64, 1)], 1)
        d["ropeA_o"] = _rope_tab_axial(np.arange(s - HALO, s + 4096 + HALO))
        r_own0 = (s - HALO) // 64
        r0s = [r_own0 + 2 * jq for jq in (2, 3, 32, 33)]
        rl = [[-2, -1, 0, 1, 2, 3]] * 2 + [[-3, -2, -1, 0, 1, 2]] * 2
        d["biasE"] = np.stack(_bias_tables(rpb, r0s, rl))
        per.append(d)
    return shared, per


L0_KEYS = ["cmod", "ada_w", "ada_b", "norm_g", "ident", "w_up", "w_down", "xw", "xb", "w_a", "w_b", "w_out0", "gains",
           "ropeA_b", "ropeA_o", "ropeB_w", "bmask"]
L1_KEYS = ["cmod", "ada_w", "ada_b", "norm_g", "ident", "w_up", "w_down", "w_in1", "w_out1", "biasI", "biasE"]

MODE = "FUSED"


def _run(part, shared, per, extra=None, dbg=()):
    nc = bass.Bass("TRN2", target_bir_lowering=False)
    Builder(nc, part, dbg).build()
    keys = {"L0": L0_KEYS, "L1": L1_KEYS, "ALL": sorted(set(L0_KEYS + L1_KEYS))}[part]
    in_maps = []
    for core in range(8):
        m = {}
        for k in keys:
            m[k] = per[core][k] if k in per[core] else shared[k]
        if extra is not None:
            m.update(extra[core])
        in_maps.append(m)
    res = run_bass_kernel_spmd(nc, in_maps, core_ids=list(range(8)))
    return res.results


def kernel(**inputs):
    shared, per = _host_prep(inputs)
    if MODE == "SPLIT":
        r0 = _run("L0", shared, per)
        extra = [{"x1": np.asarray(r0[c]["x1"], np.float32)} for c in range(8)]
        r1 = _run("L1", shared, per, extra)
    else:
        r1 = _run("ALL", shared, per)
    out = np.empty((2, NB, D), np.float32)
    for core in range(8):
        b, q = core // 4, core % 4
        out[b, 4096 * q:4096 * (q + 1)] = np.asarray(r1[core]["out"], np.float32)
    return out
```

```python
import numpy as np
from contextlib import ExitStack
import concourse.bass as bass
import concourse.mybir as mybir
from concourse.bass_utils import run_bass_kernel_spmd

F32 = mybir.dt.float32
BF16 = mybir.dt.bfloat16
AF = mybir.ActivationFunctionType
ALU = mybir.AluOpType
AX = mybir.AxisListType

D = 1024
NB = 16384
NBT = 128
OWN = 4608
OWNT = 36
HALO = 256
WPAD = 2048
WIN = OWN + 2 * WPAD
WINT = WIN // 128
OWN0T = WPAD // 128
EPS = 1e-6
NEG = -30000.0
VS = 72


class Prog:
    ENGS = ("pe", "act", "dve", "pool", "sp")

    def __init__(self, nc):
        self.nc = nc
        self.root = ExitStack()
        self.scopes = []
        self.ops = []
        self.nalloc = 0
        self.freed = []
        self.alias = {}
        self.scope_names = []

    def _stack(self):
        return self.scopes[-1] if self.scopes else self.root

    def push(self):
        self.scopes.append(ExitStack())
        self.scope_names.append([])

    def pop(self):
        self.scopes.pop().close()
        self.freed.extend(self.scope_names.pop())

    def sb(self, name, shape, dtype):
        self.nalloc += 1
        nm = "%s_%d" % (name, self.nalloc)
        t = self._stack().enter_context(self.nc.sbuf_tensor(nm, list(shape), dtype))
        if self.scope_names:
            self.scope_names[-1].append(nm)
        if self.freed:
            self.alias[nm] = len(self.freed)
        return t

    def ps(self, name, shape, dtype):
        return self.root.enter_context(self.nc.psum_tensor(name, list(shape), dtype))

    def dram(self, name, shape, dtype, kind):
        return self.nc.dram_tensor(name, list(shape), dtype, kind=kind).ap()

    def rot(self, name, shape, dtype, n):
        ts = [self.sb("%s%d" % (name, i), shape, dtype) for i in range(n)]
        st = {"i": -1}

        def nxt():
            st["i"] += 1
            return ts[st["i"] % n]
        nxt.tensors = ts
        return nxt

    def op(self, eng, fn, reads, writes, dma=None):
        rd = [r if isinstance(r, str) else r.name for r in reads]
        wr = [w if isinstance(w, str) else w.name for w in writes]
        self.ops.append(dict(eng=eng, fn=fn, reads=rd, writes=wr, dma=dma))

    def dma(self, q, out, in_, sem, reads=None, writes=None):
        self.op(q, lambda e: e.dma_start(out=out, in_=in_),
                [in_] if reads is None else reads, [out] if writes is None else writes, dma=sem)

    def mm(self, out, lhsT, rhs, start, stop, reads, writes, **kw):
        self.op("pe", lambda e: e.matmul(out, lhsT=lhsT, rhs=rhs, start=start, stop=stop, **kw), reads, writes)

    def tr(self, out, in_, ident, reads, writes):
        self.op("pe", lambda e: e.transpose(out, in_, ident), reads, writes)

    def act(self, out, in_, func, reads, writes, **kw):
        self.op("act", lambda e: e.activation(out, in_, func, **kw), reads, writes)

    def copy(self, eng, out, in_, reads, writes):
        if eng == "act":
            self.op("act", lambda e: e.copy(out, in_), reads, writes)
        else:
            self.op(eng, lambda e: e.tensor_copy(out, in_), reads, writes)

    def tt(self, eng, out, in0, in1, op, reads, writes):
        self.op(eng, lambda e: e.tensor_tensor(out=out, in0=in0, in1=in1, op=op), reads, writes)

    def ts(self, out, in0, s1, s2, op0, op1, reads, writes):
        if op1 is None:
            self.op("dve", lambda e: e.tensor_scalar(out=out, in0=in0, scalar1=s1, scalar2=None, op0=op0), reads, writes)
        else:
            self.op("dve", lambda e: e.tensor_scalar(out=out, in0=in0, scalar1=s1, scalar2=s2, op0=op0, op1=op1), reads, writes)

    def stt(self, out, in0, scalar, in1, op0, op1, reads, writes):
        self.op("dve", lambda e: e.scalar_tensor_tensor(out=out, in0=in0, scalar=scalar, in1=in1, op0=op0, op1=op1), reads, writes)

    def memset(self, eng, out, val, writes):
        self.op(eng, lambda e: e.memset(out, val), [], writes)

    def finish(self):
        nc = self.nc
        ops = self.ops
        wstate, rstate = {}, {}
        seen = set()
        deps = [None] * len(ops)
        needed = [False] * len(ops)
        for i, o in enumerate(ops):
            sk = ("dma", o["dma"]) if o["dma"] else ("eng", o["eng"])
            o["sk"] = sk
            d = {}
            isdma = bool(o["dma"])
            ispe = o["eng"] == "pe"
            for t in o["reads"] + o["writes"]:
                if t in self.alias and t not in seen:
                    seen.add(t)
                    mr = rstate.setdefault(t, {})
                    for a in self.freed[:self.alias[t]]:
                        for stt_ in (wstate.get(a), rstate.get(a)):
                            if stt_:
                                for skp, j in stt_.items():
                                    if mr.get(skp, -1) < j:
                                        mr[skp] = j
            for t in o["reads"]:
                for skp, j in wstate.get(t, {}).items():
                    if skp == sk and (isdma or ispe):
                        continue
                    if d.get(skp, -1) < j:
                        d[skp] = j
            for t in o["writes"]:
                for skp, j in wstate.get(t, {}).items():
                    if skp == sk:
                        continue
                    if d.get(skp, -1) < j:
                        d[skp] = j
                for skp, j in rstate.get(t, {}).items():
                    if skp == sk:
                        continue
                    if d.get(skp, -1) < j:
                        d[skp] = j
            deps[i] = d
            for j in d.values():
                needed[j] = True
            for t in o["reads"]:
                rstate.setdefault(t, {})[sk] = i
            for t in o["writes"]:
                wstate.setdefault(t, {})[sk] = i
        cnt = {}
        val = [None] * len(ops)
        issued_at = [None] * len(ops)
        run = {}
        for i, o in enumerate(ops):
            sk = o["sk"]
            if o["dma"]:
                cnt[sk] = cnt.get(sk, 0) + 16
                val[i] = cnt[sk]
                run[sk] = val[i]
            elif needed[i]:
                cnt[sk] = cnt.get(sk, 0) + 1
                val[i] = cnt[sk]
            issued_at[i] = dict(run) if deps[i] and any(k[0] == "dma" for k in deps[i]) else None
        sems = {}
        for sk in sorted(cnt, key=str):
            sems[sk] = self.root.enter_context(nc.semaphore("s_%s_%s" % sk))
        self.n_sems = len(sems)
        self.cnt = dict(cnt)
        per = {e: [] for e in self.ENGS}
        for i, o in enumerate(ops):
            per[o["eng"]].append(i)

        def emit(engname, e):
            waited = {}
            for i in per[engname]:
                o = ops[i]
                for skp in sorted(deps[i], key=str):
                    v = val[deps[i][skp]]
                    if skp[0] == "dma":
                        v = issued_at[i][skp]
                    if waited.get(skp, 0) >= v:
                        continue
                    e.wait_ge(sems[skp], v)
                    waited[skp] = v
                ins = o["fn"](e)
                if o["dma"]:
                    ins.then_inc(sems[o["sk"]], 16)
                elif needed[i]:
                    ins.then_inc(sems[o["sk"]], 1)
            if engname == "sp":
                for sk in sorted(cnt, key=str):
                    if sk[0] == "dma" and waited.get(sk, 0) < cnt[sk]:
                        e.wait_ge(sems[sk], cnt[sk])

        with nc.Block() as block:
            @block.tensor
            def _(e):
                emit("pe", e)

            @block.scalar
            def _(e):
                emit("act", e)

            @block.vector
            def _(e):
                emit("dve", e)

            @block.gpsimd
            def _(e):
                emit("pool", e)

            @block.sync
            def _(e):
                emit("sp", e)
        while self.scopes:
            self.pop()
        self.root.close()


def rows_ap(t, row0, nrows, rstride, ncols, rowlen):
    return bass.AP(t.tensor, row0 * rowlen, [[rstride * rowlen, nrows], [1, ncols]])


class Builder:
    def __init__(self, nc, part, dbg=(), upto=None):
        self.nc = nc
        self.part = part
        self.dbg = set(dbg)
        self.upto = upto
        P = self.P = Prog(nc)
        L0 = part in ("L0", "ALL")
        L1 = part in ("L1", "ALL")
        I = "ExternalInput"
        self.cmod = P.dram("cmod", [128, 8], F32, I)
        self.ada_w = P.dram("ada_w", [4, 128, 8, 3072], F32, I)
        self.ada_b = P.dram("ada_b", [4, 3072], F32, I)
        self.norm_g = P.dram("norm_g", [8, 1024], F32, I)
        self.ident_in = P.dram("ident", [128, 128], F32, I)
        self.w_up = P.dram("w_up", [2, 128, 8, 4096], F32, I)
        self.w_down = P.dram("w_down", [2, 128, 32, 1024], F32, I)
        if L0:
            self.xw = P.dram("xw", [WIN, D], F32, I)
            self.xb = P.dram("xb", [NB, D], F32, I)
            self.w_a = P.dram("w_a", [128, 8, 768], F32, I)
            self.w_b = P.dram("w_b", [3, 128, 8, 768], F32, I)
            self.w_out0 = P.dram("w_out0", [128, 6, 1024], F32, I)
            self.gains = P.dram("gains", [2, 64], F32, I)
            self.ropeA_b = P.dram("ropeA_b", [NB, 128], F32, I)
            self.ropeA_o = P.dram("ropeA_o", [OWN, 128], F32, I)
            self.ropeB_w = P.dram("ropeB_w", [WIN, 192], F32, I)
            self.bmask = P.dram("bmask", [128, 3, 128], F32, I)
        if L1:
            self.w_in1 = P.dram("w_in1", [128, 8, 3072], F32, I)
            self.w_out1 = P.dram("w_out1", [128, 8, 1024], F32, I)
            self.biasI = P.dram("biasI", [128, 16, 5, 128], F32, I)
            self.biasE = P.dram("biasE", [4, 128, 16, 6, 128], F32, I)
        if part == "L0":
            self.x1 = P.dram("x1", [OWN, D], F32, "ExternalOutput")
        elif part == "L1":
            self.x1 = P.dram("x1", [OWN, D], F32, I)
        else:
            self.x1 = P.dram("x1", [OWN, D], F32, "Internal")
        if L1:
            self.out = P.dram("out", [4096, D], F32, "ExternalOutput")
        self.modrows = P.dram("modrows", [12, D], F32, "Internal")
        if L0:
            self.h_w = P.dram("h_w", [WIN, D], BF16, "Internal")
            self.OB = P.dram("OB", [OWN, 3 * 260], F32, "Internal")
            self.x_mid = P.dram("x_mid", [OWN, D], F32, "Internal")
        if L1:
            self.x2 = P.dram("x2", [4096, D], F32, "Internal")
        self.dbg_out = {}
        self.psum = P.ps("psum", [128, 4096], F32)
        self.psum_bf = self.psum[:].bitcast(BF16)
        self.ident = P.sb("ident", [128, 128], BF16)
        P.dma("pool", self.ident[:], self.ident_in, "c_ident")
        self.m05 = P.sb("m05", [128, 16], F32)
        P.memset("pool", self.m05[:], -0.5, [self.m05])
        self.junk = P.sb("junk", [128, 1024], BF16)
        self.small = P.rot("small", [128, 16], F32, 12)

    def bank(self, b0, ncols=512, p0=0, p1=128):
        return self.psum[p0:p1, b0 * 512: b0 * 512 + ncols]

    def bank_bf(self, b0, ncols=1024):
        return self.psum_bf[:, b0 * 1024: b0 * 1024 + ncols]

    def dbg_dump_dram(self, name, src_ap, shape, dtype):
        if name in self.dbg:
            o = self.P.dram("dbg_" + name, shape, dtype, "ExternalOutput")
            self.P.dma("sp", o, src_ap, "dbg", reads=[src_ap.name], writes=["dbg_" + name])

    def dbg_dump_sb(self, name, t, shape, dtype):
        if name in self.dbg:
            o = self.P.dram("dbg_" + name, shape, dtype, "ExternalOutput")
            self.P.dma("sp", o, t, "dbg", reads=[t.name], writes=["dbg_" + name])

    def load_bc(self, m, which, name):
        t = self.P.sb(name, [128, D], F32)
        r = 3 * m + which
        self.P.dma("sp", t[:], self.modrows[r:r + 1, :].partition_broadcast(128), "ld_bc",
                   reads=["modrows"], writes=[t])
        return t

    def rstd_from_ss(self, ss, n, inv):
        P = self.P
        v = self.small()
        P.ts(v[:, 0:n], ss, inv, EPS, ALU.mult, ALU.add, [ss], [v])
        r = self.small()
        P.tt("pool", r[:, 0:n], v[:, 0:n], self.m05[:, 0:n], ALU.pow, [v, self.m05], [r])
        return r

    def prenorm(self, xt, A_bc, B_bc, h_out, tmp):
        P = self.P
        ss = self.small()
        P.memset("pool", ss[:, 0:1], 0.0, [ss])
        P.act(self.junk[:], xt[:], AF.Square, [xt], [self.junk, ss], accum_out=ss[:, 0:1])
        r = self.rstd_from_ss(ss[:, 0:1], 1, 1.0 / D)
        P.stt(tmp[:], xt[:], r[:, 0:1], A_bc[:], ALU.mult, ALU.mult, [xt, r, A_bc], [tmp])
        P.tt("dve", h_out[:], tmp[:], B_bc[:], ALU.add, [tmp, B_bc], [h_out])

    def transposes(self, src, ncol, bankno, dst_ap, dst_key, eng="act", src_key=None):
        P = self.P
        key = "ps%d" % bankno
        pb = self.bank_bf(bankno, ncol)
        sk = src_key if src_key is not None else src
        for c in range(ncol // 128):
            P.tr(pb[:, c * 128:(c + 1) * 128], src[:, c * 128:(c + 1) * 128], self.ident[:],
                 [sk, self.ident], [key])
        return pb, key

    def post_residual(self, ykey, yap, xt, G_bc, xout, tmp):
        P = self.P
        ss = self.small()
        P.memset("pool", ss[:, 0:1], 0.0, [ss])
        P.act(self.junk[:], yap, AF.Square, [ykey], [self.junk, ss], accum_out=ss[:, 0:1])
        r = self.rstd_from_ss(ss[:, 0:1], 1, 1.0 / D)
        P.stt(tmp[:], yap, r[:, 0:1], G_bc[:], ALU.mult, ALU.mult, [ykey, r, G_bc], [tmp])
        P.tt("pool", xout[:], tmp[:], xt[:], ALU.add, [tmp, xt], [xout])

    def qk_rope(self, src_ap, src_key, H, gain, rope_t, out_bf, axial, st):
        self.qk_rope_a(src_ap, src_key, H, st["qs"])
        self.qk_rope_b(st["qs"], H, gain, rope_t, out_bf, axial, st)

    def qk_rope_a(self, src_ap, src_key, H, qs):
        self.P.copy("act", qs[:, 0:H * 64], src_ap, [src_key], [qs])

    def qk_rope_b(self, qs, H, gain, rope_t, out_bf, axial, st):
        P = self.P
        W = H * 64
        t1, t2 = st["t1"], st["t2"]
        v3 = lambda t: t[:, 0:W].rearrange("p (h d) -> p h d", d=64)
        if gain is not None:
            P.tt("pool", t1[:, 0:W], qs[:, 0:W], qs[:, 0:W], ALU.mult, [qs], [t1])
            ssq = self.small()
            self.P.op("dve", lambda e: e.tensor_reduce(out=ssq[:, 0:H], in_=v3(t1), axis=AX.X, op=ALU.add), [t1], [ssq])
            r = self.rstd_from_ss(ssq[:, 0:H], H, 1.0 / 64)
            P.tt("dve", v3(t2), v3(qs), r[:, 0:H].unsqueeze(2).to_broadcast([128, H, 64]), ALU.mult, [qs, r], [t2])
            P.tt("pool", v3(qs), v3(t2), gain[:, 0:64].unsqueeze(1).to_broadcast([128, H, 64]), ALU.mult, [t2, gain], [qs])
        cosb = rope_t[:, 0:64].unsqueeze(1).to_broadcast([128, H, 64])
        P.tt("dve", v3(t1), v3(qs), cosb, ALU.mult, [qs, rope_t], [t1])
        if axial:
            hv = lambda t, off: bass.AP(t[:].tensor, off, [[t[:].ap[0][0], 128], [64, H], [32, 2], [1, 16]])
            sv = lambda off: bass.AP(rope_t[:].tensor, 64 + off, [[rope_t[:].ap[0][0], 128], [0, H], [32, 2], [1, 16]])
            hw = 16
        else:
            hv = lambda t, off: bass.AP(t[:].tensor, off, [[t[:].ap[0][0], 128], [64, H], [1, 32]])
            sv = lambda off: bass.AP(rope_t[:].tensor, 64 + off, [[rope_t[:].ap[0][0], 128], [0, H], [1, 32]])
            hw = 32
        P.tt("pool", hv(t2, 0), hv(qs, hw), sv(0), ALU.mult, [qs, rope_t], [t2])
        P.tt("pool", hv(t2, hw), hv(qs, 0), sv(hw), ALU.mult, [qs, rope_t], [t2])
        P.tt("dve", out_bf, t1[:, 0:W], t2[:, 0:W], ALU.add, [t1, t2], [out_bf.name if hasattr(out_bf, "name") else out_bf])

    def phase_mods(self, ms):
        P = self.P
        P.push()
        cT = P.sb("cT", [128, 8], F32)
        condT = P.sb("condT", [128, 8], F32)
        P.dma("sp", cT[:], self.cmod, "ld_c")
        P.act(condT[:], cT[:], AF.Silu, [cT], [condT])
        brow = P.sb("brow", [1, 3072], F32)
        grow = P.sb("grow", [1, 2, D], F32)
        mrow = P.sb("mrow", [1, 3072], F32)
        orow = P.sb("orow", [1, 3, D], F32)
        wch = P.rot("wch", [128, 8, 512], F32, 2)
        for m in ms:
            P.dma("sp", brow[:], self.ada_b[m:m + 1, :], "ld_b")
            P.dma("sp", grow[:], self.norm_g[2 * m:2 * m + 2, :].rearrange("(o r) d -> o r d", o=1), "ld_g")
            for n6 in range(6):
                w = wch()
                P.dma("sp", w[:], self.ada_w[m, :, :, n6 * 512:(n6 + 1) * 512], "ld_w%d" % (n6 % 2))
                for kc in range(8):
                    P.mm(self.bank(0, 512, 0, 1), condT[:, kc:kc + 1], w[:, kc, :], kc == 0, kc == 7,
                         [condT, w], ["ps0"])
                P.tt("dve", mrow[:, n6 * 512:(n6 + 1) * 512], self.bank(0, 512, 0, 1), brow[:, n6 * 512:(n6 + 1) * 512],
                     ALU.add, ["ps0", brow], [mrow])
            P.stt(orow[:, 0, :], mrow[:, D:2 * D], 1.0, grow[:, 0, :], ALU.add, ALU.mult, [mrow, grow], [orow])
            P.copy("dve", orow[:, 1, :], mrow[:, 0:D], [mrow], [orow])
            P.tt("dve", orow[:, 2, :], mrow[:, 2 * D:3 * D], grow[:, 1, :], ALU.mult, [mrow, grow], [orow])
            P.dma("sp", self.modrows[3 * m:3 * m + 3, :].rearrange("(o r) d -> o r d", o=1), orow[:], "st_mod",
                  reads=[orow], writes=["modrows"])
        P.pop()
        self.dbg_dump_dram("modrows", self.modrows, [12, D], F32)

    def phase_l0_prep(self):
        P = self.P
        P.push()
        self.KAT = P.sb("KAT", [128, NB], BF16)
        self.VA = P.sb("VA", [128, NBT, 2, VS], BF16)
        self.QAT = P.sb("QAT", [128, 4, OWN], BF16)
        P.memset("pool", self.VA[:, :, :, 64:65], 1.0, [self.VA])
        P.push()
        WA = P.sb("WA", [128, 8, 768], BF16)
        P.dma("pool", WA[:], self.w_a, "ld_WA")
        gains = P.sb("gains", [128, 2, 64], F32)
        P.dma("sp", gains[:], bass.AP(self.gains.tensor, 0, [[0, 128], [64, 2], [1, 64]]), "ld_gain", reads=[], writes=[gains])
        A_bc = self.load_bc(0, 0, "A_bc")
        B_bc = self.load_bc(0, 1, "B_bc")
        xrot = P.rot("xt", [128, D], F32, 3)
        tmp = P.rot("tmp", [128, D], F32, 3)
        hrot = P.rot("h", [128, D], BF16, 3)
        hTrot = P.rot("hT", [128, 8, 128], BF16, 2)
        rrot = P.rot("ropeT", [128, 128], F32, 4)
        st = dict(qs=P.sb("qs", [128, 512], F32), t1=P.sb("t1", [128, 512], F32), t2=P.sb("t2", [128, 512], F32))
        qbf = P.rot("qbf", [128, 512], BF16, 2)
        nld = [0]

        def load_x(src, row0):
            xt = xrot()
            k = nld[0] % 3
            nld[0] += 1
            P.dma("sp", xt[:], src[row0:row0 + 128, :], "ld_x%d" % k)
            return xt, k

        qsrot = P.rot("qsr", [128, 512], F32, 3)

        def skew(stages, depth=2):
            pend = []
            for A, Bst in stages:
                A()
                pend.append(Bst)
                if len(pend) > depth:
                    pend.pop(0)()
            for Bst in pend:
                Bst()

        stages = []
        for ti in range(WINT):
            t = ti - OWN0T
            own = 0 <= t < OWNT
            box = {}

            def A(ti=ti, t=t, own=own, box=box):
                xt, k = load_x(self.xw, ti * 128)
                h = hrot()
                self.prenorm(xt, A_bc, B_bc, h, tmp())
                P.dma("pool", self.h_w[ti * 128:(ti + 1) * 128, :], h[:], "st_h%d" % k, reads=[h], writes=["h_w"])
                if own:
                    rt = rrot()
                    P.dma("sp", rt[:], self.ropeA_o[t * 128:(t + 1) * 128, :], "ld_r%d" % (t % 3))
                    hT = hTrot()
                    pb, key = self.transposes(h, D, 0, None, None)
                    P.copy("act", hT[:].rearrange("p k t -> p (k t)"), pb, [key], [hT])
                    for kc in range(8):
                        P.mm(self.bank(1), hT[:, kc, :], WA[:, kc, 0:512], kc == 0, kc == 7, [hT, WA], ["ps1"])
                    qs = qsrot()
                    self.qk_rope_a(self.bank(1), "ps1", 8, qs)
                    box["rt"], box["qs"] = rt, qs

            def Bst(t=t, own=own, box=box):
                if not own:
                    return
                qb = qbf()
                self.qk_rope_b(box["qs"], 8, gains[:, 0, :], box["rt"], qb[:], True, st)
                pb2, key2 = self.transposes(qb, 512, 2, None, None)
                P.copy("act", self.QAT[:, :, t * 128:(t + 1) * 128], pb2.rearrange("p (g t) -> p g t", g=4), [key2], [self.QAT])
            stages.append((A, Bst))
        skew(stages)
        stages = []
        for c in range(NBT):
            box = {}

            def A(c=c, box=box):
                xt, k = load_x(self.xb, c * 128)
                h = hrot()
                self.prenorm(xt, A_bc, B_bc, h, tmp())
                rt = rrot()
                P.dma("sp", rt[:], self.ropeA_b[c * 128:(c + 1) * 128, :], "ld_rb%d" % (c % 3))
                hT = hTrot()
                pb, key = self.transposes(h, D, 0, None, None)
                P.copy("act", hT[:].rearrange("p k t -> p (k t)"), pb, [key], [hT])
                for kc in range(8):
                    P.mm(self.bank(1, 256), hT[:, kc, :], WA[:, kc, 512:768], kc == 0, kc == 7, [hT, WA], ["ps1"])
                qs = qsrot()
                self.qk_rope_a(self.bank(1, 128), "ps1", 2, qs)
                P.copy("act", self.VA[:, c, :, 0:64], self.psum[:, 512 + 128:512 + 256].rearrange("p (h d) -> p h d", d=64),
                       ["ps1"], [self.VA])
                box["rt"], box["qs"] = rt, qs

            def Bst(c=c, box=box):
                kb = qbf()
                self.qk_rope_b(box["qs"], 2, gains[:, 1, :], box["rt"], kb[:, 0:128], True, st)
                pb2, key2 = self.transposes(kb, 128, 2, None, None)
                P.copy("dve", self.KAT[:, c * 128:(c + 1) * 128], pb2, [key2], [self.KAT])
            stages.append((A, Bst))
        skew(stages)
        P.pop()
        self.dbg_dump_dram("h_w", self.h_w, [WIN, D], BF16)
        self.dbg_dump_sb("KAT", self.KAT[:], [128, NB], BF16)
        self.dbg_dump_sb("VA", self.VA[:], [128, NBT, 2, VS], BF16)
        self.dbg_dump_sb("QAT", self.QAT[:], [128, 4, OWN], BF16)

    def phase_l0_B(self):
        P = self.P
        P.push()
        bmask = P.sb("bmask", [128, 3, 128], BF16)
        P.dma("pool", bmask[:], self.bmask, "ld_bmask")
        hrot = P.rot("hB", [128, D], BF16, 3)
        hTrot = P.rot("hTB", [128, 8, 128], BF16, 2)
        rrot = P.rot("ropeB", [128, 192], F32, 3)
        st = dict(t1=P.sb("t1B", [128, 512], F32), t2=P.sb("t2B", [128, 512], F32))
        qsrot = P.rot("qsrB", [128, 512], F32, 2)
        qkbf = P.rot("qkbf", [128, 512], BF16, 2)
        QKT = P.rot("QKT", [128, 4, 128], BF16, 4)
        QZrot = P.rot("QZB", [128, 2, 2, 128], BF16, 4)
        for z in QZrot.tensors:
            P.memset("pool", z[:], 0.0, [z])
        VB = P.rot("VB", [128, 4, VS], BF16, 4)
        PT = P.rot("PTB", [128, 1536], BF16, 2)
        osb = P.rot("osb", [128, 260], F32, 2)
        WB = P.rot("WB", [128, 8, 768], BF16, 2)
        nchunk = [0]
        nblk = [0]
        for g, d in enumerate((1, 4, 16)):
            W = WB()
            P.dma("pool", W[:], self.w_b[g], "ld_WB%d" % (g % 2))
            own_u = OWN // d
            nb = (own_u + 127) // 128
            jmax = (own_u + 64 + 127) // 128 - 1
            for r in range(d):
                chunks = {}
                boxes = {}
                nxt = [0]

                def prodA(j, r=r, W=W, d=d, boxes=boxes):
                    n = nchunk[0]
                    nchunk[0] += 1
                    w0 = WPAD + r + d * 128 * j
                    h = hrot()
                    P.dma("sp", h[:], rows_ap(self.h_w, w0, 128, d, D, D), "ld_hB%d" % (n % 3), reads=["h_w"], writes=[h])
                    rt = rrot()
                    P.dma("sp", rt[:], rows_ap(self.ropeB_w, w0, 128, d, 192, 192), "ld_rB%d" % (n % 3), reads=[], writes=[rt])
                    vt = rt[:, 128:129]
                    hT = hTrot()
                    pb, key = self.transposes(h, D, 0, None, None)
                    P.copy("act", hT[:].rearrange("p k t -> p (k t)"), pb, [key], [hT])
                    for kc in range(8):
                        P.mm(self.bank(1), hT[:, kc, :], W[:, kc, 0:512], kc == 0, kc == 7, [hT, W], ["ps1"])
                    for kc in range(8):
                        P.mm(self.bank(2, 256), hT[:, kc, :], W[:, kc, 512:768], kc == 0, kc == 7, [hT, W], ["ps2"])
                    qs = qsrot()
                    self.qk_rope_a(self.bank(1), "ps1", 8, qs)
                    V = VB()
                    P.act(V[:, :, 0:64], self.bank(2, 256).rearrange("p (h d) -> p h d", d=64), AF.Identity, ["ps2", vt], [V],
                          scale=vt)
                    P.copy("dve", V[:, :, 64:65], vt.unsqueeze(1).to_broadcast([128, 4, 1]), [vt], [V])
                    boxes[j] = (rt, qs, V)

                def prodB(j, boxes=boxes, chunks=chunks):
                    rt, qs, V = boxes.pop(j)
                    qk = qkbf()
                    self.qk_rope_b(qs, 8, None, rt, qk[:], False, st)
                    T = QKT()
                    pb2, key2 = self.transposes(qk, 512, 3, None, None)
                    P.copy("dve", T[:].rearrange("p a t -> p (a t)"), pb2, [key2], [T])
                    QZ = QZrot()
                    P.copy("pool", QZ[0:64, :, 0, :], T[0:64, 0:2, :], [T], [QZ])
                    P.copy("pool", QZ[64:128, :, 1, :], T[64:128, 0:2, :], [T], [QZ])
                    chunks[j] = (T, V, QZ)

                def attend(jb, r=r, d=d, g=g, chunks=chunks, own_u=own_u):
                    rels = [rel for rel in (-1, 0, 1) if (jb + rel) in chunks]
                    nr = len(rels)
                    QZq = chunks[jb][2]
                    bi = nblk[0]
                    nblk[0] += 1
                    for hb in range(4):
                        pr, hf = hb // 2, hb % 2
                        for ri, rel in enumerate(rels):
                            Tk = chunks[jb + rel][0]
                            col = 4 * 512 + (hb * nr + ri) * 128
                            so = self.psum[:, col:col + 128]
                            P.mm(so, Tk[:, 2 + pr, :], QZq[:, pr, hf, :], True, False, [Tk, QZq], ["psS"])
                            P.mm(so, self.ident[:], bmask[:, rel + 1, :], False, True, [self.ident, bmask], ["psS"])
                    pt = PT()
                    ncol = 4 * nr * 128
                    P.act(pt[:, 0:ncol], self.psum[:, 4 * 512:4 * 512 + ncol], AF.Exp, ["psS"], [pt], scale=0.125)
                    for hb in range(4):
                        for ri, rel in enumerate(rels):
                            Vk = chunks[jb + rel][1]
                            blk = (hb * nr + ri) * 128
                            P.mm(self.psum[:, 7 * 512 + hb * 65: 7 * 512 + hb * 65 + 65], pt[:, blk:blk + 128], Vk[:, hb, 0:65],
                                 ri == 0, ri == nr - 1, [pt, Vk], ["psO"])
                    o = osb()
                    P.copy("dve", o[:], self.psum[:, 7 * 512:7 * 512 + 260], ["psO"], [o])
                    nq = min(128, own_u - jb * 128)
                    t0 = r + d * 128 * jb
                    dst = bass.AP(self.OB.tensor, t0 * 780 + g * 260, [[d * 780, nq], [1, 260]])
                    P.dma("pool", dst, o[0:nq, :], "st_OB%d" % (bi % 2), reads=[o], writes=["OB"])

                def attend_ready(jdone):
                    while nxt[0] < nb and min(nxt[0] + 1, jmax) <= jdone:
                        attend(nxt[0])
                        nxt[0] += 1

                js = list(range(-1, jmax + 1))
                for idx, j in enumerate(js):
                    prodA(j)
                    if idx >= 1:
                        prodB(js[idx - 1])
                        attend_ready(js[idx - 1])
                prodB(js[-1])
                attend_ready(js[-1])
        P.pop()
        self.dbg_dump_dram("OB", self.OB, [OWN, 780], F32)

    def phase_l0_attnA(self):
        P = self.P
        P.push()
        Wout = P.sb("Wout0", [128, 6, D], BF16)
        P.dma("pool", Wout[:], self.w_out0, "ld_Wout0")
        G_bc = self.load_bc(0, 2, "G_bc0")
        PT = P.rot("PTA", [128, 1024], BF16, 3)
        obrot = P.rot("obin", [128, 780], F32, 2)
        xrot = P.rot("xtA", [128, D], F32, 2)
        orot = P.rot("otok", [128, 768], BF16, 2)
        oTrot = P.rot("oT", [128, 6, 128], BF16, 2)
        tmp = P.rot("tmpA", [128, D], F32, 2)
        xo = P.rot("xoA", [128, D], F32, 2)
        bsum = P.sb("bsum", [128, 260], F32)
        qzrot = P.rot("qz", [128, 2, 4, 128], BF16, 2)
        for z in qzrot.tensors:
            P.memset("pool", z[:], 0.0, [z])
        for t in range(OWNT):
            q0 = t * 128
            xt = xrot()
            P.dma("sp", xt[:], self.xw[WPAD + q0:WPAD + q0 + 128, :], "ld_xA%d" % (t % 2))
            ob = obrot()
            P.dma("sp", ob[:], self.OB[q0:q0 + 128, :], "ld_ob%d" % (t % 2), reads=["OB"], writes=[ob])
            first = {0: True, 1: True}
            o = orot()
            qz = qzrot()
            P.copy("pool", qz[0:64, 0, :, :], self.QAT[0:64, :, q0:q0 + 128], [self.QAT], [qz])
            P.copy("pool", qz[64:128, 1, :, :], self.QAT[64:128, :, q0:q0 + 128], [self.QAT], [qz])
            steps = [(kv, c2) for kv in range(2) for c2 in range(NBT // 2)]
            pts = {}

            def emitS(k):
                kv, c2 = steps[k]
                slot = k % 2
                skey = "psSA%d" % slot
                for u in range(2):
                    c = 2 * c2 + u
                    so = self.psum[:, slot * 1024 + u * 512: slot * 1024 + (u + 1) * 512]
                    P.mm(so, self.KAT[:, c * 128:(c + 1) * 128], qz[:, kv, :, :], True, True, [self.KAT, qz], [skey])

            def emitExp(k):
                slot = k % 2
                pt = PT()
                pts[k] = pt
                P.act(pt[:], self.psum[:, slot * 1024:(slot + 1) * 1024], AF.Exp, ["psSA%d" % slot], [pt], scale=0.125)

            def emitPV(k):
                kv, c2 = steps[k]
                okey = "psOA%d" % kv
                pt = pts.pop(k)
                for u in range(2):
                    c = 2 * c2 + u
                    for gq in range(4):
                        oo = self.psum[:, (4 + kv) * 512 + gq * 65:(4 + kv) * 512 + gq * 65 + 65]
                        P.mm(oo, pt[:, u * 512 + gq * 128: u * 512 + (gq + 1) * 128], self.VA[:, c, kv, 0:65],
                             first[kv], False, [pt, self.VA], [okey], skip_group_check=True)
                        first[kv] = False

            def normalize(kv):
                okey = "psOA%d" % kv
                ov = self.psum[:, (4 + kv) * 512:(4 + kv) * 512 + 260].rearrange("p (g d) -> p g d", d=65)
                rl = self.small()
                P.op("dve", (lambda rl=rl, ov=ov: (lambda e: e.reciprocal(rl[:, 0:4], ov[:, :, 64])))(), [okey], [rl])
                P.tt("dve", o[:, kv * 256:(kv + 1) * 256].rearrange("p (g d) -> p g d", d=64), ov[:, :, 0:64],
                     rl[:, 0:4].unsqueeze(2).to_broadcast([128, 4, 64]), ALU.mult, [okey, rl], [o])

            emitS(0)
            for k in range(len(steps)):
                if k + 1 < len(steps):
                    emitS(k + 1)
                emitExp(k)
                emitPV(k)
                if k == NBT // 2 - 1:
                    normalize(0)
            normalize(1)
            obv = ob[:].rearrange("p (g c) -> p g c", g=3)
            P.tt("pool", bsum[:], obv[:, 0, :], obv[:, 1, :], ALU.add, [ob], [bsum])
            P.tt("pool", bsum[:], bsum[:], obv[:, 2, :], ALU.add, [ob, bsum], [bsum])
            bv = bsum[:].rearrange("p (g d) -> p g d", d=65)
            rlb = self.small()
            P.op("dve", (lambda rlb=rlb, bv=bv: (lambda e: e.reciprocal(rlb[:, 0:4], bv[:, :, 64])))(), [bsum], [rlb])
            P.tt("dve", o[:, 512:768].rearrange("p (g d) -> p g d", d=64), bv[:, :, 0:64],
                 rlb[:, 0:4].unsqueeze(2).to_broadcast([128, 4, 64]), ALU.mult, [bsum, rlb], [o])
            oT = oTrot()
            pb, key = self.transposes(o, 768, 6, None, None)
            P.copy("dve", oT[:].rearrange("p k t -> p (k t)"), pb, [key], [oT])
            for n2 in range(2):
                for kc in range(6):
                    P.mm(self.bank(6 + n2), oT[:, kc, :], Wout[:, kc, n2 * 512:(n2 + 1) * 512], kc == 0, kc == 5,
                         [oT, Wout], ["ps%d" % (6 + n2)])
            xout = xo()
            self.post_residual_2(6, xt, G_bc, xout, tmp())
            P.dma("pool", self.x_mid[q0:q0 + 128, :], xout[:], "st_xm%d" % (t % 2), reads=[xout], writes=["x_mid"])
        P.pop()
        P.pop()
        self.dbg_dump_dram("x_mid", self.x_mid, [OWN, D], F32)

    def post_residual_2(self, b0, xt, G_bc, xout, tmp):
        yap = self.psum[:, b0 * 512:(b0 + 2) * 512]
        P = self.P
        keys = ["ps%d" % b0, "ps%d" % (b0 + 1)]
        ss = self.small()
        P.memset("pool", ss[:, 0:1], 0.0, [ss])
        P.act(self.junk[:], yap, AF.Square, keys, [self.junk, ss], accum_out=ss[:, 0:1])
        r = self.rstd_from_ss(ss[:, 0:1], 1, 1.0 / D)
        P.stt(tmp[:], yap, r[:, 0:1], G_bc[:], ALU.mult, ALU.mult, keys + [r, G_bc], [tmp])
        P.tt("pool", xout[:], tmp[:], xt[:], ALU.add, [tmp, xt], [xout])

    def phase_mlp(self, layer, src, dst, ntiles, tag):
        P = self.P
        m = 2 * layer + 1
        P.push()
        Wup = P.sb("Wup", [128, 8, 4096], BF16)
        Wdn = P.sb("Wdn", [128, 32, D], BF16)
        for q4 in range(4):
            P.dma("pool", Wup[:, :, q4 * 1024:(q4 + 1) * 1024], self.w_up[layer, :, :, q4 * 1024:(q4 + 1) * 1024], "ld_Wup",
                  reads=[], writes=[Wup])
            P.dma("pool", Wdn[:, q4 * 8:(q4 + 1) * 8, :], self.w_down[layer, :, q4 * 8:(q4 + 1) * 8, :], "ld_Wdn",
                  reads=[], writes=[Wdn])
        A_bc = self.load_bc(m, 0, "A_bcM")
        B_bc = self.load_bc(m, 1, "B_bcM")
        G_bc = self.load_bc(m, 2, "G_bcM")
        xrot = P.rot("xtM", [128, D], F32, 4)
        tmp = P.rot("tmpM", [128, D], F32, 2)
        hrot = P.rot("hM", [128, D], BF16, 2)
        hT = P.sb("hTM", [128, 8, 256], BF16)
        uT = P.sb("uTM", [128, 32, 256], BF16)
        rl = P.rot("relu", [128, 512], F32, 3)
        xo = P.rot("xoM", [128, D], F32, 2)
        srckey = src.name
        for gidx in range(ntiles // 2):
            xts = []
            for i in range(2):
                ti = gidx * 2 + i
                xt = xrot()
                P.dma("sp", xt[:], src[ti * 128:(ti + 1) * 128, :], "ld_xM%d" % (ti % 4), reads=[srckey], writes=[xt])
                xts.append(xt)
                h = hrot()
                self.prenorm(xt, A_bc, B_bc, h, tmp())
                pb, key = self.transposes(h, D, 3, None, None)
                P.copy("act", hT[:, :, i * 128:(i + 1) * 128], pb.rearrange("p (k t) -> p k t", k=8), [key], [hT])
            for hp in range(16):
                bno = hp % 3
                bkey = "ps%d" % bno
                for u in range(2):
                    hc = 2 * hp + u
                    for kc in range(8):
                        P.mm(self.psum[:, bno * 512 + u * 256: bno * 512 + (u + 1) * 256], Wup[:, kc, hc * 128:(hc + 1) * 128],
                             hT[:, kc, :], kc == 0, kc == 7, [Wup, hT], [bkey])
                r = rl()
                P.act(r[:], self.bank(bno), AF.Relu, [bkey], [r])
                P.tt("dve", uT[:, 2 * hp:2 * hp + 2, :].rearrange("p a t -> p (a t)"), r[:], r[:], ALU.mult, [r], [uT])
            for i in range(2):
                ti = gidx * 2 + i
                b0 = 4 + 2 * i
                for n2 in range(2):
                    for hc in range(32):
                        P.mm(self.bank(b0 + n2), uT[:, hc, i * 128:(i + 1) * 128], Wdn[:, hc, n2 * 512:(n2 + 1) * 512],
                             hc == 0, hc == 31, [uT, Wdn], ["ps%d" % (b0 + n2)])
                xout = xo()
                self.post_residual_2(b0, xts[i], G_bc, xout, tmp())
                P.dma("pool", dst[ti * 128:(ti + 1) * 128, :], xout[:], "st_%s%d" % (tag, ti % 2), reads=[xout], writes=[dst.name])
        P.pop()

    def phase_l1(self):
        P = self.P
        P.push()
        Win = P.sb("Win1", [128, 8, 3072], BF16)
        for q3 in range(3):
            P.dma("pool", Win[:, :, q3 * 1024:(q3 + 1) * 1024], self.w_in1[:, :, q3 * 1024:(q3 + 1) * 1024], "ld_Win1",
                  reads=[], writes=[Win])
        Wout = P.sb("Wout1", [128, 8, D], BF16)
        P.dma("pool", Wout[:], self.w_out1, "ld_Wout1")
        biasI = P.sb("biasI", [128, 16, 5, 128], BF16)
        for q4 in range(4):
            P.dma("pool", biasI[:, q4 * 4:(q4 + 1) * 4], self.biasI[:, q4 * 4:(q4 + 1) * 4], "ld_biasI", reads=[], writes=[biasI])
        biasE = P.sb("biasE", [128, 16, 6, 128], BF16)
        A_bc = self.load_bc(2, 0, "A_bc1")
        B_bc = self.load_bc(2, 1, "B_bc1")
        G_bc = self.load_bc(2, 2, "G_bc1")
        xrot = P.rot("xt1", [128, D], F32, 2)
        tmp = P.rot("tmp1", [128, D], F32, 1)
        hrot = P.rot("h1", [128, D], BF16, 1)
        hTrot = P.rot("hT1", [128, 8, 128], BF16, 2)
        qkbf = P.rot("qkbf1", [128, 1024], BF16, 1)
        NS = 7
        QT = P.rot("QT1", [128, 8, 128], BF16, NS)
        KT = P.rot("KT1", [128, 8, 128], BF16, NS)
        VC = P.rot("VC1", [128, 16, VS], BF16, NS)
        for v in VC.tensors:
            P.memset("pool", v[:, :, 64:65], 1.0, [v])
        PT = P.rot("PT1", [128, 768], BF16, 3)
        orot = P.rot("otok1", [128, D], BF16, 2)
        oTrot = P.rot("oT1", [128, 8, 128], BF16, 2)
        xo = P.rot("xo1", [128, D], F32, 1)
        chunks = {}
        npass = [0]

        def produce(j):
            xt = xrot()
            P.dma("sp", xt[:], self.x1[j * 128:(j + 1) * 128, :], "ld_x1%d" % (j % 3), reads=["x1"], writes=[xt])
            h = hrot()
            self.prenorm(xt, A_bc, B_bc, h, tmp())
            hT = hTrot()
            pb, key = self.transposes(h, D, 0, None, None)
            P.copy("dve", hT[:].rearrange("p k t -> p (k t)"), pb, [key], [hT])
            Tq, Tk, V = QT(), KT(), VC()
            for part, T in ((0, Tq), (1, Tk)):
                for n2 in range(2):
                    for kc in range(8):
                        c0 = part * 1024 + n2 * 512
                        P.mm(self.bank(1 + n2), hT[:, kc, :], Win[:, kc, c0:c0 + 512], kc == 0, kc == 7, [hT, Win], ["ps%d" % (1 + n2)])
                qk = qkbf()
                P.copy("act", qk[:], self.psum[:, 512:1536], ["ps1", "ps2"], [qk])
                pb2, key2 = self.transposes(qk, 1024, 0, None, None)
                P.copy("dve", T[:].rearrange("p k t -> p (k t)"), pb2, [key2], [T])
            for n2 in range(2):
                for kc in range(8):
                    c0 = 2048 + n2 * 512
                    P.mm(self.bank(1 + n2), hT[:, kc, :], Win[:, kc, c0:c0 + 512], kc == 0, kc == 7, [hT, Win], ["ps%d" % (1 + n2)])
            P.copy("act", V[:, :, 0:64], self.psum[:, 512:1536].rearrange("p (h d) -> p h d", d=64), ["ps1", "ps2"], [V])
            chunks[j] = (Tq, Tk, V)

        def attend(jq):
            if jq in (2, 3):
                rels = [-2, -1, 0, 1, 2, 3]
                et = jq - 2
            elif jq in (32, 33):
                rels = [-3, -2, -1, 0, 1, 2]
                et = jq - 30
            else:
                rels = [-2, -1, 0, 1, 2]
                et = None
            if et is not None:
                for q4 in range(4):
                    P.dma("pool", biasE[:, q4 * 4:(q4 + 1) * 4], self.biasE[et, :, q4 * 4:(q4 + 1) * 4], "ld_biasE", reads=[], writes=[biasE])
                bias = biasE
            else:
                bias = biasI
            nr = len(rels)
            xt = xrot()
            P.dma("sp", xt[:], self.x1[jq * 128:(jq + 1) * 128, :], "ld_x1r%d" % (jq % 3), reads=["x1"], writes=[xt])
            Tq = chunks[jq][0]
            o = orot()
            pts = {}

            def emitS(hd):
                hp, hf = hd // 2, hd % 2
                slot = hd % 2
                skey = "psS1_%d" % slot
                sbase = (3 + 2 * slot) * 512
                for ri, rel in enumerate(rels):
                    Tk = chunks[jq + rel][1]
                    so = self.psum[:, sbase + ri * 128: sbase + (ri + 1) * 128]
                    P.mm(so, Tk[hf * 64:(hf + 1) * 64, hp, :], Tq[hf * 64:(hf + 1) * 64, hp, :], True, False, [Tk, Tq], [skey])
                    P.mm(so, self.ident[:], bias[:, hd, ri, :], False, True, [self.ident, bias], [skey])

            def emitExp(hd):
                slot = hd % 2
                sbase = (3 + 2 * slot) * 512
                pt = PT()
                pts[hd] = pt
                P.act(pt[:, 0:nr * 128], self.psum[:, sbase:sbase + nr * 128], AF.Exp, ["psS1_%d" % slot], [pt], scale=0.125)

            def emitPV(hd):
                pt = pts.pop(hd)
                okey = "psO1"
                obase = 7 * 512
                for ri, rel in enumerate(rels):
                    Vk = chunks[jq + rel][2]
                    P.mm(self.psum[:, obase:obase + 65], pt[:, ri * 128:(ri + 1) * 128], Vk[:, hd, 0:65], ri == 0, ri == nr - 1,
                         [pt, Vk], [okey])
                rl = self.small()
                P.op("dve", (lambda rl=rl, ob=obase: (lambda e: e.reciprocal(rl[:, 0:1], self.psum[:, ob + 64:ob + 65])))(), [okey], [rl])
                P.ts(o[:, hd * 64:(hd + 1) * 64], self.psum[:, obase:obase + 64], rl[:, 0:1], None, ALU.mult, None, [okey, rl], [o])

            emitS(0)
            for hd in range(16):
                if hd + 1 < 16:
                    emitS(hd + 1)
                emitExp(hd)
                emitPV(hd)
            oT = oTrot()
            pb, key = self.transposes(o, D, 0, None, None)
            P.copy("dve", oT[:].rearrange("p k t -> p (k t)"), pb, [key], [oT])
            for n2 in range(2):
                for kc in range(8):
                    P.mm(self.bank(1 + n2), oT[:, kc, :], Wout[:, kc, n2 * 512:(n2 + 1) * 512], kc == 0, kc == 7,
                         [oT, Wout], ["ps%d" % (1 + n2)])
            xout = xo()
            self.post_residual_2(1, xt, G_bc, xout, tmp())
            P.dma("pool", self.x2[(jq - 2) * 128:(jq - 1) * 128, :], xout[:], "st_x2%d" % (jq % 2), reads=[xout], writes=["x2"])

        nxt_q = 2
        for j in range(OWNT):
            produce(j)
            while nxt_q <= 33:
                need = min(OWNT - 1, nxt_q + (3 if nxt_q in (2, 3) else 2))
                if need > j:
                    break
                attend(nxt_q)
                nxt_q += 1
        P.pop()
        self.dbg_dump_dram("x2", self.x2, [4096, D], F32)

    def build(self):
        part = self.part
        steps = []
        if part in ("L0", "ALL"):
            steps.append(("mods", lambda: self.phase_mods([0, 1] if part == "L0" else [0, 1, 2, 3])))
            steps.append(("prep", self.phase_l0_prep))
            steps.append(("B", self.phase_l0_B))
            steps.append(("attnA", self.phase_l0_attnA))
            steps.append(("mlp0", lambda: self.phase_mlp(0, self.x_mid, self.x1, OWNT, "x1")))
        if part == "L1":
            steps.append(("mods", lambda: self.phase_mods([2, 3])))
        if part in ("L1", "ALL"):
            steps.append(("l1", self.phase_l1))
            steps.append(("mlp1", lambda: self.phase_mlp(1, self.x2, self.out, 32, "out")))
        for name, fn in steps:
            fn()
            if self.upto == name:
                break
        self.P.finish()


def _rope_cs(pos, dim):
    inv = (10000.0 ** (-np.arange(0, dim, 2, dtype=np.float32) / dim)).astype(np.float32)
    ang = pos.astype(np.float32)[:, None] * inv[None, :]
    return np.cos(ang).astype(np.float32), np.sin(ang).astype(np.float32)


def _rope_tab_axial(pos):
    pos = np.clip(pos, 0, NB - 1)
    cr, sr = _rope_cs(pos // 64, 32)
    cc, sc = _rope_cs(pos % 64, 32)
    return np.concatenate([cr, cr, cc, cc, -sr, sr, -sc, sc], 1).astype(np.float32)


def _rope_tab_1d(pos):
    pos = np.clip(pos, 0, NB - 1)
    c, s = _rope_cs(pos, 64)
    return np.concatenate([c, c, -s, s], 1).astype(np.float32)


def _pk(w, kchunks):
    K, N = w.shape
    return np.ascontiguousarray(w.reshape(kchunks, 128, N).transpose(1, 0, 2))


def _bias_tables(rpb, r0_list, rel_lists):
    outs = []
    ik = np.arange(128)
    for r0, rels in zip(r0_list, rel_lists):
        t = np.full((128, 16, len(rels), 128), NEG, np.float32)
        qrow = r0 + ik // 64
        qc = ik % 64
        rs = np.clip(qrow - 4, 0, 256 - 8)
        cs = np.clip(qc - 8, 0, 64 - 16)
        for ri, rel in enumerate(rels):
            krow = r0 + 2 * rel + ik // 64
            kc = ik % 64
            ok = ((krow[:, None] >= rs[None, :]) & (krow[:, None] < rs[None, :] + 8) &
                  (kc[:, None] >= cs[None, :]) & (kc[:, None] < cs[None, :] + 16) &
                  (krow[:, None] >= 0) & (krow[:, None] < 256) & (qrow[None, :] >= 0) & (qrow[None, :] < 256))
            dr = np.clip(krow[:, None] - qrow[None, :] + 7, 0, 14)
            dc = np.clip(kc[:, None] - qc[None, :] + 15, 0, 30)
            vals = rpb[:, dr, dc]
            t[:, :, ri, :] = np.where(ok[None], vals, np.float32(NEG)).transpose(1, 0, 2)
        outs.append(t)
    return outs


def _host_prep(inp):
    x = np.asarray(inp["x"], np.float32)
    shared = {}
    shared["ada_w"] = np.ascontiguousarray(
        np.asarray(inp["ada_w"], np.float32).reshape(4, 8, 128, 3072).transpose(0, 2, 1, 3))
    shared["ada_b"] = np.ascontiguousarray(np.asarray(inp["ada_b"], np.float32).reshape(4, 3072))
    shared["norm_g"] = np.ascontiguousarray(np.asarray(inp["norm_g"], np.float32).reshape(8, 1024))
    shared["ident"] = np.eye(128, dtype=np.float32)
    shared["w_up"] = np.stack([_pk(np.asarray(inp["mlp_w_up"][l], np.float32), 8) for l in range(2)])
    shared["w_down"] = np.stack([_pk(np.asarray(inp["mlp_w_down"][l], np.float32), 32) for l in range(2)])
    w_in = np.asarray(inp["ab_w_in"][0], np.float32)
    qa = w_in[:, 0:512].reshape(1024, 2, 4, 64).transpose(0, 2, 1, 3).reshape(1024, 512)
    shared["w_a"] = _pk(np.concatenate([qa, w_in[:, 512:768]], 1), 8)
    wb = []
    for g in range(3):
        cols = [w_in[:, 768 + part * 768 + g * 256: 768 + part * 768 + (g + 1) * 256] for part in range(3)]
        wb.append(_pk(np.concatenate(cols, 1), 8))
    shared["w_b"] = np.stack(wb)
    shared["w_out0"] = _pk(np.asarray(inp["ab_w_out"][0], np.float32), 6)
    shared["gains"] = np.stack([np.asarray(inp["a_q_gain"][0], np.float32), np.asarray(inp["a_k_gain"][0], np.float32)])
    shared["ropeA_b"] = _rope_tab_axial(np.arange(NB))
    ik = np.arange(128)
    bm = np.full((128, 3, 128), NEG, np.float32)
    dk = ik[:, None] - ik[None, :]
    bm[:, 0, :] = np.where(dk >= 64, 0.0, NEG)
    bm[:, 1, :] = np.where(np.abs(dk) <= 64, 0.0, NEG)
    bm[:, 2, :] = np.where(dk <= -64, 0.0, NEG)
    shared["bmask"] = bm
    shared["w_in1"] = _pk(np.asarray(inp["c_w_in"][0], np.float32), 8)
    shared["w_out1"] = _pk(np.asarray(inp["c_w_out"][0], np.float32), 8)
    rpb = np.asarray(inp["c_rpb"][0], np.float32)
    shared["biasI"] = _bias_tables(rpb, [100], [[-2, -1, 0, 1, 2]])[0]
    per = []
    for core in range(8):
        b, q = core // 4, core % 4
        s = 4096 * q
        d = {}
        xp = np.zeros((WIN, D), np.float32)
        lo, hi = s - HALO - WPAD, s + 4096 + HALO + WPAD
        a, bnd = max(lo, 0), min(hi, NB)
        xp[a - lo:bnd - lo] = x[b, a:bnd]
        d["xw"] = xp
        d["xb"] = np.ascontiguousarray(x[b])
        d["cmod"] = np.ascontiguousarray(np.asarray(inp["c"], np.float32)[b].reshape(8, 128).T)
        wpos = np.arange(lo, hi)
        vw = ((wpos >= 0) & (wpos < NB)).astype(np.float32)[:, None]
        d["ropeB_w"] = np.concatenate([_rope_tab_1d(wpos), np.repeat(vw, 64, 1)], 1)
        d["ropeA_o"] = _rope_tab_axial(np.arange(s - HALO, s + 4096 + HALO))
        r_own0 = (s - HALO) // 64
        r0s = [r_own0 + 2 * jq for jq in (2, 3, 32, 33)]
        rl = [[-2, -1, 0, 1, 2, 3]] * 2 + [[-3, -2, -1, 0, 1, 2]] * 2
        d["biasE"] = np.stack(_bias_tables(rpb, r0s, rl))
        per.append(d)
    return shared, per


L0_KEYS = ["cmod", "ada_w", "ada_b", "norm_g", "ident", "w_up", "w_down", "xw", "xb", "w_a", "w_b", "w_out0", "gains",
           "ropeA_b", "ropeA_o", "ropeB_w", "bmask"]
L1_KEYS = ["cmod", "ada_w", "ada_b", "norm_g", "ident", "w_up", "w_down", "w_in1", "w_out1", "biasI", "biasE"]

MODE = "FUSED"


def _run(part, shared, per, extra=None, dbg=()):
    nc = bass.Bass("TRN2", target_bir_lowering=False)
    Builder(nc, part, dbg).build()
    keys = {"L0": L0_KEYS, "L1": L1_KEYS, "ALL": sorted(set(L0_KEYS + L1_KEYS))}[part]
    in_maps = []
    for core in range(8):
        m = {}
        for k in keys:
            m[k] = per[core][k] if k in per[core] else shared[k]
        if extra is not None:
            m.update(extra[core])
        in_maps.append(m)
    res = run_bass_kernel_spmd(nc, in_maps, core_ids=list(range(8)))
    return res.results


def kernel(**inputs):
    shared, per = _host_prep(inputs)
    if MODE == "SPLIT":
        r0 = _run("L0", shared, per)
        extra = [{"x1": np.asarray(r0[c]["x1"], np.float32)} for c in range(8)]
        r1 = _run("L1", shared, per, extra)
    else:
        r1 = _run("ALL", shared, per)
    out = np.empty((2, NB, D), np.float32)
    for core in range(8):
        b, q = core // 4, core % 4
        out[b, 4096 * q:4096 * (q + 1)] = np.asarray(r1[core]["out"], np.float32)
    return out
```
